# Optimizing a Trainium2 kernel written in Bass

```python
import jax
import jax.numpy as jnp
from jax import lax
import numpy as np

D_MODEL = 1024
BATCH = 8
SEQ = 4096
DEPTH = 4

N_META = 16
EXPAND = 2
D_INNER = EXPAND * D_MODEL
BLOCK = 128
ROPE_THETA = 10000.0
EPS = 1e-6
NEG_INF = -1e30

MLA_HEADS = 8
MLA_Q_RANK = 384
MLA_KV_RANK = 256
MLA_NOPE = 128
MLA_ROPE = 64
MLA_QK = MLA_NOPE + MLA_ROPE
MLA_V = 128
MLA_WIDTH = MLA_HEADS * MLA_V

SWA_HEADS = 8
SWA_KV_HEADS = 2
SWA_REP = SWA_HEADS // SWA_KV_HEADS
SWA_HEAD_DIM = 128
SWA_WINDOW = 128
SWA_WIDTH = SWA_HEADS * SWA_HEAD_DIM
SWA_KV_WIDTH = SWA_KV_HEADS * SWA_HEAD_DIM

LRU_WIDTH = D_INNER
LRU_BLOCKS = 16
LRU_BLOCK_DIM = LRU_WIDTH // LRU_BLOCKS
LRU_C = 8.0
CONV_WIDTH = 4
CONV_PAD_LEFT = 2
CONV_PAD_RIGHT = 1

EVEN_SPLITS = (MLA_Q_RANK, MLA_KV_RANK, MLA_ROPE, SWA_WIDTH, SWA_KV_WIDTH, SWA_KV_WIDTH, D_INNER)
EVEN_IN = MLA_Q_RANK + MLA_KV_RANK + MLA_ROPE + SWA_WIDTH + 2 * SWA_KV_WIDTH + D_INNER
ODD_SPLITS = (LRU_WIDTH, D_INNER)
ODD_IN = LRU_WIDTH + D_INNER
N_EVEN = (DEPTH + 1) // 2
N_ODD = DEPTH // 2

kernel_name = "hybrid_mla_swa_rglru_encoder"


def rms_norm(x, g):
    xf = x.astype(jnp.float32)
    y = xf * lax.rsqrt(jnp.mean(xf * xf, axis=-1, keepdims=True) + EPS)
    return (y * g.astype(jnp.float32)).astype(x.dtype)


def rope_tables(T, dim):
    inv = ROPE_THETA ** (-jnp.arange(0, dim, 2, dtype=jnp.float32) / dim)
    ang = jnp.arange(T, dtype=jnp.float32)[:, None] * inv[None, :]
    return jnp.cos(ang), jnp.sin(ang)


def apply_rope(x, cos, sin):
    x1, x2 = jnp.split(x, 2, axis=-1)
    c = cos.astype(x.dtype)
    s = sin.astype(x.dtype)
    return jnp.concatenate([x1 * c - x2 * s, x1 * s + x2 * c], axis=-1)


def split_cols(z, sizes):
    out = []
    start = 0
    for s in sizes:
        out.append(z[..., start:start + s])
        start += s
    return out


def blocked_dense_attention(q, k, v, scale):
    B, H, T, dq = q.shape
    dv = v.shape[-1]
    nb = (T - N_META) // BLOCK

    def attend(qb):
        s = jnp.einsum('bhqd,bhkd->bhqk', qb, k, preferred_element_type=jnp.float32) * scale
        p = jax.nn.softmax(s, axis=-1).astype(v.dtype)
        return jnp.einsum('bhqk,bhkd->bhqd', p, v)

    o_meta = attend(q[:, :, :N_META])
    qb = q[:, :, N_META:].reshape(B, H, nb, BLOCK, dq).transpose(2, 0, 1, 3, 4)
    o_real = lax.map(attend, qb).transpose(1, 2, 0, 3, 4).reshape(B, H, nb * BLOCK, dv)
    return jnp.concatenate([o_meta, o_real], axis=2)


def banded_window_attention(q, k, v, sink, scale):
    B, G, R, T, d = q.shape
    L = T - N_META
    nb = L // BLOCK
    k_m, v_m = k[:, :, :N_META], v[:, :, :N_META]
    k_r, v_r = k[:, :, N_META:], v[:, :, N_META:]
    sink_l = sink.astype(jnp.float32).reshape(G, R)

    def band(t):
        tp = jnp.pad(t, ((0, 0), (0, 0), (BLOCK, BLOCK), (0, 0))).reshape(B, G, nb + 2, BLOCK, d)
        return jnp.concatenate([tp[:, :, :-2], tp[:, :, 1:-1], tp[:, :, 2:]], axis=3)

    k_b, v_b = band(k_r), band(v_r)
    q_b = q[:, :, :, N_META:].reshape(B, G, R, nb, BLOCK, d)
    blk = jnp.arange(nb)[:, None, None]
    qpos = blk * BLOCK + jnp.arange(BLOCK)[None, :, None]
    kpos = blk * BLOCK - BLOCK + jnp.arange(3 * BLOCK)[None, None, :]
    valid = (jnp.abs(qpos - kpos) <= SWA_WINDOW) & (kpos >= 0) & (kpos < L)
    s_meta = jnp.einsum('bgrnqd,bgmd->bgrnqm', q_b, k_m, preferred_element_type=jnp.float32) * scale
    s_band = jnp.einsum('bgrnqd,bgnkd->bgrnqk', q_b, k_b, preferred_element_type=jnp.float32) * scale
    s_band = jnp.where(valid, s_band, NEG_INF)
    s_sink = jnp.broadcast_to(sink_l[None, :, :, None, None, None], (B, G, R, nb, BLOCK, 1))
    p = jax.nn.softmax(jnp.concatenate([s_sink, s_meta, s_band], axis=-1), axis=-1).astype(v.dtype)
    o_real = (jnp.einsum('bgrnqm,bgmd->bgrnqd', p[..., 1:1 + N_META], v_m)
              + jnp.einsum('bgrnqk,bgnkd->bgrnqd', p[..., 1 + N_META:], v_b))
    o_real = o_real.reshape(B, G, R, L, d)

    q_m = q[:, :, :, :N_META]
    k_f, v_f = k_r[:, :, :BLOCK], v_r[:, :, :BLOCK]
    valid_m = (N_META + jnp.arange(BLOCK)[None, :] - jnp.arange(N_META)[:, None]) <= SWA_WINDOW
    s_mm = jnp.einsum('bgrqd,bgmd->bgrqm', q_m, k_m, preferred_element_type=jnp.float32) * scale
    s_mf = jnp.einsum('bgrqd,bgkd->bgrqk', q_m, k_f, preferred_element_type=jnp.float32) * scale
    s_mf = jnp.where(valid_m, s_mf, NEG_INF)
    s_ms = jnp.broadcast_to(sink_l[None, :, :, None, None], (B, G, R, N_META, 1))
    p_m = jax.nn.softmax(jnp.concatenate([s_ms, s_mm, s_mf], axis=-1), axis=-1).astype(v.dtype)
    o_meta = (jnp.einsum('bgrqm,bgmd->bgrqd', p_m[..., 1:1 + N_META], v_m)
              + jnp.einsum('bgrqk,bgkd->bgrqd', p_m[..., 1 + N_META:], v_f))
    return jnp.concatenate([o_meta, o_real], axis=3)


def mla_mixer(c_q, c_kv, k_pe, g_q_lat, g_kv_lat, w_uq, w_ukv, g_qn, g_kn, cos, sin):
    B, T, _ = c_q.shape
    q = jnp.einsum('btr,rhd->bhtd', rms_norm(c_q, g_q_lat), w_uq)
    kv = jnp.einsum('btr,rhd->bhtd', rms_norm(c_kv, g_kv_lat), w_ukv)
    k_nope, v = kv[..., :MLA_NOPE], kv[..., MLA_NOPE:]
    k_pe = jnp.broadcast_to(k_pe[:, None], (B, MLA_HEADS, T, MLA_ROPE))
    k = jnp.concatenate([k_nope, k_pe], axis=-1)
    q = rms_norm(q, g_qn)
    k = rms_norm(k, g_kn)
    q = jnp.concatenate([q[..., :MLA_NOPE], apply_rope(q[..., MLA_NOPE:], cos, sin)], axis=-1)
    k = jnp.concatenate([k[..., :MLA_NOPE], apply_rope(k[..., MLA_NOPE:], cos, sin)], axis=-1)
    o = blocked_dense_attention(q, k, v, MLA_QK ** -0.5)
    return o.transpose(0, 2, 1, 3).reshape(B, T, MLA_WIDTH)


def swa_mixer(q_s, k_s, v_s, g_qn, g_kn, sink, cos, sin):
    B, T, _ = q_s.shape
    q = q_s.reshape(B, T, SWA_HEADS, SWA_HEAD_DIM).transpose(0, 2, 1, 3)
    k = k_s.reshape(B, T, SWA_KV_HEADS, SWA_HEAD_DIM).transpose(0, 2, 1, 3)
    v = v_s.reshape(B, T, SWA_KV_HEADS, SWA_HEAD_DIM).transpose(0, 2, 1, 3)
    q = apply_rope(rms_norm(q, g_qn), cos, sin)
    k = apply_rope(rms_norm(k, g_kn), cos, sin)
    q = q.reshape(B, SWA_KV_HEADS, SWA_REP, T, SWA_HEAD_DIM)
    o = banded_window_attention(q, k, v, sink, SWA_HEAD_DIM ** -0.5)
    return o.reshape(B, SWA_HEADS, T, SWA_HEAD_DIM).transpose(0, 2, 1, 3).reshape(B, T, SWA_WIDTH)


def even_layer(z, w_in, g_q_lat, g_kv_lat, w_uq, w_ukv, mla_g_qn, mla_g_kn,
               swa_g_qn, swa_g_kn, sink, w_out, rope_mla, rope_swa):
    zp = jnp.einsum('btd,de->bte', z, w_in)
    c_q, c_kv, k_pe, q_s, k_s, v_s, gate = split_cols(zp, EVEN_SPLITS)
    y_a = mla_mixer(c_q, c_kv, k_pe, g_q_lat, g_kv_lat, w_uq, w_ukv,
                    mla_g_qn, mla_g_kn, rope_mla[0], rope_mla[1])
    y_b = swa_mixer(q_s, k_s, v_s, swa_g_qn, swa_g_kn, sink, rope_swa[0], rope_swa[1])
    y = jnp.concatenate([y_a, y_b], axis=-1) * jax.nn.silu(gate)
    return jnp.einsum('bte,ed->btd', y, w_out)


def lru_direction(xc, w_a, b_a, w_x, b_x, lam, reverse):
    B, T, W = xc.shape
    xb = xc.reshape(B, T, LRU_BLOCKS, LRU_BLOCK_DIM)
    r = jax.nn.sigmoid((jnp.einsum('btnd,nde->btne', xb, w_a).reshape(B, T, W) + b_a).astype(jnp.float32))
    i = jax.nn.sigmoid((jnp.einsum('btnd,nde->btne', xb, w_x).reshape(B, T, W) + b_x).astype(jnp.float32))
    log_a = -LRU_C * r * jax.nn.softplus(-lam.astype(jnp.float32))
    a = jnp.exp(log_a)
    b = jnp.sqrt(-jnp.expm1(2.0 * log_a)) * i * xc.astype(jnp.float32)

    def step(h, ab):
        a_t, b_t = ab
        h = a_t * h + b_t
        return h, h

    _, hs = lax.scan(step, jnp.zeros((B, W), jnp.float32),
                     (a.swapaxes(0, 1), b.swapaxes(0, 1)), reverse=reverse)
    return hs.swapaxes(0, 1).astype(xc.dtype)


def odd_layer(z, w_in, conv_w, conv_b, w_a, b_a, w_x, b_x, lam, w_out):
    zp = jnp.einsum('btd,de->bte', z, w_in)
    u, gate = split_cols(zp, ODD_SPLITS)
    T = u.shape[1]
    up = jnp.pad(u, ((0, 0), (CONV_PAD_LEFT, CONV_PAD_RIGHT), (0, 0)))
    xc = conv_b
    for tap in range(CONV_WIDTH):
        xc = xc + up[:, tap:tap + T] * conv_w[tap]
    y = (lru_direction(xc, w_a[0], b_a[0], w_x[0], b_x[0], lam[0], reverse=False)
         + lru_direction(xc, w_a[1], b_a[1], w_x[1], b_x[1], lam[1], reverse=True))
    y = y * jax.nn.silu(gate)
    return jnp.einsum('bte,ed->btd', y, w_out)


def setup_inputs(seed: int = 0) -> dict:
    key = jax.random.key(seed)
    ks = jax.random.split(key, 24)
    f32 = jnp.float32

    def nrm(k, shape, scale):
        return jax.random.normal(k, shape, f32) * scale

    def gain(k, shape):
        return 1.0 + 0.01 * jax.random.normal(k, shape, f32)

    u = jax.random.uniform(ks[23], (N_ODD, 2, LRU_WIDTH), f32, 0.9, 0.999)
    a0 = u ** (1.0 / LRU_C)
    lru_lambda = jnp.log(a0) - jnp.log1p(-a0)

    return {
        "x": nrm(ks[0], (BATCH, SEQ, D_MODEL), 1.0),
        "meta_tokens": nrm(ks[1], (N_META, D_MODEL), 1.0),
        "norm_g": gain(ks[2], (DEPTH, D_MODEL)),
        "even_w_in": nrm(ks[3], (N_EVEN, D_MODEL, EVEN_IN), D_MODEL ** -0.5),
        "mla_g_q_lat": gain(ks[4], (N_EVEN, MLA_Q_RANK)),
        "mla_g_kv_lat": gain(ks[5], (N_EVEN, MLA_KV_RANK)),
        "mla_w_uq": nrm(ks[6], (N_EVEN, MLA_Q_RANK, MLA_HEADS, MLA_QK), MLA_Q_RANK ** -0.5),
        "mla_w_ukv": nrm(ks[7], (N_EVEN, MLA_KV_RANK, MLA_HEADS, MLA_NOPE + MLA_V), MLA_KV_RANK ** -0.5),
        "mla_g_qn": gain(ks[8], (N_EVEN, MLA_QK)),
        "mla_g_kn": gain(ks[9], (N_EVEN, MLA_QK)),
        "swa_g_qn": gain(ks[10], (N_EVEN, SWA_HEAD_DIM)),
        "swa_g_kn": gain(ks[11], (N_EVEN, SWA_HEAD_DIM)),
        "swa_sink": nrm(ks[12], (N_EVEN, SWA_HEADS), 0.5),
        "even_w_out": nrm(ks[13], (N_EVEN, D_INNER, D_MODEL), D_INNER ** -0.5),
        "odd_w_in": nrm(ks[14], (N_ODD, D_MODEL, ODD_IN), D_MODEL ** -0.5),
        "lru_conv_w": nrm(ks[15], (N_ODD, CONV_WIDTH, LRU_WIDTH), CONV_WIDTH ** -0.5),
        "lru_conv_b": nrm(ks[16], (N_ODD, LRU_WIDTH), 0.01),
        "lru_w_a": nrm(ks[17], (N_ODD, 2, LRU_BLOCKS, LRU_BLOCK_DIM, LRU_BLOCK_DIM), LRU_BLOCK_DIM ** -0.5),
        "lru_b_a": nrm(ks[18], (N_ODD, 2, LRU_WIDTH), 0.01),
        "lru_w_x": nrm(ks[19], (N_ODD, 2, LRU_BLOCKS, LRU_BLOCK_DIM, LRU_BLOCK_DIM), LRU_BLOCK_DIM ** -0.5),
        "lru_b_x": nrm(ks[20], (N_ODD, 2, LRU_WIDTH), 0.01),
        "lru_lambda": lru_lambda,
        "odd_w_out": nrm(ks[21], (N_ODD, D_INNER, D_MODEL), D_INNER ** -0.5),
    }


def reference(x, meta_tokens, norm_g, even_w_in, mla_g_q_lat, mla_g_kv_lat, mla_w_uq, mla_w_ukv,
              mla_g_qn, mla_g_kn, swa_g_qn, swa_g_kn, swa_sink, even_w_out,
              odd_w_in, lru_conv_w, lru_conv_b, lru_w_a, lru_b_a, lru_w_x, lru_b_x,
              lru_lambda, odd_w_out):
    B = x.shape[0]
    meta = jnp.broadcast_to(meta_tokens.astype(x.dtype)[None], (B, N_META, D_MODEL))
    h = jnp.concatenate([meta, x], axis=1)
    T = h.shape[1]
    rope_mla = rope_tables(T, MLA_ROPE)
    rope_swa = rope_tables(T, SWA_HEAD_DIM)
    for l in range(DEPTH):
        z = rms_norm(h, norm_g[l])
        j = l // 2
        if l % 2 == 0:
            y = even_layer(z, even_w_in[j], mla_g_q_lat[j], mla_g_kv_lat[j], mla_w_uq[j], mla_w_ukv[j],
                           mla_g_qn[j], mla_g_kn[j], swa_g_qn[j], swa_g_kn[j], swa_sink[j],
                           even_w_out[j], rope_mla, rope_swa)
        else:
            y = odd_layer(z, odd_w_in[j], lru_conv_w[j], lru_conv_b[j], lru_w_a[j], lru_b_a[j],
                          lru_w_x[j], lru_b_x[j], lru_lambda[j], odd_w_out[j])
        h = h + y
    return h[:, N_META:]
```

```python
import math
import contextlib
import numpy as np
import concourse.bass as bass
import concourse.mybir as mybir
from concourse.bass_utils import run_bass_kernel_spmd

F32 = mybir.dt.float32
F32R = mybir.dt.float32r
BF16 = mybir.dt.bfloat16
I32 = mybir.dt.int32
AF = mybir.ActivationFunctionType
ALU = mybir.AluOpType

D = 1024
NMETA = 16
EPS = 1e-6
NPE = 27
NPO = 184


class _Op:
    __slots__ = ("eng", "fn", "dma", "deps", "needs_sig", "sem", "val", "done", "order")

    def __init__(self, eng, fn, dma):
        self.eng = eng
        self.fn = fn
        self.dma = dma
        self.deps = ()
        self.needs_sig = False
        self.sem = None
        self.val = 0
        self.done = False
        self.order = 0


class Sched:
    COMPUTE = ("pe", "act", "dve", "pool")
    RING = 8

    def __init__(self, nc, stack):
        self.nc = nc
        self.E = {"pe": nc.tensor, "act": nc.scalar, "dve": nc.vector,
                  "pool": nc.gpsimd, "sp": nc.sync}
        self.csem = {e: stack.enter_context(nc.semaphore("c_" + e)) for e in self.COMPUTE}
        self.ccount = {e: 0 for e in self.COMPUTE}
        self.rings = {}
        for q in ("sp", "pool"):
            self.rings[q] = [stack.enter_context(nc.semaphore("d_%s%d" % (q, i)))
                             for i in range(self.RING)]
        self.dcount = {q: 0 for q in self.rings}
        self.ops = []
        self.last_w = {}
        self.readers = {}
        self.known = {e: {} for e in self.E}
        self.n_inst = 0
        self.n_ops = 0

    def op(self, eng, fn, reads=(), writes=(), dma=False):
        o = _Op(eng, fn, dma)
        deps = {}

        def add(d):
            if d is None or d.done:
                return
            if (not d.dma) and (not dma) and d.eng == "pe" and eng == "pe":
                return
            deps[id(d)] = d

        for k in reads:
            add(self.last_w.get(k))
        for k in writes:
            add(self.last_w.get(k))
            r = self.readers.get(k)
            if r:
                for d in r[0].values():
                    add(d)
                for d in r[1]:
                    add(d)
        best = {}
        out = []
        for d in deps.values():
            if d.dma:
                out.append(d)
            else:
                b = best.get(d.eng)
                if b is None or d.order > b.order:
                    best[d.eng] = d
        out.extend(best.values())
        o.deps = out
        for d in out:
            d.needs_sig = True
        self.n_ops += 1
        o.order = self.n_ops
        for k in writes:
            self.last_w[k] = o
            self.readers[k] = ({}, [])
        for k in reads:
            r = self.readers.get(k)
            if r is None:
                r = self.readers[k] = ({}, [])
            if dma:
                r[1].append(o)
            else:
                r[0][eng] = o
        self.ops.append(o)
        return o

    def _wait(self, eng, sem, val):
        kn = self.known[eng]
        key = id(sem)
        if kn.get(key, 0) >= val:
            return
        self.E[eng].wait_ge(sem, val)
        kn[key] = val
        self.n_inst += 1

    def phase_end(self):
        last = {}
        for o in self.ops:
            if not o.dma:
                last[o.eng] = o
        for o in last.values():
            o.needs_sig = True
        for o in self.ops:
            E = self.E[o.eng]
            for d in o.deps:
                if d.done:
                    continue
                self._wait(o.eng, d.sem, d.val)
            if o.dma:
                ring = self.rings[o.eng]
                i = self.dcount[o.eng]
                self.dcount[o.eng] = i + 1
                sem = ring[i % self.RING]
                tgt = 16 * (i // self.RING + 1)
                if tgt > 16:
                    self._wait(o.eng, sem, tgt - 16)
                ins = o.fn(E)
                ins.then_inc(sem, 16)
                o.sem, o.val = sem, tgt
            else:
                ins = o.fn(E)
                if o.needs_sig:
                    self.ccount[o.eng] += 1
                    ins.then_inc(self.csem[o.eng], 1)
                    o.sem, o.val = self.csem[o.eng], self.ccount[o.eng]
            self.n_inst += 1
        for eng in self.E:
            for e in self.COMPUTE:
                if self.ccount[e] > 0:
                    self._wait(eng, self.csem[e], self.ccount[e])
            for q, ring in self.rings.items():
                n = self.dcount[q]
                for j in range(self.RING):
                    uses = (n - j + self.RING - 1) // self.RING if n > j else 0
                    if uses > 0:
                        self._wait(eng, ring[j], 16 * uses)
        for o in self.ops:
            o.done = True
        self.last_w = {}
        self.readers = {}
        self.ops = []


class Prog:
    def __init__(self, SEQ, n_layers=4):
        assert SEQ % 512 == 0
        self.SEQ = SEQ
        self.T = SEQ + NMETA
        self.n_layers = n_layers
        self.blocks = [(0, NMETA)] + [(NMETA + 512 * j, 512) for j in range(SEQ // 512)]
        self.NT = SEQ // 128
        self.ktiles = [(0, NMETA)] + [(NMETA + 128 * j, 128) for j in range(self.NT)]
        self.uid = 0
        self.psi = 0

    def sb(self, ph, shape, dt, name="t"):
        self.uid += 1
        return ph.enter_context(self.nc.sbuf_tensor("%s_%d" % (name, self.uid), list(shape), dt))

    def dma(self, q, out, in_, reads=(), writes=()):
        self.S.op(q, lambda e: e.dma_start(out=out, in_=in_), reads, writes, dma=True)

    def mm(self, out, pairs, reads, writes, start=True, stop=True):
        pairs = list(pairs)

        def fn(e):
            n = len(pairs)
            ins = None
            for i, (l, r) in enumerate(pairs):
                ins = e.matmul(out, l, r, start=(start and i == 0), stop=(stop and i == n - 1))
            return ins
        self.S.op("pe", fn, reads, writes)

    def act(self, out, in_, func, reads, writes, scale=1.0, bias=None):
        if bias is None:
            self.S.op("act", lambda e: e.activation(out=out, in_=in_, func=func, scale=scale), reads, writes)
        else:
            self.S.op("act", lambda e: e.activation(out=out, in_=in_, func=func, scale=scale, bias=bias), reads, writes)

    def tt(self, eng, out, a, b, op, reads, writes):
        self.S.op(eng, lambda e: e.tensor_tensor(out, a, b, op), reads, writes)

    def ts(self, eng, out, a, s1, s2, op0, op1, reads, writes):
        if s2 is None:
            self.S.op(eng, lambda e: e.tensor_scalar(out, a, s1, None, op0), reads, writes)
        else:
            self.S.op(eng, lambda e: e.tensor_scalar(out, a, s1, s2, op0, op1), reads, writes)

    def stt(self, out, a, s, b, op0, op1, reads, writes):
        self.S.op("dve", lambda e: e.scalar_tensor_tensor(out, a, s, b, op0, op1), reads, writes)

    def recip(self, out, in_, reads, writes):
        self.S.op("dve", lambda e: e.reciprocal(out, in_), reads, writes)

    def copy(self, eng, out, in_, reads, writes):
        self.S.op(eng, lambda e: e.tensor_copy(out, in_), reads, writes)

    def memset(self, eng, ap, v, writes):
        self.S.op(eng, lambda e: e.memset(ap, v), (), writes)

    def hblk(self, bi):
        n = self.blocks[bi][1]
        return self.hB[bi, :, 0:8 * n].rearrange("p (c t) -> p c t", c=8)

    def zblk(self, bi):
        n = self.blocks[bi][1]
        return self.zB[bi, :, 0:8 * n].rearrange("p (c t) -> p c t", c=8)

    def yblk(self, bi):
        n = self.blocks[bi][1]
        return self.yB[bi, :, 0:16 * n].rearrange("p (c t) -> p c t", c=16)

    def nps(self):
        i = self.psi
        self.psi = (i + 1) % 8
        return i

    def rstd(self, ph_tiles, pairs, pair_reads, inv_n, n, tag):
        sd, rs = ph_tiles
        b = self.nps()
        self.mm(self.ps[b][:, 0:n], pairs, reads=list(pair_reads) + ["ones_r"], writes=[("ps", b)])
        self.act(sd[:, 0:n], self.ps[b][:, 0:n], AF.Ln, reads=[("ps", b), "eps"], writes=[("sd", tag)],
                 scale=inv_n, bias=self.eps[:, 0:1])
        self.act(rs[:, 0:n], sd[:, 0:n], AF.Exp, reads=[("sd", tag)], writes=[("rs", tag)], scale=-0.5)

    def build(self):
        SEQ, T = self.SEQ, self.T
        nc = bass.Bass("TRN2", target_bir_lowering=False)
        self.nc = nc

        def din(name, shape):
            return nc.dram_tensor(name, list(shape), F32, kind="ExternalInput").ap()

        def dscr(name, shape, dt):
            return nc.dram_tensor(name, list(shape), dt).ap()

        self.xT = din("xT", [D, SEQ])
        self.metaT = din("metaT", [D, NMETA])
        self.e_w_in = din("even_w_in", [2, D, 4288])
        self.w_uq = din("mla_w_uq", [2, 384, 8 * 192])
        self.w_ukv = din("mla_w_ukv", [2, 256, 8 * 256])
        self.e_w_out = din("even_w_out", [2, 2048, D])
        self.o_w_in = din("odd_w_in", [2, D, 4096])
        self.w_a = din("lru_w_a", [2, 2, 16, 128, 128])
        self.w_x = din("lru_w_x", [2, 2, 16, 128, 128])
        self.o_w_out = din("odd_w_out", [2, 2048, D])
        self.pp_e = din("pp_even", [2, 128, NPE])
        self.pp_o = din("pp_odd", [2, 128, NPO])
        self.cst = din("cst", [128, 4])
        self.outT = nc.dram_tensor("outT", [D, SEQ], F32, kind="ExternalOutput").ap()

        NB_ = len(self.blocks)
        self.hB = dscr("hB", [NB_, 128, 8 * 512], F32)
        self.zB = dscr("zB", [NB_, 128, 8 * 512], BF16)
        self.tab = dscr("tab", [4, 128, T], F32)
        self.qn = dscr("qn", [8, 128, T], BF16)
        self.qr = dscr("qr", [8, 64, T], BF16)
        self.kn = dscr("kn", [8, 128, T], BF16)
        self.kr = dscr("kr", [8, 64, T], BF16)
        self.vm = dscr("vm", [8, T, 128], BF16)
        self.qs = dscr("qs", [8, 128, T], BF16)
        self.ks = dscr("ks", [2, 128, T], BF16)
        self.vs = dscr("vs", [2, T, 128], BF16)
        self.sg = dscr("sg", [16, 128, T], BF16)
        self.yB = dscr("yB", [NB_, 128, 16 * 512], BF16)

        with contextlib.ExitStack() as gs:
            self.S = Sched(nc, gs)
            self.ps = [gs.enter_context(nc.psum_tensor("ps%d" % i, [128, 512], F32)) for i in range(8)]
            self.ones_f = self.sb(gs, [128, 128], F32, "ones_f")
            self.ones_r = self.sb(gs, [128, 128], F32R, "ones_r")
            self.ones_b = self.sb(gs, [128, 128], BF16, "ones_b")
            self.eps = self.sb(gs, [128, 1], F32, "eps")
            self.cst_t = self.sb(gs, [128, 4], F32, "cst")
            self.setup()
            for l in range(self.n_layers):
                j = l // 2
                if l == 0:
                    self.phase_norm(l)
                if l % 2 == 0:
                    self.phase_mla_proj(j)
                    self.phase_swa_proj(j)
                    self.phase_mla_attn(j)
                    self.phase_swa_attn(j)
                    if getattr(self, "debug_y", False):
                        cb_ = 8 if self.debug_y == 2 else 0
                        for c in range(8):
                            for bi, (t0, n) in enumerate(self.blocks):
                                if t0 >= NMETA:
                                    self.dma("pool", self.outT[c * 128:(c + 1) * 128, t0 - NMETA:t0 - NMETA + n],
                                             self.yblk(bi)[:, cb_ + c, :])
                        self.S.phase_end()
                        return nc
                    self.phase_out(l, self.e_w_out[j])
                else:
                    self.phase_lru(j)
                    self.phase_out(l, self.o_w_out[j])
            if self.n_layers < 4:
                self.phase_dump()
        return nc

    def setup(self):
        nc, T, SEQ = self.nc, self.T, self.SEQ
        with contextlib.ExitStack() as ph:
            self.memset("dve", self.ones_f[:], 1.0, ["ones_f"])
            self.memset("dve", self.ones_b[:], 1.0, ["ones_b"])
            self.memset("dve", self.eps[:], EPS, ["eps"])
            self.act(self.ones_r[:], self.ones_f[:], AF.Copy, ["ones_f"], ["ones_r"])
            self.dma("sp", self.cst_t[:], self.cst[:, :], writes=["cst"])
            for bi, (t0, n) in enumerate(self.blocks):
                if t0 < NMETA:
                    self.dma("sp", self.hblk(bi), self.metaT.rearrange("(c p) t -> p c t", p=128))
                else:
                    self.dma("sp", self.hblk(bi), self.xT[:, t0 - NMETA:t0 - NMETA + n].rearrange("(c p) t -> p c t", p=128))
            pos = self.sb(ph, [128, T], F32, "pos")
            ang = self.sb(ph, [128, T], F32, "ang")
            kf = self.sb(ph, [128, T], F32, "kf")
            ki = self.sb(ph, [128, T], I32, "ki")
            rr = self.sb(ph, [128, T], F32, "rr")
            sn = self.sb(ph, [128, T], F32, "sn")
            sgn = self.sb(ph, [128, 2], F32, "sgn")
            self.S.op("pool", lambda e: e.iota(pos[:], [[1, T]], base=0, channel_multiplier=0,
                                               allow_small_or_imprecise_dtypes=True), (), ["pos"])
            self.memset("dve", sgn[0:64, 0:1], 1.0, ["sgn0"])
            self.memset("dve", sgn[64:128, 0:1], -1.0, ["sgn1"])
            self.memset("dve", sgn[:, 1:2], 0.0, ["sgn2"])
            self.S.op("dve", lambda e: e.memset(sgn[0:32, 1:2], 1.0), ["sgn2"], ["sgn3"])
            self.S.op("dve", lambda e: e.memset(sgn[32:64, 1:2], -1.0), ["sgn2"], ["sgn4"])
            sgn_keys = ["sgn0", "sgn1", "sgn3", "sgn4"]
            for col in range(2):
                self.ts("dve", ang[:], pos[:], self.cst_t[:, col:col + 1], None, ALU.mult, None,
                        ["pos", "cst", "ang"], ["ang"])
                for which in range(2):
                    off = math.pi / 2 if which == 0 else 0.0
                    self.ts("dve", kf[:], ang[:], off, 1.0 / (2 * math.pi), ALU.add, ALU.mult, ["ang", "kf"], ["kf"])
                    self.copy("dve", ki[:], kf[:], ["kf", "ki"], ["ki"])
                    self.copy("dve", kf[:], ki[:], ["ki", "kf"], ["kf"])
                    self.stt(rr[:], kf[:], -2 * math.pi, ang[:], ALU.mult, ALU.add, ["kf", "ang", "rr"], ["rr"])
                    self.ts("dve", rr[:], rr[:], off, 3.1415925, ALU.add, ALU.min, ["rr"], ["rr"])
                    self.ts("dve", rr[:], rr[:], -3.1415925, None, ALU.max, None, ["rr"], ["rr"])
                    self.act(sn[:], rr[:], AF.Sin, ["rr", "sn"], ["sn"])
                    if which == 1:
                        self.ts("dve", sn[:], sn[:], sgn[:, col:col + 1], None, ALU.mult, None,
                                ["sn"] + sgn_keys, ["sn"])
                    self.dma("sp", self.tab[2 * col + which], sn[:], reads=["sn"])
            self.S.phase_end()

    def phase_norm(self, l):
        j = l // 2
        with contextlib.ExitStack() as ph:
            npp = NPE if l % 2 == 0 else NPO
            pp = self.sb(ph, [128, npp], F32, "pp")
            self.dma("sp", pp[:], (self.pp_e if l % 2 == 0 else self.pp_o)[j], writes=["pp"])
            hb = [self.sb(ph, [128, 8, 512], F32, "hb") for _ in range(2)]
            sq = [self.sb(ph, [128, 8, 512], F32R, "sq") for _ in range(2)]
            zb = [self.sb(ph, [128, 8, 512], BF16, "zb") for _ in range(2)]
            sd = self.sb(ph, [128, 512], F32, "sd")
            rs = self.sb(ph, [128, 512], F32, "rs")
            def load_n(bi):
                t0, n = self.blocks[bi]
                s = bi % 2
                self.dma("sp", hb[s][:, :, 0:n], self.hblk(bi), writes=[("hb", s)])

            load_n(0)
            for bi, (t0, n) in enumerate(self.blocks):
                s = bi % 2
                if bi + 1 < len(self.blocks):
                    load_n(bi + 1)
                self.act(sq[s][:, :, 0:n], hb[s][:, :, 0:n], AF.Square, [("hb", s)], [("sq", s)])
                self.rstd((sd, rs), [(self.ones_r[:], sq[s][:, c, 0:n]) for c in range(8)], [("sq", s)],
                          1.0 / D, n, "n")
                self.tt("dve", hb[s][:, :, 0:n], hb[s][:, :, 0:n],
                        rs[:, 0:n].unsqueeze(1).to_broadcast([128, 8, n]), ALU.mult,
                        [("hb", s), ("rs", "n")], [("hb", s)])
                self.tt("pool", zb[s][:, :, 0:n], hb[s][:, :, 0:n],
                        pp[:, 0:8].unsqueeze(2).to_broadcast([128, 8, n]), ALU.mult,
                        [("hb", s), "pp"], [("zb", s)])
                self.dma("sp", self.zblk(bi), zb[s][:, :, 0:n], reads=[("zb", s)])
            self.S.phase_end()

    def rope(self, out_bf, u, tcos, tsinx, rs, half, n, tmp1, tmp2, ukey, tkey, tskey, rskey, okey, tk1, tk2):
        full = 2 * half
        self.tt("dve", tmp1[0:full, 0:n], u[0:full, 0:n], tcos[0:full, 0:n], ALU.mult, [ukey, tkey], [tk1])
        self.tt("pool", tmp2[0:half, 0:n], u[half:full, 0:n], tsinx[half:full, 0:n], ALU.mult,
                [ukey, tskey], [(tk2, 0)])
        self.tt("pool", tmp2[half:full, 0:n], u[0:half, 0:n], tsinx[0:half, 0:n], ALU.mult,
                [ukey, tskey], [(tk2, 1)])
        self.tt("pool", tmp1[0:full, 0:n], tmp1[0:full, 0:n], tmp2[0:full, 0:n], ALU.add,
                [tk1, (tk2, 0), (tk2, 1)], [tk1])
        self.tt("dve", out_bf[0:full, 0:n], tmp1[0:full, 0:n], rs[0:full, 0:n], ALU.mult, [tk1, rskey], [okey])

    def phase_mla_proj(self, j):
        T = self.T
        with contextlib.ExitStack() as ph:
            pp = self.sb(ph, [128, NPE], F32, "pp")
            self.dma("sp", pp[:], self.pp_e[j], writes=["pp"])
            wA = self.sb(ph, [128, 8, 704], BF16, "wA")
            wuq = self.sb(ph, [128, 3, 1536], BF16, "wuq")
            wukv = self.sb(ph, [128, 2, 2048], BF16, "wukv")
            for k in range(8):
                self.dma("pool", wA[:, k, :], self.e_w_in[j, k * 128:(k + 1) * 128, 0:704], writes=[("wA", k)])
            for k in range(3):
                self.dma("pool", wuq[:, k, :], self.w_uq[j, k * 128:(k + 1) * 128, :], writes=[("wuq", k)])
            for k in range(2):
                self.dma("pool", wukv[:, k, :], self.w_ukv[j, k * 128:(k + 1) * 128, :], writes=[("wukv", k)])
            wA_k = [("wA", k) for k in range(8)]
            wuq_k = [("wuq", k) for k in range(3)]
            wukv_k = [("wukv", k) for k in range(2)]
            zb = [self.sb(ph, [128, 8, 512], BF16, "zb") for _ in range(2)]
            tcm = [self.sb(ph, [64, 512], F32, "tcm") for _ in range(2)]
            tsm = [self.sb(ph, [64, 512], F32, "tsm") for _ in range(2)]
            cq = self.sb(ph, [128, 3, 512], F32, "cq")
            sqc = self.sb(ph, [128, 3, 512], F32R, "sqc")
            cqn = self.sb(ph, [128, 3, 512], BF16, "cqn")
            ckv = self.sb(ph, [128, 2, 512], F32, "ckv")
            sqk = self.sb(ph, [128, 2, 512], F32R, "sqk")
            ckvn = self.sb(ph, [128, 2, 512], BF16, "ckvn")
            sd = self.sb(ph, [128, 512], F32, "sd")
            rs = self.sb(ph, [128, 512], F32, "rs")
            kpu = self.sb(ph, [64, 512], F32, "kpu")
            kpsq = self.sb(ph, [64, 512], F32R, "kpsq")
            kpr = self.sb(ph, [64, 512], F32, "kpr")
            t1 = self.sb(ph, [128, 512], F32, "t1")
            t2 = self.sb(ph, [128, 512], F32, "t2")
            NS = 3
            sqh = [self.sb(ph, [128, 512], F32R, "sqh") for _ in range(NS)]
            sqh2 = [self.sb(ph, [64, 512], F32R, "sqh2") for _ in range(NS)]
            uh = [self.sb(ph, [128, 512], F32, "uh") for _ in range(NS)]
            uh2 = [self.sb(ph, [64, 512], F32, "uh2") for _ in range(NS)]
            sdh = [self.sb(ph, [128, 512], F32, "sdh") for _ in range(NS)]
            rsh = [self.sb(ph, [128, 512], F32, "rsh") for _ in range(NS)]
            ob = [self.sb(ph, [128, 512], BF16, "ob") for _ in range(4)]
            ob2 = [self.sb(ph, [64, 512], BF16, "ob2") for _ in range(4)]
            vb = [self.sb(ph, [128, 512], BF16, "vb") for _ in range(2)]
            obi = 0
            ob2i = 0
            vbi = 0
            hs = 0
            def load_m(bi):
                t0, n = self.blocks[bi]
                s = bi % 2
                self.dma("sp", zb[s][:, :, 0:n], self.zblk(bi), writes=[("zb", s)])
                self.dma("sp", tcm[s][:, 0:n], self.tab[2, 0:64, t0:t0 + n], writes=[("tcm", s)])
                self.dma("sp", tsm[s][:, 0:n], self.tab[3, 0:64, t0:t0 + n], writes=[("tsm", s)])

            load_m(0)
            for bi, (t0, n) in enumerate(self.blocks):
                s = bi % 2
                if bi + 1 < len(self.blocks):
                    load_m(bi + 1)
                for mc in range(3):
                    b = self.nps()
                    self.mm(self.ps[b][:, 0:n], [(wA[:, k, mc * 128:(mc + 1) * 128], zb[s][:, k, 0:n]) for k in range(8)],
                            wA_k + [("zb", s)], [("ps", b)])
                    self.act(cq[:, mc, 0:n], self.ps[b][:, 0:n], AF.Copy, [("ps", b)], [("cq", mc)])
                    self.act(sqc[:, mc, 0:n], self.ps[b][:, 0:n], AF.Square, [("ps", b)], [("sqc", mc)])
                self.rstd((sd, rs), [(self.ones_r[:], sqc[:, mc, 0:n]) for mc in range(3)],
                          [("sqc", mc) for mc in range(3)], 1.0 / 384, n, "lat")
                self.tt("dve", cq[:, :, 0:n], cq[:, :, 0:n], rs[:, 0:n].unsqueeze(1).to_broadcast([128, 3, n]),
                        ALU.mult, [("cq", mc) for mc in range(3)] + [("rs", "lat")], [("cq", mc) for mc in range(3)])
                self.tt("pool", cqn[:, :, 0:n], cq[:, :, 0:n], pp[:, 8:11].unsqueeze(2).to_broadcast([128, 3, n]),
                        ALU.mult, [("cq", mc) for mc in range(3)] + ["pp"], ["cqn"])
                for mc in range(2):
                    b = self.nps()
                    self.mm(self.ps[b][:, 0:n], [(wA[:, k, 384 + mc * 128:384 + (mc + 1) * 128], zb[s][:, k, 0:n]) for k in range(8)],
                            wA_k + [("zb", s)], [("ps", b)])
                    self.act(ckv[:, mc, 0:n], self.ps[b][:, 0:n], AF.Copy, [("ps", b)], [("ckv", mc)])
                    self.act(sqk[:, mc, 0:n], self.ps[b][:, 0:n], AF.Square, [("ps", b)], [("sqk", mc)])
                self.rstd((sd, rs), [(self.ones_r[:], sqk[:, mc, 0:n]) for mc in range(2)],
                          [("sqk", mc) for mc in range(2)], 1.0 / 256, n, "lat")
                self.tt("dve", ckv[:, :, 0:n], ckv[:, :, 0:n], rs[:, 0:n].unsqueeze(1).to_broadcast([128, 2, n]),
                        ALU.mult, [("ckv", mc) for mc in range(2)] + [("rs", "lat")], [("ckv", mc) for mc in range(2)])
                self.tt("pool", ckvn[:, :, 0:n], ckv[:, :, 0:n], pp[:, 11:13].unsqueeze(2).to_broadcast([128, 2, n]),
                        ALU.mult, [("ckv", mc) for mc in range(2)] + ["pp"], ["ckvn"])
                b = self.nps()
                self.mm(self.ps[b][0:64, 0:n], [(wA[:, k, 640:704], zb[s][:, k, 0:n]) for k in range(8)],
                        wA_k + [("zb", s)], [("ps", b)])
                self.act(kpu[:, 0:n], self.ps[b][0:64, 0:n], AF.Copy, [("ps", b), "pp"], ["kpu"], scale=pp[0:64, 16:17])
                self.act(kpsq[:, 0:n], self.ps[b][0:64, 0:n], AF.Square, [("ps", b)], ["kpsq"])
                self.tt("dve", t1[0:64, 0:n], kpu[0:64, 0:n], tcm[s][0:64, 0:n], ALU.mult, ["kpu", ("tcm", s)], ["t1"])
                self.tt("pool", t2[0:32, 0:n], kpu[32:64, 0:n], tsm[s][32:64, 0:n], ALU.mult, ["kpu", ("tsm", s)], [("t2", 0)])
                self.tt("pool", t2[32:64, 0:n], kpu[0:32, 0:n], tsm[s][0:32, 0:n], ALU.mult, ["kpu", ("tsm", s)], [("t2", 1)])
                self.tt("dve", kpr[0:64, 0:n], t1[0:64, 0:n], t2[0:64, 0:n], ALU.add, ["t1", ("t2", 0), ("t2", 1)], ["kpr"])
                pend = [None]

                def flush_pend():
                    if pend[0] is not None:
                        f_ = pend[0]
                        pend[0] = None
                        f_()

                for h in range(8):
                    hs = (hs + 1) % NS
                    bn_ = self.nps()
                    self.mm(self.ps[bn_][:, 0:n], [(wuq[:, k, h * 192:h * 192 + 128], cqn[:, k, 0:n]) for k in range(3)],
                            wuq_k + ["cqn"], [("ps", bn_)])
                    br_ = self.nps()
                    self.mm(self.ps[br_][0:64, 0:n], [(wuq[:, k, h * 192 + 128:h * 192 + 192], cqn[:, k, 0:n]) for k in range(3)],
                            wuq_k + ["cqn"], [("ps", br_)])
                    self.act(sqh[hs][:, 0:n], self.ps[bn_][:, 0:n], AF.Square, [("ps", bn_)], [("sqh", hs)])
                    self.act(sqh2[hs][:, 0:n], self.ps[br_][0:64, 0:n], AF.Square, [("ps", br_)], [("sqh2", hs)])
                    self.act(uh[hs][:, 0:n], self.ps[bn_][:, 0:n], AF.Copy, [("ps", bn_), "pp"], [("uh", hs)], scale=pp[:, 13:14])
                    self.act(uh2[hs][:, 0:n], self.ps[br_][0:64, 0:n], AF.Copy, [("ps", br_), "pp"], [("uh2", hs)], scale=pp[0:64, 14:15])
                    flush_pend()

                    def fin_q(hs=hs, h=h, obi=obi, ob2i=ob2i):
                        self.rstd((sdh[hs], rsh[hs]), [(self.ones_r[:], sqh[hs][:, 0:n]), (self.ones_r[0:64, :], sqh2[hs][0:64, 0:n])],
                                  [("sqh", hs), ("sqh2", hs)], 1.0 / 192, n, ("h", hs))
                        o = ob[obi]
                        self.tt("dve", o[:, 0:n], uh[hs][:, 0:n], rsh[hs][:, 0:n], ALU.mult,
                                [("uh", hs), ("rs", ("h", hs))], [("ob", obi)])
                        self.dma("sp", self.qn[h, :, t0:t0 + n], o[:, 0:n], reads=[("ob", obi)])
                        o2 = ob2[ob2i]
                        self.rope(o2, uh2[hs], tcm[s], tsm[s], rsh[hs], 32, n, t1, t2, ("uh2", hs), ("tcm", s), ("tsm", s),
                                  ("rs", ("h", hs)), ("ob2", ob2i), "t1", "t2")
                        self.dma("sp", self.qr[h, :, t0:t0 + n], o2[0:64, 0:n], reads=[("ob2", ob2i)])
                    pend[0] = fin_q
                    obi = (obi + 1) % 4
                    ob2i = (ob2i + 1) % 4
                    hs = (hs + 1) % NS
                    bk_ = self.nps()
                    self.mm(self.ps[bk_][:, 0:n], [(wukv[:, k, h * 256:h * 256 + 128], ckvn[:, k, 0:n]) for k in range(2)],
                            wukv_k + ["ckvn"], [("ps", bk_)])
                    self.act(sqh[hs][:, 0:n], self.ps[bk_][:, 0:n], AF.Square, [("ps", bk_)], [("sqh", hs)])
                    self.act(uh[hs][:, 0:n], self.ps[bk_][:, 0:n], AF.Copy, [("ps", bk_), "pp"], [("uh", hs)], scale=pp[:, 15:16])
                    flush_pend()

                    def fin_k(hs=hs, h=h, obi=obi, ob2i=ob2i):
                        self.rstd((sdh[hs], rsh[hs]), [(self.ones_r[:], sqh[hs][:, 0:n]), (self.ones_r[0:64, :], kpsq[0:64, 0:n])],
                                  [("sqh", hs), "kpsq"], 1.0 / 192, n, ("h", hs))
                        o = ob[obi]
                        self.tt("dve", o[:, 0:n], uh[hs][:, 0:n], rsh[hs][:, 0:n], ALU.mult,
                                [("uh", hs), ("rs", ("h", hs))], [("ob", obi)])
                        self.dma("sp", self.kn[h, :, t0:t0 + n], o[:, 0:n], reads=[("ob", obi)])
                        o2 = ob2[ob2i]
                        self.tt("pool", o2[0:64, 0:n], kpr[0:64, 0:n], rsh[hs][0:64, 0:n], ALU.mult,
                                ["kpr", ("rs", ("h", hs))], [("ob2", ob2i)])
                        self.dma("sp", self.kr[h, :, t0:t0 + n], o2[0:64, 0:n], reads=[("ob2", ob2i)])
                    pend[0] = fin_k
                    obi = (obi + 1) % 4
                    ob2i = (ob2i + 1) % 4
                flush_pend()
                ntt = max(1, n // 128)
                for tt_ in range(ntt):
                    m = min(128, n)
                    c0 = tt_ * 128
                    for hg in range(2):
                        b = self.nps()
                        rhs = [wukv[:, k, :].rearrange("p (h e) -> p h e", e=256)[:, hg * 4:(hg + 1) * 4, 128:256]
                               for k in range(2)]
                        self.mm(self.ps[b][0:m, :].rearrange("p (h e) -> p h e", e=128),
                                [(ckvn[:, k, c0:c0 + m], rhs[k]) for k in range(2)],
                                wukv_k + ["ckvn"], [("ps", b)])
                        v = vb[vbi]
                        self.act(v[0:m, :], self.ps[b][0:m, :], AF.Copy, [("ps", b)], [("vb", vbi)])
                        self.dma("sp", self.vm[hg * 4:(hg + 1) * 4, t0 + c0:t0 + c0 + m, :].rearrange("h t e -> t h e"),
                                 v[0:m, :].rearrange("p (h e) -> p h e", e=128), reads=[("vb", vbi)])
                        vbi = (vbi + 1) % 2
            self.S.phase_end()

    def phase_swa_proj(self, j):
        T = self.T
        with contextlib.ExitStack() as ph:
            pp = self.sb(ph, [128, NPE], F32, "pp")
            self.dma("sp", pp[:], self.pp_e[j], writes=["pp"])
            wB = self.sb(ph, [128, 8, 3584], BF16, "wB")
            for k in range(8):
                for half in range(2):
                    self.dma("pool", wB[:, k, half * 1792:(half + 1) * 1792],
                             self.e_w_in[j, k * 128:(k + 1) * 128, 704 + half * 1792:704 + (half + 1) * 1792],
                             writes=[("wB", k, half)])
            wB_k = [("wB", k, hf) for k in range(8) for hf in range(2)]
            zb = [self.sb(ph, [128, 8, 512], BF16, "zb") for _ in range(2)]
            tcs = [self.sb(ph, [128, 512], F32, "tcs") for _ in range(2)]
            tss = [self.sb(ph, [128, 512], F32, "tss") for _ in range(2)]
            NS = 3
            sqh = [self.sb(ph, [128, 512], F32R, "sqh") for _ in range(NS)]
            uh = [self.sb(ph, [128, 512], F32, "uh") for _ in range(NS)]
            sdh = [self.sb(ph, [128, 512], F32, "sdh") for _ in range(NS)]
            rsh = [self.sb(ph, [128, 512], F32, "rsh") for _ in range(NS)]
            t1 = [self.sb(ph, [128, 512], F32, "t1") for _ in range(NS)]
            t2 = [self.sb(ph, [128, 512], F32, "t2") for _ in range(NS)]
            ob = [self.sb(ph, [128, 512], BF16, "ob") for _ in range(4)]
            vb = [self.sb(ph, [128, 256], BF16, "vb") for _ in range(2)]
            obi = 0
            vbi = 0
            hs = 0
            def load_s(bi):
                t0, n = self.blocks[bi]
                s = bi % 2
                self.dma("sp", zb[s][:, :, 0:n], self.zblk(bi), writes=[("zb", s)])
                self.dma("sp", tcs[s][:, 0:n], self.tab[0, :, t0:t0 + n], writes=[("tcs", s)])
                self.dma("sp", tss[s][:, 0:n], self.tab[1, :, t0:t0 + n], writes=[("tss", s)])

            load_s(0)
            for bi, (t0, n) in enumerate(self.blocks):
                s = bi % 2
                if bi + 1 < len(self.blocks):
                    load_s(bi + 1)
                pend = [None]

                def flush_pend():
                    if pend[0] is not None:
                        f_ = pend[0]
                        pend[0] = None
                        f_()

                for hh in range(10):
                    hs = (hs + 1) % NS
                    c0 = hh * 128 if hh < 8 else 1024 + (hh - 8) * 128
                    gcol = 17 if hh < 8 else 18
                    b = self.nps()
                    self.mm(self.ps[b][:, 0:n], [(wB[:, k, c0:c0 + 128], zb[s][:, k, 0:n]) for k in range(8)],
                            wB_k + [("zb", s)], [("ps", b)])
                    self.act(sqh[hs][:, 0:n], self.ps[b][:, 0:n], AF.Square, [("ps", b)], [("sqh", hs)])
                    self.act(uh[hs][:, 0:n], self.ps[b][:, 0:n], AF.Copy, [("ps", b), "pp"], [("uh", hs)],
                             scale=pp[:, gcol:gcol + 1])
                    flush_pend()

                    def fin(hs=hs, hh=hh, obi=obi):
                        self.rstd((sdh[hs], rsh[hs]), [(self.ones_r[:], sqh[hs][:, 0:n])], [("sqh", hs)], 1.0 / 128, n, ("h", hs))
                        o = ob[obi]
                        self.rope(o, uh[hs], tcs[s], tss[s], rsh[hs], 64, n, t1[hs], t2[hs], ("uh", hs), ("tcs", s), ("tss", s),
                                  ("rs", ("h", hs)), ("ob", obi), ("t1", hs), ("t2", hs))
                        dst = self.qs[hh, :, t0:t0 + n] if hh < 8 else self.ks[hh - 8, :, t0:t0 + n]
                        self.dma("sp", dst, o[:, 0:n], reads=[("ob", obi)])
                    pend[0] = fin
                    obi = (obi + 1) % 4
                flush_pend()
                ntt = max(1, n // 128)
                for tt_ in range(ntt):
                    m = min(128, n)
                    cc = tt_ * 128
                    b = self.nps()
                    self.mm(self.ps[b][0:m, 0:256], [(zb[s][:, k, cc:cc + m], wB[:, k, 1280:1536]) for k in range(8)],
                            wB_k + [("zb", s)], [("ps", b)])
                    v = vb[vbi]
                    self.act(v[0:m, :], self.ps[b][0:m, 0:256], AF.Copy, [("ps", b)], [("vb", vbi)])
                    self.dma("sp", self.vs[:, t0 + cc:t0 + cc + m, :].rearrange("g t e -> t g e"),
                             v[0:m, :].rearrange("p (g e) -> p g e", e=128), reads=[("vb", vbi)])
                    vbi = (vbi + 1) % 2
                for c in range(16):
                    b = self.nps()
                    c0 = 1536 + c * 128
                    self.mm(self.ps[b][:, 0:n], [(wB[:, k, c0:c0 + 128], zb[s][:, k, 0:n]) for k in range(8)],
                            wB_k + [("zb", s)], [("ps", b)])
                    o = ob[obi]
                    self.act(o[:, 0:n], self.ps[b][:, 0:n], AF.Silu, [("ps", b)], [("ob", obi)])
                    self.dma("sp", self.sg[c, :, t0:t0 + n], o[:, 0:n], reads=[("ob", obi)])
                    obi = (obi + 1) % 4
            self.S.phase_end()

    def phase_mla_attn(self, j):
        T, NT = self.T, self.NT
        scale = 192.0 ** -0.5
        with contextlib.ExitStack() as ph:
            knb = [self.sb(ph, [128, T], BF16, "knb") for _ in range(2)]
            krb = [self.sb(ph, [128, T], BF16, "krb") for _ in range(2)]
            vtb = [self.sb(ph, [128, NT + 1, 128], BF16, "vtb") for _ in range(2)]
            qnb = [self.sb(ph, [128, 512], BF16, "qnb") for _ in range(2)]
            qrb = [self.sb(ph, [128, 512], BF16, "qrb") for _ in range(2)]
            sgb = [self.sb(ph, [128, 512], BF16, "sgb") for _ in range(2)]
            NP = 4
            NPT = 8
            pt = [self.sb(ph, [128, 512], BF16, "pt") for _ in range(NPT)]
            pacc = [self.sb(ph, [128, 512], BF16, "pacc") for _ in range(2)]
            slot0 = [0]
            pend_ones = []
            rl = self.sb(ph, [128, 512], F32, "rl")
            of = self.sb(ph, [128, 512], F32, "of")
            yb = [self.sb(ph, [128, 512], BF16, "yb") for _ in range(2)]
            qbs = [(h, bi) for h in range(8) for bi in range(len(self.blocks))]
            units = [(qi, kt) for qi in range(len(qbs)) for kt in range(len(self.ktiles))]
            LOOK = 3

            def load_head(h):
                s = h % 2
                self.dma("sp", knb[s][:, :], self.kn[h], writes=[("knb", s)])
                self.dma("sp", krb[s][0:64, :], self.kr[h], writes=[("krb", s, 0)])
                self.dma("sp", krb[s][64:128, :], self.kr[h], writes=[("krb", s, 1)])
                self.dma("sp", vtb[s][0:NMETA, 0, :], self.vm[h, 0:NMETA, :], writes=[("vtb", s, 0)])
                self.dma("sp", vtb[s][:, 1:NT + 1, :], self.vm[h, NMETA:T, :].rearrange("(k p) e -> p k e", p=128),
                         writes=[("vtb", s, 1)])

            def load_q(qi):
                h, bi = qbs[qi]
                t0, n = self.blocks[bi]
                s = qi % 2
                self.dma("sp", qnb[s][:, 0:n], self.qn[h, :, t0:t0 + n], writes=[("qnb", s)])
                self.dma("sp", qrb[s][0:64, 0:n], self.qr[h, :, t0:t0 + n], writes=[("qrb", s, 0)])
                self.dma("sp", qrb[s][64:128, 0:n], self.qr[h, :, t0:t0 + n], writes=[("qrb", s, 1)])
                self.dma("sp", sgb[s][:, 0:n], self.sg[h, :, t0:t0 + n], writes=[("sgb", s)])

            def emit_pair(u0):
                us = [u for u in (u0, u0 + 1) if u < len(units)]
                info = []
                for u in us:
                    qi, kt = units[u]
                    h, bi = qbs[qi]
                    t0, n = self.blocks[bi]
                    k0, kn_ = self.ktiles[kt]
                    info.append((u, h % 2, qi % 2, n, k0, kn_, u % NP, (u % 2) * 64))

                def fn(e):
                    ins = None
                    for (u, hsl, qs_, n, k0, kn_, b, r0) in info:
                        ins = e.matmul(self.ps[b][0:kn_, 0:n], knb[hsl][:, k0:k0 + kn_], qnb[qs_][:, 0:n], start=True, stop=False)
                    for (u, hsl, qs_, n, k0, kn_, b, r0) in info:
                        ins = e.matmul(self.ps[b][0:kn_, 0:n], krb[hsl][r0:r0 + 64, k0:k0 + kn_], qrb[qs_][r0:r0 + 64, 0:n],
                                       start=False, stop=True)
                    return ins
                reads = []
                writes = []
                for (u, hsl, qs_, n, k0, kn_, b, r0) in info:
                    reads += [("knb", hsl), ("krb", hsl, 0), ("krb", hsl, 1), ("qnb", qs_), ("qrb", qs_, 0), ("qrb", qs_, 1)]
                    writes.append(("ps", b))
                self.S.op("pe", fn, reads, writes)

            load_head(0)
            load_q(0)
            emit_pair(0)
            for u, (qi, kt) in enumerate(units):
                h, bi = qbs[qi]
                t0, n = self.blocks[bi]
                k0, kn_ = self.ktiles[kt]
                hsl, qs_ = h % 2, qi % 2
                if kt == 0:
                    if qi + 1 < len(qbs):
                        if qbs[qi + 1][0] != h:
                            load_head(qbs[qi + 1][0])
                        load_q(qi + 1)
                if u % 2 == 0 and u + 2 < len(units):
                    emit_pair(u + 2)
                b = u % NP
                pb = u % NPT
                while pend_ones and pend_ones[0][0] <= u:
                    pend_ones.pop(0)[1]()
                bo = 4 + qi % 2
                bl = 6 + qi % 2
                self.act(pt[pb][0:kn_, 0:n], self.ps[b][0:kn_, 0:n], AF.Exp, [("ps", b)], [("pt", pb)], scale=scale)
                last = (kt == len(self.ktiles) - 1)
                vkey = ("vtb", hsl, 0 if kt == 0 else 1)
                self.mm(self.ps[bo][:, 0:n], [(vtb[hsl][0:kn_, kt, :], pt[pb][0:kn_, 0:n])],
                        [vkey, ("pt", pb)], [("ps", bo)], start=(kt == 0), stop=last)
                if kt == 0:
                    self.mm(self.ps[bl][:, 0:n], [(self.ones_b[0:kn_, :], pt[pb][0:kn_, 0:n])],
                            ["ones_b", ("pt", pb)], [("ps", bl)], start=True, stop=last)
                else:
                    gi = (kt - 1) % 4
                    qd = ((kt - 1) // 4 + qi) % 2
                    if gi == 0:
                        slot0[0] = pb
                    elif gi == 1:
                        self.tt("dve", pacc[qd][:, 0:n], pt[slot0[0]][:, 0:n], pt[pb][:, 0:n], ALU.add,
                                [("pt", slot0[0]), ("pt", pb)], [("pacc", qd)])
                    else:
                        self.tt("dve", pacc[qd][:, 0:n], pacc[qd][:, 0:n], pt[pb][:, 0:n], ALU.add,
                                [("pacc", qd), ("pt", pb)], [("pacc", qd)])
                    if gi == 3:
                        def ones_mm(bl=bl, qd=qd, n=n, last=last):
                            self.mm(self.ps[bl][:, 0:n], [(self.ones_b[:, :], pacc[qd][:, 0:n])],
                                    ["ones_b", ("pacc", qd)], [("ps", bl)], start=False, stop=last)
                        if last:
                            ones_mm()
                        else:
                            pend_ones.append((u + 2, ones_mm))
                if last:
                    self.recip(rl[:, 0:n], self.ps[bl][:, 0:n], [("ps", bl)], ["rl"])
                    self.tt("dve", of[:, 0:n], self.ps[bo][:, 0:n], rl[:, 0:n], ALU.mult, [("ps", bo), "rl"], ["of"])
                    ys = qi % 2
                    self.tt("pool", yb[ys][:, 0:n], of[:, 0:n], sgb[qs_][:, 0:n], ALU.mult, ["of", ("sgb", qs_)], [("yb", ys)])
                    self.dma("sp", self.yblk(bi)[:, h, :], yb[ys][:, 0:n], reads=[("yb", ys)])
            self.S.phase_end()

    def phase_swa_attn(self, j):
        T, NT = self.T, self.NT
        scale = 128.0 ** -0.5
        with contextlib.ExitStack() as ph:
            pp = self.sb(ph, [128, NPE], F32, "pp")
            self.dma("sp", pp[:], self.pp_e[j], writes=["pp"])
            esk = self.sb(ph, [128, 8], F32, "esk")
            self.act(esk[:], pp[:, 19:27], AF.Exp, ["pp"], ["esk"])
            ones_m = self.sb(ph, [128, 128], BF16, "ones_m")
            mprev = self.sb(ph, [128, 128], BF16, "mprev")
            mnext = self.sb(ph, [128, 128], BF16, "mnext")
            mmeta = self.sb(ph, [128, 16], BF16, "mmeta")
            self.memset("pool", ones_m[:], 1.0, ["ones_m"])
            self.S.op("pool", lambda e: e.affine_select(out=mprev[:], in_=ones_m[:], pattern=[[-1, 128]], base=0,
                                                        channel_multiplier=1, compare_op=ALU.is_ge, fill=0.0),
                      ["ones_m"], ["mprev"])
            self.S.op("pool", lambda e: e.affine_select(out=mnext[:], in_=ones_m[:], pattern=[[1, 128]], base=0,
                                                        channel_multiplier=-1, compare_op=ALU.is_ge, fill=0.0),
                      ["ones_m"], ["mnext"])
            self.S.op("pool", lambda e: e.affine_select(out=mmeta[:], in_=ones_m[:, 0:16], pattern=[[1, 16]], base=112,
                                                        channel_multiplier=-1, compare_op=ALU.is_ge, fill=0.0),
                      ["ones_m"], ["mmeta"])
            ksb = [self.sb(ph, [128, T], BF16, "ksb") for _ in range(2)]
            vtb = [self.sb(ph, [128, NT + 1, 128], BF16, "vtb") for _ in range(2)]
            q4 = [self.sb(ph, [128, 4, 512], BF16, "q4") for _ in range(3)]
            sg4 = [self.sb(ph, [128, 4, 512], BF16, "sg4") for _ in range(3)]
            NP = 4
            pt = [self.sb(ph, [128, 4, 128], BF16, "pt") for _ in range(NP)]
            lf = self.sb(ph, [128, 4, 128], F32, "lf")
            rl = self.sb(ph, [128, 4, 128], F32, "rl")
            of = self.sb(ph, [128, 4, 128], F32, "of")
            yb = [self.sb(ph, [128, 4, 128], BF16, "yb") for _ in range(2)]
            pi = 0
            qt_count = 0
            for g in range(2):
                self.dma("sp", ksb[g][:, :], self.ks[g], writes=[("ksb", g)])
                self.dma("sp", vtb[g][0:NMETA, 0, :], self.vs[g, 0:NMETA, :], writes=[("vtb", g, 0)])
                self.dma("sp", vtb[g][:, 1:NT + 1, :], self.vs[g, NMETA:T, :].rearrange("(k p) e -> p k e", p=128),
                         writes=[("vtb", g, 1)])
            blks = [(g, bi) for g in range(2) for bi in range(len(self.blocks))]
            qts = []
            units = []
            for bk, (g, bi) in enumerate(blks):
                t0, n = self.blocks[bi]
                nq = 1 if n == NMETA else n // 128
                for qt in range(nq):
                    w = NMETA if n == NMETA else 128
                    if n == NMETA:
                        tiles = [(0, None, None), (1, mmeta, "mmeta")]
                    else:
                        kti = (t0 - NMETA) // 128 + qt + 1
                        tiles = [(0, None, None)]
                        if kti - 1 >= 1:
                            tiles.append((kti - 1, mprev, "mprev"))
                        tiles.append((kti, None, None))
                        if kti + 1 <= NT:
                            tiles.append((kti + 1, mnext, "mnext"))
                    qts.append(dict(bk=bk, g=g, t0=t0, n=n, s=bk % 3, c0=qt * 128, w=w, tiles=tiles, qt=qt))
                    for ti in range(len(tiles)):
                        units.append((len(qts) - 1, ti))
            LOOK = 3

            def load_blk(bk):
                g, bi = blks[bk]
                t0, n = self.blocks[bi]
                s_ = bk % 3
                self.dma("sp", q4[s_][:, :, 0:n], self.qs[g * 4:(g + 1) * 4, :, t0:t0 + n].rearrange("r p t -> p r t"),
                         writes=[("q4", s_)])
                self.dma("sp", sg4[s_][:, :, 0:n], self.sg[8 + g * 4:8 + (g + 1) * 4, :, t0:t0 + n].rearrange("r p t -> p r t"),
                         writes=[("sg4", s_)])

            loaded = [0]

            def ensure_loaded(bk):
                while loaded[0] <= bk and loaded[0] < len(blks):
                    load_blk(loaded[0])
                    loaded[0] += 1

            def emit_s(u):
                qi, ti = units[u]
                q = qts[qi]
                ensure_loaded(q["bk"])
                kt = q["tiles"][ti][0]
                k0, kn_ = self.ktiles[kt]
                b = u % NP
                w = q["w"]
                self.mm(self.ps[b][0:kn_, 0:4 * w].rearrange("p (r i) -> p r i", r=4),
                        [(ksb[q["g"]][:, k0:k0 + kn_], q4[q["s"]][:, :, q["c0"]:q["c0"] + w])],
                        [("ksb", q["g"]), ("q4", q["s"])], [("ps", b)])

            for u in range(min(LOOK, len(units))):
                emit_s(u)
            for u, (qi, ti) in enumerate(units):
                q = qts[qi]
                g, w, c0, s_, t0 = q["g"], q["w"], q["c0"], q["s"], q["t0"]
                kt, mask, mk = q["tiles"][ti]
                k0, kn_ = self.ktiles[kt]
                b = u % NP
                bo = 4 + qi % 2
                bl = 6 + qi % 2
                if q["qt"] == 0 and ti == 0:
                    ensure_loaded(q["bk"] + 1)
                pv = pt[b][0:kn_, :, 0:w]
                self.act(pv, self.ps[b][0:kn_, 0:4 * w].rearrange("p (r i) -> p r i", r=4), AF.Exp,
                         [("ps", b)], [("pt", b)], scale=scale)
                if mask is not None:
                    self.tt("pool", pv, pv, mask[0:kn_, 0:w].unsqueeze(1).to_broadcast([kn_, 4, w]), ALU.mult,
                            [("pt", b), mk], [("pt", b)])
                vkey = ("vtb", g, 0 if kt == 0 else 1)
                first, last = (ti == 0), (ti == len(q["tiles"]) - 1)
                self.mm(self.ps[bo][:, 0:4 * w].rearrange("p (r i) -> p r i", r=4),
                        [(vtb[g][0:kn_, kt, :], pv)], [vkey, ("pt", b)], [("ps", bo)], start=first, stop=last)
                self.mm(self.ps[bl][:, 0:4 * w].rearrange("p (r i) -> p r i", r=4),
                        [(self.ones_b[0:kn_, :], pv)], ["ones_b", ("pt", b)], [("ps", bl)], start=first, stop=last)
                if u + LOOK < len(units):
                    emit_s(u + LOOK)
                if last:
                    lv = lf[:, :, 0:w]
                    self.tt("dve", lv, self.ps[bl][:, 0:4 * w].rearrange("p (r i) -> p r i", r=4),
                            esk[:, g * 4:(g + 1) * 4].unsqueeze(2).to_broadcast([128, 4, w]), ALU.add,
                            [("ps", bl), "esk"], ["lf"])
                    self.act(lv, lv, AF.Ln, ["lf"], ["lf"])
                    self.act(rl[:, :, 0:w], lv, AF.Exp, ["lf"], ["rl"], scale=-1.0)
                    self.tt("dve", of[:, :, 0:w], self.ps[bo][:, 0:4 * w].rearrange("p (r i) -> p r i", r=4),
                            rl[:, :, 0:w], ALU.mult, [("ps", bo), "rl"], ["of"])
                    ys = qi % 2
                    self.tt("dve", yb[ys][:, :, 0:w], of[:, :, 0:w], sg4[s_][:, :, c0:c0 + w], ALU.mult,
                            ["of", ("sg4", s_)], [("yb", ys)])
                    self.dma("sp", self.yblk(blks[q["bk"]][1])[:, 8 + g * 4:8 + (g + 1) * 4, c0:c0 + w],
                             yb[ys][:, :, 0:w], reads=[("yb", ys)])
            self.S.phase_end()

    def phase_out(self, l, w_out):
        T = self.T
        final = (l == 3)
        with contextlib.ExitStack() as ph:
            wo = self.sb(ph, [128, 16, D], BF16, "wo")
            for k in range(16):
                self.dma("pool", wo[:, k, :], w_out[k * 128:(k + 1) * 128, :], writes=[("wo", k)])
            wo_k = [("wo", k) for k in range(16)]
            yb = [self.sb(ph, [128, 16, 512], BF16, "yb") for _ in range(2)]
            hb = [self.sb(ph, [128, 8, 512], F32, "hb") for _ in range(2)]
            fuse = (not final) and (l + 1 < self.n_layers)
            if fuse:
                ln_ = l + 1
                npp = NPE if ln_ % 2 == 0 else NPO
                ppn = self.sb(ph, [128, npp], F32, "ppn")
                self.dma("sp", ppn[:], (self.pp_e if ln_ % 2 == 0 else self.pp_o)[ln_ // 2], writes=["ppn"])
                sqn = [self.sb(ph, [128, 8, 512], F32R, "sqn") for _ in range(2)]
                hn = [self.sb(ph, [128, 8, 512], F32, "hn") for _ in range(2)]
                zbn = [self.sb(ph, [128, 8, 512], BF16, "zbn") for _ in range(2)]
                sdn = self.sb(ph, [128, 512], F32, "sdn")
                rsn = self.sb(ph, [128, 512], F32, "rsn")

            def load_o(bi):
                t0, n = self.blocks[bi]
                s = bi % 2
                self.dma("sp", yb[s][:, :, 0:n], self.yblk(bi), writes=[("yb", s)])
                self.dma("sp", hb[s][:, :, 0:n], self.hblk(bi), writes=[("hb", s)])

            load_o(0)
            for bi, (t0, n) in enumerate(self.blocks):
                s = bi % 2
                if bi + 1 < len(self.blocks):
                    load_o(bi + 1)
                for mc in range(8):
                    b = self.nps()
                    self.mm(self.ps[b][:, 0:n], [(wo[:, k, mc * 128:(mc + 1) * 128], yb[s][:, k, 0:n]) for k in range(16)],
                            wo_k + [("yb", s)], [("ps", b)])
                    self.tt("dve", hb[s][:, mc, 0:n], hb[s][:, mc, 0:n], self.ps[b][:, 0:n], ALU.add,
                            [("hb", s), ("ps", b)], [("hb", s)])
                if final:
                    if t0 >= NMETA:
                        self.dma("sp", self.outT[:, t0 - NMETA:t0 - NMETA + n].rearrange("(c p) t -> p c t", p=128),
                                 hb[s][:, :, 0:n], reads=[("hb", s)])
                else:
                    self.dma("sp", self.hblk(bi), hb[s][:, :, 0:n], reads=[("hb", s)])
                if fuse:
                    self.act(sqn[s][:, :, 0:n], hb[s][:, :, 0:n], AF.Square, [("hb", s)], [("sqn", s)])
                    self.rstd((sdn, rsn), [(self.ones_r[:], sqn[s][:, c, 0:n]) for c in range(8)], [("sqn", s)],
                              1.0 / D, n, "nf")
                    self.tt("dve", hn[s][:, :, 0:n], hb[s][:, :, 0:n],
                            rsn[:, 0:n].unsqueeze(1).to_broadcast([128, 8, n]), ALU.mult,
                            [("hb", s), ("rs", "nf")], [("hn", s)])
                    self.tt("pool", zbn[s][:, :, 0:n], hn[s][:, :, 0:n],
                            ppn[:, 0:8].unsqueeze(2).to_broadcast([128, 8, n]), ALU.mult,
                            [("hn", s), "ppn"], [("zbn", s)])
                    self.dma("sp", self.zblk(bi), zbn[s][:, :, 0:n], reads=[("zbn", s)])
            self.S.phase_end()

    def phase_dump(self):
        for bi, (t0, n) in enumerate(self.blocks):
            if t0 >= NMETA:
                self.dma("sp", self.outT[:, t0 - NMETA:t0 - NMETA + n].rearrange("(c p) t -> p c t", p=128), self.hblk(bi))
        self.S.phase_end()

    def phase_lru(self, j):
        T = self.T
        NC = 16
        with contextlib.ExitStack() as ph:
            pp = self.sb(ph, [128, NPO], F32, "pp")
            self.dma("sp", pp[:], self.pp_o[j], writes=["pp"])
            cs = self.sb(ph, [128, 32], F32, "cs")
            one_t = self.sb(ph, [128, 1], F32, "one_t")
            self.memset("dve", one_t[:], 1.0, ["one_t"])
            self.act(cs[:], pp[:, 152:184], AF.Exp, ["pp"], ["cs"], scale=-1.0)
            self.act(cs[:], cs[:], AF.Ln, ["cs", "one_t"], ["cs"], bias=one_t[:, 0:1])
            self.ts("dve", cs[:], cs[:], -8.0, None, ALU.mult, None, ["cs"], ["cs"])
            cs2 = self.sb(ph, [128, 32], F32, "cs2")
            self.ts("dve", cs2[:], cs[:], 2.0, None, ALU.mult, None, ["cs"], ["cs2"])
            NZ = 2
            zb = [self.sb(ph, [128, 8, 512], BF16, "zb") for _ in range(NZ)]
            wu = [self.sb(ph, [128, 8, 128], BF16, "wu") for _ in range(2)]
            wg = [self.sb(ph, [128, 8, 128], BF16, "wg") for _ in range(2)]
            wgt = [self.sb(ph, [128, 4, 128], BF16, "wgt") for _ in range(2)]
            u = [self.sb(ph, [128, T + 3], F32, "u") for _ in range(2)]
            xc = [self.sb(ph, [128, T], F32, "xc") for _ in range(2)]
            xcb = [self.sb(ph, [128, T], BF16, "xcb") for _ in range(2)]
            sgc = [self.sb(ph, [128, T], BF16, "sgc") for _ in range(3)]
            sgs = [self.sb(ph, [128, 512], F32, "sgs") for _ in range(2)]
            gpre = [self.sb(ph, [128, 512], F32, "gpre") for _ in range(2)]
            ra = self.sb(ph, [128, T], F32, "ra")
            ib = self.sb(ph, [128, T], F32, "ib")
            a2 = self.sb(ph, [128, T], F32, "a2")
            hh0 = self.sb(ph, [128, T], F32, "hh0")
            for s in range(2):
                self.memset("dve", u[s][:, 0:2], 0.0, [("upad0", s)])
                self.memset("dve", u[s][:, T + 2:T + 3], 0.0, [("upad1", s)])
            zcount = [0]

            def load_wa(c):
                if c >= NC:
                    return
                s = c % 2
                self.dma("pool", wu[s][:, :, :], self.o_w_in[j, :, c * 128:(c + 1) * 128].rearrange("(k p) e -> p k e", p=128),
                         writes=[("wu", s)])
                self.dma("pool", wg[s][:, :, :], self.o_w_in[j, :, 2048 + c * 128:2048 + (c + 1) * 128].rearrange("(k p) e -> p k e", p=128),
                         writes=[("wg", s)])

            def load_wb(c):
                if c >= NC:
                    return
                s = c % 2
                for d_ in range(2):
                    self.dma("pool", wgt[s][:, d_, :], self.w_a[j, d_, c], writes=[("wgt", s, d_)])
                    self.dma("pool", wgt[s][:, 2 + d_, :], self.w_x[j, d_, c], writes=[("wgt", s, 2 + d_)])

            def a_block(c, bi, t0, n):
                s, s3 = c % 2, c % 3
                zs = zcount[0] % NZ
                zcount[0] += 1
                self.dma("sp", zb[zs][:, :, 0:n], self.zblk(bi), writes=[("zb", zs)])
                b = self.nps()
                self.mm(self.ps[b][:, 0:n], [(wu[s][:, k, :], zb[zs][:, k, 0:n]) for k in range(8)],
                        [("zb", zs), ("wu", s)], [("ps", b)])
                self.act(u[s][:, 2 + t0:2 + t0 + n], self.ps[b][:, 0:n], AF.Copy, [("ps", b)], [("u", s, t0)])
                b = self.nps()
                self.mm(self.ps[b][:, 0:n], [(wg[s][:, k, :], zb[zs][:, k, 0:n]) for k in range(8)],
                        [("zb", zs), ("wg", s)], [("ps", b)])
                self.act(sgs[zs][:, 0:n], self.ps[b][:, 0:n], AF.Sigmoid, [("ps", b)], [("sgs", zs)])
                self.act(gpre[zs][:, 0:n], self.ps[b][:, 0:n], AF.Copy, [("ps", b)], [("gpre", zs)])
                self.tt("pool", sgc[s3][:, t0:t0 + n], gpre[zs][:, 0:n], sgs[zs][:, 0:n], ALU.mult,
                        [("gpre", zs), ("sgs", zs)], [("sgc", s3, t0)])

            def conv(c):
                s = c % 2
                ukeys = [("u", s, t0) for (t0, n) in self.blocks] + [("upad0", s), ("upad1", s)]
                self.ts("dve", xc[s][:, :], u[s][:, 0:T], pp[:, 24 + c:25 + c], pp[:, 8 + c:9 + c], ALU.mult, ALU.add,
                        ukeys + ["pp"], [("xc", s)])
                for tap in range(1, 4):
                    self.stt(xc[s][:, :], u[s][:, tap:tap + T], pp[:, 24 + tap * 16 + c:25 + tap * 16 + c], xc[s][:, :],
                             ALU.mult, ALU.add, ukeys + ["pp", ("xc", s)], [("xc", s)])
                self.copy("dve", xcb[s][:, :], xc[s][:, :], [("xc", s)], [("xcb", s)])

            def stage_b(c, fillers, mid):
                s = c % 2
                rakeys = [("ra", t0) for (t0, n) in self.blocks]
                ibkeys = [("ib", t0) for (t0, n) in self.blocks]
                sgkeys = [("sgc", c % 3, t0) for (t0, n) in self.blocks]
                cnt = [0]

                def fill():
                    cnt[0] += 1
                    if cnt[0] % 4 == 0 and fillers:
                        fillers.pop(0)()

                for d_ in range(2):
                    for (t0, n) in self.blocks:
                        b = self.nps()
                        self.mm(self.ps[b][:, 0:n], [(wgt[s][:, d_, :], xcb[s][:, t0:t0 + n])], [("wgt", s, d_), ("xcb", s)], [("ps", b)])
                        self.act(ra[:, t0:t0 + n], self.ps[b][:, 0:n], AF.Sigmoid, [("ps", b), "pp"], [("ra", t0)],
                                 bias=pp[:, 88 + d_ * 16 + c:89 + d_ * 16 + c])
                        fill()
                    for (t0, n) in self.blocks:
                        b = self.nps()
                        self.mm(self.ps[b][:, 0:n], [(wgt[s][:, 2 + d_, :], xcb[s][:, t0:t0 + n])], [("wgt", s, 2 + d_), ("xcb", s)], [("ps", b)])
                        self.act(ib[:, t0:t0 + n], self.ps[b][:, 0:n], AF.Sigmoid, [("ps", b), "pp"], [("ib", t0)],
                                 bias=pp[:, 120 + d_ * 16 + c:121 + d_ * 16 + c])
                        fill()
                    if d_ == 1:
                        while fillers:
                            fillers.pop(0)()
                    self.act(a2[:, :], ra[:, :], AF.Exp, rakeys + ["cs2", "a2"], ["a2"], scale=cs2[:, d_ * 16 + c:d_ * 16 + c + 1])
                    self.act(ra[:, :], ra[:, :], AF.Exp, rakeys + ["cs"], rakeys, scale=cs[:, d_ * 16 + c:d_ * 16 + c + 1])
                    self.act(a2[:, :], a2[:, :], AF.Sqrt, ["a2", "one_t"], ["a2"], scale=-1.0, bias=one_t[:, 0:1])
                    self.tt("pool", ib[:, :], ib[:, :], xc[s][:, :], ALU.mult, ibkeys + [("xc", s)], ibkeys)
                    self.tt("dve", ib[:, :], ib[:, :], a2[:, :], ALU.mult, ibkeys + ["a2"], ibkeys)
                    if d_ == 0:
                        self.S.op("dve", lambda e: e.tensor_tensor_scan(hh0[:, :], ra[:, :], ib[:, :], 0.0, ALU.mult, ALU.add),
                                  rakeys + ibkeys, ["hh0"])
                        if mid is not None:
                            mid()
                    else:
                        self.S.op("dve", lambda e: e.tensor_tensor_scan(a2[:, ::-1], ra[:, ::-1], ib[:, ::-1], 0.0, ALU.mult, ALU.add),
                                  rakeys + ibkeys, ["a2"])
                self.tt("dve", hh0[:, :], hh0[:, :], a2[:, :], ALU.add, ["hh0", "a2"], ["hh0"])
                self.tt("dve", xcb[s][:, :], hh0[:, :], sgc[c % 3][:, :], ALU.mult, ["hh0"] + sgkeys, [("xcb", s)])
                for bi, (t0, n) in enumerate(self.blocks):
                    self.dma("pool", self.yblk(bi)[:, c, :], xcb[s][:, t0:t0 + n], reads=[("xcb", s)])

            def a_fillers(c):
                if c >= NC:
                    return []
                return [(lambda bi=bi, t0=t0, n=n: a_block(c, bi, t0, n)) for bi, (t0, n) in enumerate(self.blocks)]

            load_wa(0)
            load_wa(1)
            load_wb(0)
            load_wb(1)
            for c in range(2):
                for f in a_fillers(c):
                    f()
                load_wa(c + 2)
            conv(0)
            for k in range(NC):
                fl = a_fillers(k + 2)
                stage_b(k, fl, (lambda k=k: conv(k + 1)) if k + 1 < NC else None)
                load_wa(k + 4)
                load_wb(k + 2)
            self.S.phase_end()


def _pack_params(inp):
    f = lambda a: np.asarray(a, dtype=np.float32)
    pe = np.zeros((2, 128, NPE), np.float32)
    po = np.zeros((2, 128, NPO), np.float32)
    for j in range(2):
        pe[j, :, 0:8] = f(inp["norm_g"])[2 * j].reshape(8, 128).T
        pe[j, :, 8:11] = f(inp["mla_g_q_lat"])[j].reshape(3, 128).T
        pe[j, :, 11:13] = f(inp["mla_g_kv_lat"])[j].reshape(2, 128).T
        pe[j, :, 13] = f(inp["mla_g_qn"])[j, 0:128]
        pe[j, 0:64, 14] = f(inp["mla_g_qn"])[j, 128:192]
        pe[j, :, 15] = f(inp["mla_g_kn"])[j, 0:128]
        pe[j, 0:64, 16] = f(inp["mla_g_kn"])[j, 128:192]
        pe[j, :, 17] = f(inp["swa_g_qn"])[j]
        pe[j, :, 18] = f(inp["swa_g_kn"])[j]
        pe[j, :, 19:27] = np.broadcast_to(f(inp["swa_sink"])[j][None, :], (128, 8))
        po[j, :, 0:8] = f(inp["norm_g"])[2 * j + 1].reshape(8, 128).T
        po[j, :, 8:24] = f(inp["lru_conv_b"])[j].reshape(16, 128).T
        for tap in range(4):
            po[j, :, 24 + tap * 16:24 + (tap + 1) * 16] = f(inp["lru_conv_w"])[j, tap].reshape(16, 128).T
        for d_ in range(2):
            po[j, :, 88 + d_ * 16:88 + (d_ + 1) * 16] = f(inp["lru_b_a"])[j, d_].reshape(16, 128).T
            po[j, :, 120 + d_ * 16:120 + (d_ + 1) * 16] = f(inp["lru_b_x"])[j, d_].reshape(16, 128).T
            po[j, :, 152 + d_ * 16:152 + (d_ + 1) * 16] = f(inp["lru_lambda"])[j, d_].reshape(16, 128).T
    return pe, po


def _rope_consts():
    cst = np.zeros((128, 4), np.float32)
    inv128 = (10000.0 ** (-np.arange(0, 128, 2, dtype=np.float32) / np.float32(128))).astype(np.float32)
    inv64 = (10000.0 ** (-np.arange(0, 64, 2, dtype=np.float32) / np.float32(64))).astype(np.float32)
    cst[:, 0] = np.concatenate([inv128, inv128])
    cst[0:64, 1] = np.concatenate([inv64, inv64])
    return cst


def make_in_maps(inp, n_cores):
    f = lambda a: np.ascontiguousarray(np.asarray(a, dtype=np.float32))
    pe, po = _pack_params(inp)
    shared = {
        "metaT": f(np.asarray(inp["meta_tokens"]).T),
        "even_w_in": f(inp["even_w_in"]),
        "mla_w_uq": f(np.asarray(inp["mla_w_uq"]).reshape(2, 384, 8 * 192)),
        "mla_w_ukv": f(np.asarray(inp["mla_w_ukv"]).reshape(2, 256, 8 * 256)),
        "even_w_out": f(inp["even_w_out"]),
        "odd_w_in": f(inp["odd_w_in"]),
        "lru_w_a": f(inp["lru_w_a"]),
        "lru_w_x": f(inp["lru_w_x"]),
        "odd_w_out": f(inp["odd_w_out"]),
        "pp_even": pe, "pp_odd": po, "cst": _rope_consts(),
    }
    x = np.asarray(inp["x"], dtype=np.float32)
    maps = []
    for b in range(n_cores):
        m = dict(shared)
        m["xT"] = np.ascontiguousarray(x[b].T)
        maps.append(m)
    return maps


_CACHE = {}


def kernel(**inputs):
    x = np.asarray(inputs["x"])
    B, SEQ, _ = x.shape
    key = (SEQ, 4)
    if key not in _CACHE:
        _CACHE[key] = Prog(SEQ, 4).build()
    nc = _CACHE[key]
    in_maps = make_in_maps(inputs, B)
    res = run_bass_kernel_spmd(nc, in_maps, core_ids=list(range(B)))
    out = np.stack([np.asarray(r["outT"]).T for r in res.results], axis=0)
    return np.ascontiguousarray(out.astype(np.float32))
```

```python
import math
import contextlib
import numpy as np
import concourse.bass as bass
import concourse.mybir as mybir
from concourse.bass_utils import run_bass_kernel_spmd

F32 = mybir.dt.float32
F32R = mybir.dt.float32r
BF16 = mybir.dt.bfloat16
I32 = mybir.dt.int32
AF = mybir.ActivationFunctionType
ALU = mybir.AluOpType

D = 1024
NMETA = 16
EPS = 1e-6
NPE = 27
NPO = 184


class _Op:
    __slots__ = ("eng", "fn", "dma", "deps", "needs_sig", "sem", "val", "done", "order")

    def __init__(self, eng, fn, dma):
        self.eng = eng
        self.fn = fn
        self.dma = dma
        self.deps = ()
        self.needs_sig = False
        self.sem = None
        self.val = 0
        self.done = False
        self.order = 0


class Sched:
    COMPUTE = ("pe", "act", "dve", "pool")
    RING = 16

    def __init__(self, nc, stack):
        self.nc = nc
        self.E = {"pe": nc.tensor, "act": nc.scalar, "dve": nc.vector,
                  "pool": nc.gpsimd, "sp": nc.sync}
        self.csem = {e: stack.enter_context(nc.semaphore("c_" + e)) for e in self.COMPUTE}
        self.ccount = {e: 0 for e in self.COMPUTE}
        self.rings = {}
        for q in ("sp", "pool"):
            self.rings[q] = [stack.enter_context(nc.semaphore("d_%s%d" % (q, i)))
                             for i in range(self.RING)]
        self.dcount = {q: 0 for q in self.rings}
        self.ops = []
        self.last_w = {}
        self.readers = {}
        self.known = {e: {} for e in self.E}
        self.n_inst = 0
        self.n_ops = 0

    def op(self, eng, fn, reads=(), writes=(), dma=False):
        o = _Op(eng, fn, dma)
        deps = {}

        def add(d):
            if d is None or d.done:
                return
            if (not d.dma) and (not dma) and d.eng == "pe" and eng == "pe":
                return
            deps[id(d)] = d

        for k in reads:
            add(self.last_w.get(k))
        for k in writes:
            add(self.last_w.get(k))
            r = self.readers.get(k)
            if r:
                for d in r[0].values():
                    add(d)
                for d in r[1]:
                    add(d)
        best = {}
        out = []
        for d in deps.values():
            if d.dma:
                out.append(d)
            else:
                b = best.get(d.eng)
                if b is None or d.order > b.order:
                    best[d.eng] = d
        out.extend(best.values())
        o.deps = out
        for d in out:
            d.needs_sig = True
        self.n_ops += 1
        o.order = self.n_ops
        for k in writes:
            self.last_w[k] = o
            self.readers[k] = ({}, [])
        for k in reads:
            r = self.readers.get(k)
            if r is None:
                r = self.readers[k] = ({}, [])
            if dma:
                r[1].append(o)
            else:
                r[0][eng] = o
        self.ops.append(o)
        return o

    def _wait(self, eng, sem, val):
        kn = self.known[eng]
        key = id(sem)
        if kn.get(key, 0) >= val:
            return
        self.E[eng].wait_ge(sem, val)
        kn[key] = val
        self.n_inst += 1

    def phase_end(self):
        last = {}
        for o in self.ops:
            if not o.dma:
                last[o.eng] = o
        for o in last.values():
            o.needs_sig = True
        for o in self.ops:
            E = self.E[o.eng]
            for d in o.deps:
                if d.done:
                    continue
                self._wait(o.eng, d.sem, d.val)
            if o.dma:
                ring = self.rings[o.eng]
                i = self.dcount[o.eng]
                self.dcount[o.eng] = i + 1
                sem = ring[i % self.RING]
                tgt = 16 * (i // self.RING + 1)
                if tgt > 16:
                    self._wait(o.eng, sem, tgt - 16)
                ins = o.fn(E)
                ins.then_inc(sem, 16)
                o.sem, o.val = sem, tgt
            else:
                ins = o.fn(E)
                if o.needs_sig:
                    self.ccount[o.eng] += 1
                    ins.then_inc(self.csem[o.eng], 1)
                    o.sem, o.val = self.csem[o.eng], self.ccount[o.eng]
            self.n_inst += 1
        for eng in self.E:
            for e in self.COMPUTE:
                if self.ccount[e] > 0:
                    self._wait(eng, self.csem[e], self.ccount[e])
            for q, ring in self.rings.items():
                n = self.dcount[q]
                for j in range(self.RING):
                    uses = (n - j + self.RING - 1) // self.RING if n > j else 0
                    if uses > 0:
                        self._wait(eng, ring[j], 16 * uses)
        for o in self.ops:
            o.done = True
        self.last_w = {}
        self.readers = {}
        self.ops = []


class Prog:
    def __init__(self, SEQ, n_layers=4):
        assert SEQ % 512 == 0
        self.SEQ = SEQ
        self.T = SEQ + NMETA
        self.n_layers = n_layers
        self.blocks = [(0, NMETA)] + [(NMETA + 512 * j, 512) for j in range(SEQ // 512)]
        self.NT = SEQ // 128
        self.ktiles = [(0, NMETA)] + [(NMETA + 128 * j, 128) for j in range(self.NT)]
        self.uid = 0
        self.psi = 0

    def sb(self, ph, shape, dt, name="t"):
        self.uid += 1
        return ph.enter_context(self.nc.sbuf_tensor("%s_%d" % (name, self.uid), list(shape), dt))

    def dma(self, q, out, in_, reads=(), writes=()):
        self.S.op(q, lambda e: e.dma_start(out=out, in_=in_), reads, writes, dma=True)

    def mm(self, out, pairs, reads, writes, start=True, stop=True):
        pairs = list(pairs)

        def fn(e):
            n = len(pairs)
            ins = None
            for i, (l, r) in enumerate(pairs):
                ins = e.matmul(out, l, r, start=(start and i == 0), stop=(stop and i == n - 1))
            return ins
        self.S.op("pe", fn, reads, writes)

    def act(self, out, in_, func, reads, writes, scale=1.0, bias=None):
        if bias is None:
            self.S.op("act", lambda e: e.activation(out=out, in_=in_, func=func, scale=scale), reads, writes)
        else:
            self.S.op("act", lambda e: e.activation(out=out, in_=in_, func=func, scale=scale, bias=bias), reads, writes)

    def tt(self, eng, out, a, b, op, reads, writes):
        self.S.op(eng, lambda e: e.tensor_tensor(out, a, b, op), reads, writes)

    def ts(self, eng, out, a, s1, s2, op0, op1, reads, writes):
        if s2 is None:
            self.S.op(eng, lambda e: e.tensor_scalar(out, a, s1, None, op0), reads, writes)
        else:
            self.S.op(eng, lambda e: e.tensor_scalar(out, a, s1, s2, op0, op1), reads, writes)

    def stt(self, out, a, s, b, op0, op1, reads, writes):
        self.S.op("dve", lambda e: e.scalar_tensor_tensor(out, a, s, b, op0, op1), reads, writes)

    def recip(self, out, in_, reads, writes):
        self.S.op("dve", lambda e: e.reciprocal(out, in_), reads, writes)

    def copy(self, eng, out, in_, reads, writes):
        self.S.op(eng, lambda e: e.tensor_copy(out, in_), reads, writes)

    def memset(self, eng, ap, v, writes):
        self.S.op(eng, lambda e: e.memset(ap, v), (), writes)

    def hblk(self, bi):
        n = self.blocks[bi][1]
        return self.hB[bi, :, 0:8 * n].rearrange("p (c t) -> p c t", c=8)

    def zblk(self, bi):
        n = self.blocks[bi][1]
        return self.zB[bi, :, 0:8 * n].rearrange("p (c t) -> p c t", c=8)

    def yblk(self, bi):
        n = self.blocks[bi][1]
        return self.yB[bi, :, 0:16 * n].rearrange("p (c t) -> p c t", c=16)

    def nps(self):
        i = self.psi
        self.psi = (i + 1) % 8
        return i

    def rstd(self, ph_tiles, pairs, pair_reads, inv_n, n, tag):
        sd, rs = ph_tiles
        b = self.nps()
        self.mm(self.ps[b][:, 0:n], pairs, reads=list(pair_reads) + ["ones_r"], writes=[("ps", b)])
        self.act(sd[:, 0:n], self.ps[b][:, 0:n], AF.Ln, reads=[("ps", b), "eps"], writes=[("sd", tag)],
                 scale=inv_n, bias=self.eps[:, 0:1])
        self.act(rs[:, 0:n], sd[:, 0:n], AF.Exp, reads=[("sd", tag)], writes=[("rs", tag)], scale=-0.5)

    def build(self):
        SEQ, T = self.SEQ, self.T
        nc = bass.Bass("TRN2", target_bir_lowering=False)
        self.nc = nc

        def din(name, shape):
            return nc.dram_tensor(name, list(shape), F32, kind="ExternalInput").ap()

        def dscr(name, shape, dt):
            return nc.dram_tensor(name, list(shape), dt).ap()

        self.xT = din("xT", [D, SEQ])
        self.metaT = din("metaT", [D, NMETA])
        self.e_w_in = din("even_w_in", [2, D, 4288])
        self.w_uq = din("mla_w_uq", [2, 384, 8 * 192])
        self.w_ukv = din("mla_w_ukv", [2, 256, 8 * 256])
        self.e_w_out = din("even_w_out", [2, 2048, D])
        self.o_w_in = din("odd_w_in", [2, D, 4096])
        self.w_a = din("lru_w_a", [2, 2, 16, 128, 128])
        self.w_x = din("lru_w_x", [2, 2, 16, 128, 128])
        self.o_w_out = din("odd_w_out", [2, 2048, D])
        self.pp_e = din("pp_even", [2, 128, NPE])
        self.pp_o = din("pp_odd", [2, 128, NPO])
        self.cst = din("cst", [128, 4])
        self.outT = nc.dram_tensor("outT", [D, SEQ], F32, kind="ExternalOutput").ap()

        NB_ = len(self.blocks)
        self.hB = dscr("hB", [NB_, 128, 8 * 512], F32)
        self.zB = dscr("zB", [NB_, 128, 8 * 512], BF16)
        self.tab = dscr("tab", [4, 128, T], F32)
        self.qn = dscr("qn", [8, 128, T], BF16)
        self.qr = dscr("qr", [8, 64, T], BF16)
        self.kn = dscr("kn", [8, 128, T], BF16)
        self.kr = dscr("kr", [8, 64, T], BF16)
        self.vm = dscr("vm", [8, T, 128], BF16)
        self.qs = dscr("qs", [8, 128, T], BF16)
        self.ks = dscr("ks", [2, 128, T], BF16)
        self.vs = dscr("vs", [2, T, 128], BF16)
        self.sg = dscr("sg", [16, 128, T], BF16)
        self.yB = dscr("yB", [NB_, 128, 16 * 512], BF16)

        with contextlib.ExitStack() as gs:
            self.S = Sched(nc, gs)
            self.ps = [gs.enter_context(nc.psum_tensor("ps%d" % i, [128, 512], F32)) for i in range(8)]
            self.ones_f = self.sb(gs, [128, 128], F32, "ones_f")
            self.ones_r = self.sb(gs, [128, 128], F32R, "ones_r")
            self.ones_b = self.sb(gs, [128, 128], BF16, "ones_b")
            self.eps = self.sb(gs, [128, 1], F32, "eps")
            self.cst_t = self.sb(gs, [128, 4], F32, "cst")
            self.setup()
            for l in range(self.n_layers):
                j = l // 2
                if l == 0:
                    self.phase_norm(l)
                if l % 2 == 0:
                    self.phase_mla_proj(j)
                    self.phase_swa_proj(j)
                    self.phase_mla_attn(j)
                    self.phase_swa_attn(j)
                    if getattr(self, "debug_y", False):
                        cb_ = 8 if self.debug_y == 2 else 0
                        for c in range(8):
                            for bi, (t0, n) in enumerate(self.blocks):
                                if t0 >= NMETA:
                                    self.dma("pool", self.outT[c * 128:(c + 1) * 128, t0 - NMETA:t0 - NMETA + n],
                                             self.yblk(bi)[:, cb_ + c, :])
                        self.S.phase_end()
                        return nc
                    self.phase_out(l, self.e_w_out[j])
                else:
                    self.phase_lru(j)
                    self.phase_out(l, self.o_w_out[j])
            if self.n_layers < 4:
                self.phase_dump()
        return nc

    def setup(self):
        nc, T, SEQ = self.nc, self.T, self.SEQ
        with contextlib.ExitStack() as ph:
            self.memset("dve", self.ones_f[:], 1.0, ["ones_f"])
            self.memset("dve", self.ones_b[:], 1.0, ["ones_b"])
            self.memset("dve", self.eps[:], EPS, ["eps"])
            self.act(self.ones_r[:], self.ones_f[:], AF.Copy, ["ones_f"], ["ones_r"])
            self.dma("sp", self.cst_t[:], self.cst[:, :], writes=["cst"])
            for bi, (t0, n) in enumerate(self.blocks):
                if t0 < NMETA:
                    self.dma("sp", self.hblk(bi), self.metaT.rearrange("(c p) t -> p c t", p=128))
                else:
                    self.dma("sp", self.hblk(bi), self.xT[:, t0 - NMETA:t0 - NMETA + n].rearrange("(c p) t -> p c t", p=128))
            pos = self.sb(ph, [128, T], F32, "pos")
            ang = self.sb(ph, [128, T], F32, "ang")
            kf = self.sb(ph, [128, T], F32, "kf")
            ki = self.sb(ph, [128, T], I32, "ki")
            rr = self.sb(ph, [128, T], F32, "rr")
            sn = self.sb(ph, [128, T], F32, "sn")
            sgn = self.sb(ph, [128, 2], F32, "sgn")
            self.S.op("pool", lambda e: e.iota(pos[:], [[1, T]], base=0, channel_multiplier=0,
                                               allow_small_or_imprecise_dtypes=True), (), ["pos"])
            self.memset("dve", sgn[0:64, 0:1], 1.0, ["sgn0"])
            self.memset("dve", sgn[64:128, 0:1], -1.0, ["sgn1"])
            self.memset("dve", sgn[:, 1:2], 0.0, ["sgn2"])
            self.S.op("dve", lambda e: e.memset(sgn[0:32, 1:2], 1.0), ["sgn2"], ["sgn3"])
            self.S.op("dve", lambda e: e.memset(sgn[32:64, 1:2], -1.0), ["sgn2"], ["sgn4"])
            sgn_keys = ["sgn0", "sgn1", "sgn3", "sgn4"]
            for col in range(2):
                self.ts("dve", ang[:], pos[:], self.cst_t[:, col:col + 1], None, ALU.mult, None,
                        ["pos", "cst", "ang"], ["ang"])
                for which in range(2):
                    off = math.pi / 2 if which == 0 else 0.0
                    self.ts("dve", kf[:], ang[:], off, 1.0 / (2 * math.pi), ALU.add, ALU.mult, ["ang", "kf"], ["kf"])
                    self.copy("dve", ki[:], kf[:], ["kf", "ki"], ["ki"])
                    self.copy("dve", kf[:], ki[:], ["ki", "kf"], ["kf"])
                    self.stt(rr[:], kf[:], -2 * math.pi, ang[:], ALU.mult, ALU.add, ["kf", "ang", "rr"], ["rr"])
                    self.ts("dve", rr[:], rr[:], off, 3.1415925, ALU.add, ALU.min, ["rr"], ["rr"])
                    self.ts("dve", rr[:], rr[:], -3.1415925, None, ALU.max, None, ["rr"], ["rr"])
                    self.act(sn[:], rr[:], AF.Sin, ["rr", "sn"], ["sn"])
                    if which == 1:
                        self.ts("dve", sn[:], sn[:], sgn[:, col:col + 1], None, ALU.mult, None,
                                ["sn"] + sgn_keys, ["sn"])
                    self.dma("sp", self.tab[2 * col + which], sn[:], reads=["sn"])
            self.S.phase_end()

    def phase_norm(self, l):
        j = l // 2
        with contextlib.ExitStack() as ph:
            npp = NPE if l % 2 == 0 else NPO
            pp = self.sb(ph, [128, npp], F32, "pp")
            self.dma("sp", pp[:], (self.pp_e if l % 2 == 0 else self.pp_o)[j], writes=["pp"])
            hb = [self.sb(ph, [128, 8, 512], F32, "hb") for _ in range(2)]
            sq = [self.sb(ph, [128, 8, 512], F32R, "sq") for _ in range(2)]
            zb = [self.sb(ph, [128, 8, 512], BF16, "zb") for _ in range(2)]
            sd = self.sb(ph, [128, 512], F32, "sd")
            rs = self.sb(ph, [128, 512], F32, "rs")
            def load_n(bi):
                t0, n = self.blocks[bi]
                s = bi % 2
                self.dma("sp", hb[s][:, :, 0:n], self.hblk(bi), writes=[("hb", s)])

            load_n(0)
            for bi, (t0, n) in enumerate(self.blocks):
                s = bi % 2
                if bi + 1 < len(self.blocks):
                    load_n(bi + 1)
                self.act(sq[s][:, :, 0:n], hb[s][:, :, 0:n], AF.Square, [("hb", s)], [("sq", s)])
                self.rstd((sd, rs), [(self.ones_r[:], sq[s][:, c, 0:n]) for c in range(8)], [("sq", s)],
                          1.0 / D, n, "n")
                self.tt("dve", hb[s][:, :, 0:n], hb[s][:, :, 0:n],
                        rs[:, 0:n].unsqueeze(1).to_broadcast([128, 8, n]), ALU.mult,
                        [("hb", s), ("rs", "n")], [("hb", s)])
                self.tt("pool", zb[s][:, :, 0:n], hb[s][:, :, 0:n],
                        pp[:, 0:8].unsqueeze(2).to_broadcast([128, 8, n]), ALU.mult,
                        [("hb", s), "pp"], [("zb", s)])
                self.dma("sp", self.zblk(bi), zb[s][:, :, 0:n], reads=[("zb", s)])
            self.S.phase_end()

    def rope(self, out_bf, u, tcos, tsinx, rs, half, n, tmp1, tmp2, ukey, tkey, tskey, rskey, okey, tk1, tk2):
        full = 2 * half
        self.tt("dve", tmp1[0:full, 0:n], u[0:full, 0:n], tcos[0:full, 0:n], ALU.mult, [ukey, tkey], [tk1])
        self.tt("pool", tmp2[0:half, 0:n], u[half:full, 0:n], tsinx[half:full, 0:n], ALU.mult,
                [ukey, tskey], [(tk2, 0)])
        self.tt("pool", tmp2[half:full, 0:n], u[0:half, 0:n], tsinx[0:half, 0:n], ALU.mult,
                [ukey, tskey], [(tk2, 1)])
        self.tt("pool", tmp1[0:full, 0:n], tmp1[0:full, 0:n], tmp2[0:full, 0:n], ALU.add,
                [tk1, (tk2, 0), (tk2, 1)], [tk1])
        self.tt("dve", out_bf[0:full, 0:n], tmp1[0:full, 0:n], rs[0:full, 0:n], ALU.mult, [tk1, rskey], [okey])

    def phase_mla_proj(self, j):
        T = self.T
        with contextlib.ExitStack() as ph:
            pp = self.sb(ph, [128, NPE], F32, "pp")
            self.dma("sp", pp[:], self.pp_e[j], writes=["pp"])
            wA = self.sb(ph, [128, 8, 704], BF16, "wA")
            wuq = self.sb(ph, [128, 3, 1536], BF16, "wuq")
            wukv = self.sb(ph, [128, 2, 2048], BF16, "wukv")
            for k in range(8):
                self.dma("pool", wA[:, k, :], self.e_w_in[j, k * 128:(k + 1) * 128, 0:704], writes=[("wA", k)])
            for k in range(3):
                self.dma("pool", wuq[:, k, :], self.w_uq[j, k * 128:(k + 1) * 128, :], writes=[("wuq", k)])
            for k in range(2):
                self.dma("pool", wukv[:, k, :], self.w_ukv[j, k * 128:(k + 1) * 128, :], writes=[("wukv", k)])
            wA_k = [("wA", k) for k in range(8)]
            wuq_k = [("wuq", k) for k in range(3)]
            wukv_k = [("wukv", k) for k in range(2)]
            zb = [self.sb(ph, [128, 8, 512], BF16, "zb") for _ in range(2)]
            tcm = [self.sb(ph, [64, 512], F32, "tcm") for _ in range(2)]
            tsm = [self.sb(ph, [64, 512], F32, "tsm") for _ in range(2)]
            cq = self.sb(ph, [128, 3, 512], F32, "cq")
            sqc = self.sb(ph, [128, 3, 512], F32R, "sqc")
            cqn = self.sb(ph, [128, 3, 512], BF16, "cqn")
            ckv = self.sb(ph, [128, 2, 512], F32, "ckv")
            sqk = self.sb(ph, [128, 2, 512], F32R, "sqk")
            ckvn = self.sb(ph, [128, 2, 512], BF16, "ckvn")
            sd = self.sb(ph, [128, 512], F32, "sd")
            rs = self.sb(ph, [128, 512], F32, "rs")
            kpu = self.sb(ph, [64, 512], F32, "kpu")
            kpsq = self.sb(ph, [64, 512], F32R, "kpsq")
            kpr = self.sb(ph, [64, 512], F32, "kpr")
            t1 = self.sb(ph, [128, 512], F32, "t1")
            t2 = self.sb(ph, [128, 512], F32, "t2")
            NS = 3
            sqh = [self.sb(ph, [128, 512], F32R, "sqh") for _ in range(NS)]
            sqh2 = [self.sb(ph, [64, 512], F32R, "sqh2") for _ in range(NS)]
            uh = [self.sb(ph, [128, 512], F32, "uh") for _ in range(NS)]
            uh2 = [self.sb(ph, [64, 512], F32, "uh2") for _ in range(NS)]
            sdh = [self.sb(ph, [128, 512], F32, "sdh") for _ in range(NS)]
            rsh = [self.sb(ph, [128, 512], F32, "rsh") for _ in range(NS)]
            ob = [self.sb(ph, [128, 512], BF16, "ob") for _ in range(4)]
            ob2 = [self.sb(ph, [64, 512], BF16, "ob2") for _ in range(4)]
            vb = [self.sb(ph, [128, 512], BF16, "vb") for _ in range(2)]
            obi = 0
            ob2i = 0
            vbi = 0
            hs = 0
            def load_m(bi):
                t0, n = self.blocks[bi]
                s = bi % 2
                self.dma("sp", zb[s][:, :, 0:n], self.zblk(bi), writes=[("zb", s)])
                self.dma("sp", tcm[s][:, 0:n], self.tab[2, 0:64, t0:t0 + n], writes=[("tcm", s)])
                self.dma("sp", tsm[s][:, 0:n], self.tab[3, 0:64, t0:t0 + n], writes=[("tsm", s)])

            load_m(0)
            for bi, (t0, n) in enumerate(self.blocks):
                s = bi % 2
                if bi + 1 < len(self.blocks):
                    load_m(bi + 1)
                for mc in range(3):
                    b = self.nps()
                    self.mm(self.ps[b][:, 0:n], [(wA[:, k, mc * 128:(mc + 1) * 128], zb[s][:, k, 0:n]) for k in range(8)],
                            wA_k + [("zb", s)], [("ps", b)])
                    self.act(cq[:, mc, 0:n], self.ps[b][:, 0:n], AF.Copy, [("ps", b)], [("cq", mc)])
                    self.act(sqc[:, mc, 0:n], self.ps[b][:, 0:n], AF.Square, [("ps", b)], [("sqc", mc)])
                self.rstd((sd, rs), [(self.ones_r[:], sqc[:, mc, 0:n]) for mc in range(3)],
                          [("sqc", mc) for mc in range(3)], 1.0 / 384, n, "lat")
                self.tt("dve", cq[:, :, 0:n], cq[:, :, 0:n], rs[:, 0:n].unsqueeze(1).to_broadcast([128, 3, n]),
                        ALU.mult, [("cq", mc) for mc in range(3)] + [("rs", "lat")], [("cq", mc) for mc in range(3)])
                self.tt("pool", cqn[:, :, 0:n], cq[:, :, 0:n], pp[:, 8:11].unsqueeze(2).to_broadcast([128, 3, n]),
                        ALU.mult, [("cq", mc) for mc in range(3)] + ["pp"], ["cqn"])
                for mc in range(2):
                    b = self.nps()
                    self.mm(self.ps[b][:, 0:n], [(wA[:, k, 384 + mc * 128:384 + (mc + 1) * 128], zb[s][:, k, 0:n]) for k in range(8)],
                            wA_k + [("zb", s)], [("ps", b)])
                    self.act(ckv[:, mc, 0:n], self.ps[b][:, 0:n], AF.Copy, [("ps", b)], [("ckv", mc)])
                    self.act(sqk[:, mc, 0:n], self.ps[b][:, 0:n], AF.Square, [("ps", b)], [("sqk", mc)])
                self.rstd((sd, rs), [(self.ones_r[:], sqk[:, mc, 0:n]) for mc in range(2)],
                          [("sqk", mc) for mc in range(2)], 1.0 / 256, n, "lat")
                self.tt("dve", ckv[:, :, 0:n], ckv[:, :, 0:n], rs[:, 0:n].unsqueeze(1).to_broadcast([128, 2, n]),
                        ALU.mult, [("ckv", mc) for mc in range(2)] + [("rs", "lat")], [("ckv", mc) for mc in range(2)])
                self.tt("pool", ckvn[:, :, 0:n], ckv[:, :, 0:n], pp[:, 11:13].unsqueeze(2).to_broadcast([128, 2, n]),
                        ALU.mult, [("ckv", mc) for mc in range(2)] + ["pp"], ["ckvn"])
                b = self.nps()
                self.mm(self.ps[b][0:64, 0:n], [(wA[:, k, 640:704], zb[s][:, k, 0:n]) for k in range(8)],
                        wA_k + [("zb", s)], [("ps", b)])
                self.act(kpu[:, 0:n], self.ps[b][0:64, 0:n], AF.Copy, [("ps", b), "pp"], ["kpu"], scale=pp[0:64, 16:17])
                self.act(kpsq[:, 0:n], self.ps[b][0:64, 0:n], AF.Square, [("ps", b)], ["kpsq"])
                self.tt("dve", t1[0:64, 0:n], kpu[0:64, 0:n], tcm[s][0:64, 0:n], ALU.mult, ["kpu", ("tcm", s)], ["t1"])
                self.tt("pool", t2[0:32, 0:n], kpu[32:64, 0:n], tsm[s][32:64, 0:n], ALU.mult, ["kpu", ("tsm", s)], [("t2", 0)])
                self.tt("pool", t2[32:64, 0:n], kpu[0:32, 0:n], tsm[s][0:32, 0:n], ALU.mult, ["kpu", ("tsm", s)], [("t2", 1)])
                self.tt("dve", kpr[0:64, 0:n], t1[0:64, 0:n], t2[0:64, 0:n], ALU.add, ["t1", ("t2", 0), ("t2", 1)], ["kpr"])
                pend = [None]

                def flush_pend():
                    if pend[0] is not None:
                        f_ = pend[0]
                        pend[0] = None
                        f_()

                for h in range(8):
                    hs = (hs + 1) % NS
                    bn_ = self.nps()
                    self.mm(self.ps[bn_][:, 0:n], [(wuq[:, k, h * 192:h * 192 + 128], cqn[:, k, 0:n]) for k in range(3)],
                            wuq_k + ["cqn"], [("ps", bn_)])
                    br_ = self.nps()
                    self.mm(self.ps[br_][0:64, 0:n], [(wuq[:, k, h * 192 + 128:h * 192 + 192], cqn[:, k, 0:n]) for k in range(3)],
                            wuq_k + ["cqn"], [("ps", br_)])
                    self.act(sqh[hs][:, 0:n], self.ps[bn_][:, 0:n], AF.Square, [("ps", bn_)], [("sqh", hs)])
                    self.act(sqh2[hs][:, 0:n], self.ps[br_][0:64, 0:n], AF.Square, [("ps", br_)], [("sqh2", hs)])
                    self.act(uh[hs][:, 0:n], self.ps[bn_][:, 0:n], AF.Copy, [("ps", bn_), "pp"], [("uh", hs)], scale=pp[:, 13:14])
                    self.act(uh2[hs][:, 0:n], self.ps[br_][0:64, 0:n], AF.Copy, [("ps", br_), "pp"], [("uh2", hs)], scale=pp[0:64, 14:15])
                    flush_pend()

                    def fin_q(hs=hs, h=h, obi=obi, ob2i=ob2i):
                        self.rstd((sdh[hs], rsh[hs]), [(self.ones_r[:], sqh[hs][:, 0:n]), (self.ones_r[0:64, :], sqh2[hs][0:64, 0:n])],
                                  [("sqh", hs), ("sqh2", hs)], 1.0 / 192, n, ("h", hs))
                        o = ob[obi]
                        self.tt("dve", o[:, 0:n], uh[hs][:, 0:n], rsh[hs][:, 0:n], ALU.mult,
                                [("uh", hs), ("rs", ("h", hs))], [("ob", obi)])
                        self.dma("sp", self.qn[h, :, t0:t0 + n], o[:, 0:n], reads=[("ob", obi)])
                        o2 = ob2[ob2i]
                        self.rope(o2, uh2[hs], tcm[s], tsm[s], rsh[hs], 32, n, t1, t2, ("uh2", hs), ("tcm", s), ("tsm", s),
                                  ("rs", ("h", hs)), ("ob2", ob2i), "t1", "t2")
                        self.dma("sp", self.qr[h, :, t0:t0 + n], o2[0:64, 0:n], reads=[("ob2", ob2i)])
                    pend[0] = fin_q
                    obi = (obi + 1) % 4
                    ob2i = (ob2i + 1) % 4
                    hs = (hs + 1) % NS
                    bk_ = self.nps()
                    self.mm(self.ps[bk_][:, 0:n], [(wukv[:, k, h * 256:h * 256 + 128], ckvn[:, k, 0:n]) for k in range(2)],
                            wukv_k + ["ckvn"], [("ps", bk_)])
                    self.act(sqh[hs][:, 0:n], self.ps[bk_][:, 0:n], AF.Square, [("ps", bk_)], [("sqh", hs)])
                    self.act(uh[hs][:, 0:n], self.ps[bk_][:, 0:n], AF.Copy, [("ps", bk_), "pp"], [("uh", hs)], scale=pp[:, 15:16])
                    flush_pend()

                    def fin_k(hs=hs, h=h, obi=obi, ob2i=ob2i):
                        self.rstd((sdh[hs], rsh[hs]), [(self.ones_r[:], sqh[hs][:, 0:n]), (self.ones_r[0:64, :], kpsq[0:64, 0:n])],
                                  [("sqh", hs), "kpsq"], 1.0 / 192, n, ("h", hs))
                        o = ob[obi]
                        self.tt("dve", o[:, 0:n], uh[hs][:, 0:n], rsh[hs][:, 0:n], ALU.mult,
                                [("uh", hs), ("rs", ("h", hs))], [("ob", obi)])
                        self.dma("sp", self.kn[h, :, t0:t0 + n], o[:, 0:n], reads=[("ob", obi)])
                        o2 = ob2[ob2i]
                        self.tt("pool", o2[0:64, 0:n], kpr[0:64, 0:n], rsh[hs][0:64, 0:n], ALU.mult,
                                ["kpr", ("rs", ("h", hs))], [("ob2", ob2i)])
                        self.dma("sp", self.kr[h, :, t0:t0 + n], o2[0:64, 0:n], reads=[("ob2", ob2i)])
                    pend[0] = fin_k
                    obi = (obi + 1) % 4
                    ob2i = (ob2i + 1) % 4
                flush_pend()
                ntt = max(1, n // 128)
                for tt_ in range(ntt):
                    m = min(128, n)
                    c0 = tt_ * 128
                    for hg in range(2):
                        b = self.nps()
                        rhs = [wukv[:, k, :].rearrange("p (h e) -> p h e", e=256)[:, hg * 4:(hg + 1) * 4, 128:256]
                               for k in range(2)]
                        self.mm(self.ps[b][0:m, :].rearrange("p (h e) -> p h e", e=128),
                                [(ckvn[:, k, c0:c0 + m], rhs[k]) for k in range(2)],
                                wukv_k + ["ckvn"], [("ps", b)])
                        v = vb[vbi]
                        self.act(v[0:m, :], self.ps[b][0:m, :], AF.Copy, [("ps", b)], [("vb", vbi)])
                        self.dma("sp", self.vm[hg * 4:(hg + 1) * 4, t0 + c0:t0 + c0 + m, :].rearrange("h t e -> t h e"),
                                 v[0:m, :].rearrange("p (h e) -> p h e", e=128), reads=[("vb", vbi)])
                        vbi = (vbi + 1) % 2
            self.S.phase_end()

    def phase_swa_proj(self, j):
        T = self.T
        with contextlib.ExitStack() as ph:
            pp = self.sb(ph, [128, NPE], F32, "pp")
            self.dma("sp", pp[:], self.pp_e[j], writes=["pp"])
            wB = self.sb(ph, [128, 8, 3584], BF16, "wB")
            for k in range(8):
                for half in range(2):
                    self.dma("pool", wB[:, k, half * 1792:(half + 1) * 1792],
                             self.e_w_in[j, k * 128:(k + 1) * 128, 704 + half * 1792:704 + (half + 1) * 1792],
                             writes=[("wB", k, half)])
            wB_k = [("wB", k, hf) for k in range(8) for hf in range(2)]
            zb = [self.sb(ph, [128, 8, 512], BF16, "zb") for _ in range(2)]
            tcs = [self.sb(ph, [128, 512], F32, "tcs") for _ in range(2)]
            tss = [self.sb(ph, [128, 512], F32, "tss") for _ in range(2)]
            NS = 3
            sqh = [self.sb(ph, [128, 512], F32R, "sqh") for _ in range(NS)]
            uh = [self.sb(ph, [128, 512], F32, "uh") for _ in range(NS)]
            sdh = [self.sb(ph, [128, 512], F32, "sdh") for _ in range(NS)]
            rsh = [self.sb(ph, [128, 512], F32, "rsh") for _ in range(NS)]
            t1 = [self.sb(ph, [128, 512], F32, "t1") for _ in range(NS)]
            t2 = [self.sb(ph, [128, 512], F32, "t2") for _ in range(NS)]
            ob = [self.sb(ph, [128, 512], BF16, "ob") for _ in range(4)]
            vb = [self.sb(ph, [128, 256], BF16, "vb") for _ in range(2)]
            obi = 0
            vbi = 0
            hs = 0
            def load_s(bi):
                t0, n = self.blocks[bi]
                s = bi % 2
                self.dma("sp", zb[s][:, :, 0:n], self.zblk(bi), writes=[("zb", s)])
                self.dma("sp", tcs[s][:, 0:n], self.tab[0, :, t0:t0 + n], writes=[("tcs", s)])
                self.dma("sp", tss[s][:, 0:n], self.tab[1, :, t0:t0 + n], writes=[("tss", s)])

            load_s(0)
            for bi, (t0, n) in enumerate(self.blocks):
                s = bi % 2
                if bi + 1 < len(self.blocks):
                    load_s(bi + 1)
                pend = [None]

                def flush_pend():
                    if pend[0] is not None:
                        f_ = pend[0]
                        pend[0] = None
                        f_()

                for hh in range(10):
                    hs = (hs + 1) % NS
                    c0 = hh * 128 if hh < 8 else 1024 + (hh - 8) * 128
                    gcol = 17 if hh < 8 else 18
                    b = self.nps()
                    self.mm(self.ps[b][:, 0:n], [(wB[:, k, c0:c0 + 128], zb[s][:, k, 0:n]) for k in range(8)],
                            wB_k + [("zb", s)], [("ps", b)])
                    self.act(sqh[hs][:, 0:n], self.ps[b][:, 0:n], AF.Square, [("ps", b)], [("sqh", hs)])
                    self.act(uh[hs][:, 0:n], self.ps[b][:, 0:n], AF.Copy, [("ps", b), "pp"], [("uh", hs)],
                             scale=pp[:, gcol:gcol + 1])
                    flush_pend()

                    def fin(hs=hs, hh=hh, obi=obi):
                        self.rstd((sdh[hs], rsh[hs]), [(self.ones_r[:], sqh[hs][:, 0:n])], [("sqh", hs)], 1.0 / 128, n, ("h", hs))
                        o = ob[obi]
                        self.rope(o, uh[hs], tcs[s], tss[s], rsh[hs], 64, n, t1[hs], t2[hs], ("uh", hs), ("tcs", s), ("tss", s),
                                  ("rs", ("h", hs)), ("ob", obi), ("t1", hs), ("t2", hs))
                        dst = self.qs[hh, :, t0:t0 + n] if hh < 8 else self.ks[hh - 8, :, t0:t0 + n]
                        self.dma("sp", dst, o[:, 0:n], reads=[("ob", obi)])
                    pend[0] = fin
                    obi = (obi + 1) % 4
                flush_pend()
                ntt = max(1, n // 128)
                for tt_ in range(ntt):
                    m = min(128, n)
                    cc = tt_ * 128
                    b = self.nps()
                    self.mm(self.ps[b][0:m, 0:256], [(zb[s][:, k, cc:cc + m], wB[:, k, 1280:1536]) for k in range(8)],
                            wB_k + [("zb", s)], [("ps", b)])
                    v = vb[vbi]
                    self.act(v[0:m, :], self.ps[b][0:m, 0:256], AF.Copy, [("ps", b)], [("vb", vbi)])
                    self.dma("sp", self.vs[:, t0 + cc:t0 + cc + m, :].rearrange("g t e -> t g e"),
                             v[0:m, :].rearrange("p (g e) -> p g e", e=128), reads=[("vb", vbi)])
                    vbi = (vbi + 1) % 2
                for c in range(16):
                    b = self.nps()
                    c0 = 1536 + c * 128
                    self.mm(self.ps[b][:, 0:n], [(wB[:, k, c0:c0 + 128], zb[s][:, k, 0:n]) for k in range(8)],
                            wB_k + [("zb", s)], [("ps", b)])
                    o = ob[obi]
                    self.act(o[:, 0:n], self.ps[b][:, 0:n], AF.Silu, [("ps", b)], [("ob", obi)])
                    self.dma("sp", self.sg[c, :, t0:t0 + n], o[:, 0:n], reads=[("ob", obi)])
                    obi = (obi + 1) % 4
            self.S.phase_end()

    def phase_mla_attn(self, j):
        T, NT = self.T, self.NT
        scale = 192.0 ** -0.5
        with contextlib.ExitStack() as ph:
            knb = [self.sb(ph, [128, T], BF16, "knb") for _ in range(2)]
            krb = [self.sb(ph, [128, T], BF16, "krb") for _ in range(2)]
            vtb = [self.sb(ph, [128, NT + 1, 128], BF16, "vtb") for _ in range(2)]
            qnb = [self.sb(ph, [128, 512], BF16, "qnb") for _ in range(2)]
            qrb = [self.sb(ph, [128, 512], BF16, "qrb") for _ in range(2)]
            sgb = [self.sb(ph, [128, 512], BF16, "sgb") for _ in range(2)]
            NP = 4
            NPT = 8
            pt = [self.sb(ph, [128, 512], BF16, "pt") for _ in range(NPT)]
            pacc = [self.sb(ph, [128, 512], BF16, "pacc") for _ in range(2)]
            slot0 = [0]
            pend_ones = []
            rl = self.sb(ph, [128, 512], F32, "rl")
            of = self.sb(ph, [128, 512], F32, "of")
            yb = [self.sb(ph, [128, 512], BF16, "yb") for _ in range(2)]
            qbs = [(h, bi) for h in range(8) for bi in range(len(self.blocks))]
            units = [(qi, kt) for qi in range(len(qbs)) for kt in range(len(self.ktiles))]
            LOOK = 3

            def load_head(h):
                s = h % 2
                self.dma("sp", knb[s][:, :], self.kn[h], writes=[("knb", s)])
                self.dma("sp", krb[s][0:64, :], self.kr[h], writes=[("krb", s, 0)])
                self.dma("sp", krb[s][64:128, :], self.kr[h], writes=[("krb", s, 1)])
                self.dma("sp", vtb[s][0:NMETA, 0, :], self.vm[h, 0:NMETA, :], writes=[("vtb", s, 0)])
                self.dma("sp", vtb[s][:, 1:NT + 1, :], self.vm[h, NMETA:T, :].rearrange("(k p) e -> p k e", p=128),
                         writes=[("vtb", s, 1)])

            def load_q(qi):
                h, bi = qbs[qi]
                t0, n = self.blocks[bi]
                s = qi % 2
                self.dma("sp", qnb[s][:, 0:n], self.qn[h, :, t0:t0 + n], writes=[("qnb", s)])
                self.dma("sp", qrb[s][0:64, 0:n], self.qr[h, :, t0:t0 + n], writes=[("qrb", s, 0)])
                self.dma("sp", qrb[s][64:128, 0:n], self.qr[h, :, t0:t0 + n], writes=[("qrb", s, 1)])
                self.dma("sp", sgb[s][:, 0:n], self.sg[h, :, t0:t0 + n], writes=[("sgb", s)])

            def emit_pair(u0):
                us = [u for u in (u0, u0 + 1) if u < len(units)]
                info = []
                for u in us:
                    qi, kt = units[u]
                    h, bi = qbs[qi]
                    t0, n = self.blocks[bi]
                    k0, kn_ = self.ktiles[kt]
                    info.append((u, h % 2, qi % 2, n, k0, kn_, u % NP, (u % 2) * 64))

                def fn(e):
                    ins = None
                    for (u, hsl, qs_, n, k0, kn_, b, r0) in info:
                        ins = e.matmul(self.ps[b][0:kn_, 0:n], knb[hsl][:, k0:k0 + kn_], qnb[qs_][:, 0:n], start=True, stop=False)
                    for (u, hsl, qs_, n, k0, kn_, b, r0) in info:
                        ins = e.matmul(self.ps[b][0:kn_, 0:n], krb[hsl][r0:r0 + 64, k0:k0 + kn_], qrb[qs_][r0:r0 + 64, 0:n],
                                       start=False, stop=True)
                    return ins
                reads = []
                writes = []
                for (u, hsl, qs_, n, k0, kn_, b, r0) in info:
                    reads += [("knb", hsl), ("krb", hsl, 0), ("krb", hsl, 1), ("qnb", qs_), ("qrb", qs_, 0), ("qrb", qs_, 1)]
                    writes.append(("ps", b))
                self.S.op("pe", fn, reads, writes)

            load_head(0)
            load_q(0)
            emit_pair(0)
            for u, (qi, kt) in enumerate(units):
                h, bi = qbs[qi]
                t0, n = self.blocks[bi]
                k0, kn_ = self.ktiles[kt]
                hsl, qs_ = h % 2, qi % 2
                if kt == 0:
                    if qi + 1 < len(qbs):
                        if qbs[qi + 1][0] != h:
                            load_head(qbs[qi + 1][0])
                        load_q(qi + 1)
                if u % 2 == 0 and u + 2 < len(units):
                    emit_pair(u + 2)
                b = u % NP
                pb = u % NPT
                while pend_ones and pend_ones[0][0] <= u:
                    pend_ones.pop(0)[1]()
                bo = 4 + qi % 2
                bl = 6 + qi % 2
                self.act(pt[pb][0:kn_, 0:n], self.ps[b][0:kn_, 0:n], AF.Exp, [("ps", b)], [("pt", pb)], scale=scale)
                last = (kt == len(self.ktiles) - 1)
                vkey = ("vtb", hsl, 0 if kt == 0 else 1)
                self.mm(self.ps[bo][:, 0:n], [(vtb[hsl][0:kn_, kt, :], pt[pb][0:kn_, 0:n])],
                        [vkey, ("pt", pb)], [("ps", bo)], start=(kt == 0), stop=last)
                if kt == 0:
                    self.mm(self.ps[bl][:, 0:n], [(self.ones_b[0:kn_, :], pt[pb][0:kn_, 0:n])],
                            ["ones_b", ("pt", pb)], [("ps", bl)], start=True, stop=last)
                else:
                    gi = (kt - 1) % 4
                    qd = ((kt - 1) // 4 + qi) % 2
                    if gi == 0:
                        slot0[0] = pb
                    elif gi == 1:
                        self.tt("dve", pacc[qd][:, 0:n], pt[slot0[0]][:, 0:n], pt[pb][:, 0:n], ALU.add,
                                [("pt", slot0[0]), ("pt", pb)], [("pacc", qd)])
                    else:
                        self.tt("dve", pacc[qd][:, 0:n], pacc[qd][:, 0:n], pt[pb][:, 0:n], ALU.add,
                                [("pacc", qd), ("pt", pb)], [("pacc", qd)])
                    if gi == 3:
                        def ones_mm(bl=bl, qd=qd, n=n, last=last):
                            self.mm(self.ps[bl][:, 0:n], [(self.ones_b[:, :], pacc[qd][:, 0:n])],
                                    ["ones_b", ("pacc", qd)], [("ps", bl)], start=False, stop=last)
                        if last:
                            ones_mm()
                        else:
                            pend_ones.append((u + 2, ones_mm))
                if last:
                    self.recip(rl[:, 0:n], self.ps[bl][:, 0:n], [("ps", bl)], ["rl"])
                    self.tt("dve", of[:, 0:n], self.ps[bo][:, 0:n], rl[:, 0:n], ALU.mult, [("ps", bo), "rl"], ["of"])
                    ys = qi % 2
                    self.tt("pool", yb[ys][:, 0:n], of[:, 0:n], sgb[qs_][:, 0:n], ALU.mult, ["of", ("sgb", qs_)], [("yb", ys)])
                    self.dma("sp", self.yblk(bi)[:, h, :], yb[ys][:, 0:n], reads=[("yb", ys)])
            self.S.phase_end()

    def phase_swa_attn(self, j):
        T, NT = self.T, self.NT
        scale = 128.0 ** -0.5
        with contextlib.ExitStack() as ph:
            pp = self.sb(ph, [128, NPE], F32, "pp")
            self.dma("sp", pp[:], self.pp_e[j], writes=["pp"])
            esk = self.sb(ph, [128, 8], F32, "esk")
            self.act(esk[:], pp[:, 19:27], AF.Exp, ["pp"], ["esk"])
            ones_m = self.sb(ph, [128, 128], BF16, "ones_m")
            mprev = self.sb(ph, [128, 128], BF16, "mprev")
            mnext = self.sb(ph, [128, 128], BF16, "mnext")
            mmeta = self.sb(ph, [128, 16], BF16, "mmeta")
            self.memset("pool", ones_m[:], 1.0, ["ones_m"])
            self.S.op("pool", lambda e: e.affine_select(out=mprev[:], in_=ones_m[:], pattern=[[-1, 128]], base=0,
                                                        channel_multiplier=1, compare_op=ALU.is_ge, fill=0.0),
                      ["ones_m"], ["mprev"])
            self.S.op("pool", lambda e: e.affine_select(out=mnext[:], in_=ones_m[:], pattern=[[1, 128]], base=0,
                                                        channel_multiplier=-1, compare_op=ALU.is_ge, fill=0.0),
                      ["ones_m"], ["mnext"])
            self.S.op("pool", lambda e: e.affine_select(out=mmeta[:], in_=ones_m[:, 0:16], pattern=[[1, 16]], base=112,
                                                        channel_multiplier=-1, compare_op=ALU.is_ge, fill=0.0),
                      ["ones_m"], ["mmeta"])
            ksb = [self.sb(ph, [128, T], BF16, "ksb") for _ in range(2)]
            vtb = [self.sb(ph, [128, NT + 1, 128], BF16, "vtb") for _ in range(2)]
            q4 = [self.sb(ph, [128, 4, 512], BF16, "q4") for _ in range(3)]
            sg4 = [self.sb(ph, [128, 4, 512], BF16, "sg4") for _ in range(3)]
            NP = 4
            pt = [self.sb(ph, [128, 4, 128], BF16, "pt") for _ in range(NP)]
            lf = self.sb(ph, [128, 4, 128], F32, "lf")
            rl = self.sb(ph, [128, 4, 128], F32, "rl")
            of = self.sb(ph, [128, 4, 128], F32, "of")
            yb = [self.sb(ph, [128, 4, 128], BF16, "yb") for _ in range(2)]
            pi = 0
            qt_count = 0
            for g in range(2):
                self.dma("sp", ksb[g][:, :], self.ks[g], writes=[("ksb", g)])
                self.dma("sp", vtb[g][0:NMETA, 0, :], self.vs[g, 0:NMETA, :], writes=[("vtb", g, 0)])
                self.dma("sp", vtb[g][:, 1:NT + 1, :], self.vs[g, NMETA:T, :].rearrange("(k p) e -> p k e", p=128),
                         writes=[("vtb", g, 1)])
            blks = [(g, bi) for g in range(2) for bi in range(len(self.blocks))]
            qts = []
            units = []
            for bk, (g, bi) in enumerate(blks):
                t0, n = self.blocks[bi]
                nq = 1 if n == NMETA else n // 128
                for qt in range(nq):
                    w = NMETA if n == NMETA else 128
                    if n == NMETA:
                        tiles = [(0, None, None), (1, mmeta, "mmeta")]
                    else:
                        kti = (t0 - NMETA) // 128 + qt + 1
                        tiles = [(0, None, None)]
                        if kti - 1 >= 1:
                            tiles.append((kti - 1, mprev, "mprev"))
                        tiles.append((kti, None, None))
                        if kti + 1 <= NT:
                            tiles.append((kti + 1, mnext, "mnext"))
                    qts.append(dict(bk=bk, g=g, t0=t0, n=n, s=bk % 3, c0=qt * 128, w=w, tiles=tiles, qt=qt))
                    for ti in range(len(tiles)):
                        units.append((len(qts) - 1, ti))
            LOOK = 3

            def load_blk(bk):
                g, bi = blks[bk]
                t0, n = self.blocks[bi]
                s_ = bk % 3
                self.dma("sp", q4[s_][:, :, 0:n], self.qs[g * 4:(g + 1) * 4, :, t0:t0 + n].rearrange("r p t -> p r t"),
                         writes=[("q4", s_)])
                self.dma("sp", sg4[s_][:, :, 0:n], self.sg[8 + g * 4:8 + (g + 1) * 4, :, t0:t0 + n].rearrange("r p t -> p r t"),
                         writes=[("sg4", s_)])

            loaded = [0]

            def ensure_loaded(bk):
                while loaded[0] <= bk and loaded[0] < len(blks):
                    load_blk(loaded[0])
                    loaded[0] += 1

            def emit_s(u):
                qi, ti = units[u]
                q = qts[qi]
                ensure_loaded(q["bk"])
                kt = q["tiles"][ti][0]
                k0, kn_ = self.ktiles[kt]
                b = u % NP
                w = q["w"]
                self.mm(self.ps[b][0:kn_, 0:4 * w].rearrange("p (r i) -> p r i", r=4),
                        [(ksb[q["g"]][:, k0:k0 + kn_], q4[q["s"]][:, :, q["c0"]:q["c0"] + w])],
                        [("ksb", q["g"]), ("q4", q["s"])], [("ps", b)])

            for u in range(min(LOOK, len(units))):
                emit_s(u)
            for u, (qi, ti) in enumerate(units):
                q = qts[qi]
                g, w, c0, s_, t0 = q["g"], q["w"], q["c0"], q["s"], q["t0"]
                kt, mask, mk = q["tiles"][ti]
                k0, kn_ = self.ktiles[kt]
                b = u % NP
                bo = 4 + qi % 2
                bl = 6 + qi % 2
                if q["qt"] == 0 and ti == 0:
                    ensure_loaded(q["bk"] + 1)
                pv = pt[b][0:kn_, :, 0:w]
                self.act(pv, self.ps[b][0:kn_, 0:4 * w].rearrange("p (r i) -> p r i", r=4), AF.Exp,
                         [("ps", b)], [("pt", b)], scale=scale)
                if mask is not None:
                    self.tt("pool", pv, pv, mask[0:kn_, 0:w].unsqueeze(1).to_broadcast([kn_, 4, w]), ALU.mult,
                            [("pt", b), mk], [("pt", b)])
                vkey = ("vtb", g, 0 if kt == 0 else 1)
                first, last = (ti == 0), (ti == len(q["tiles"]) - 1)
                self.mm(self.ps[bo][:, 0:4 * w].rearrange("p (r i) -> p r i", r=4),
                        [(vtb[g][0:kn_, kt, :], pv)], [vkey, ("pt", b)], [("ps", bo)], start=first, stop=last)
                self.mm(self.ps[bl][:, 0:4 * w].rearrange("p (r i) -> p r i", r=4),
                        [(self.ones_b[0:kn_, :], pv)], ["ones_b", ("pt", b)], [("ps", bl)], start=first, stop=last)
                if u + LOOK < len(units):
                    emit_s(u + LOOK)
                if last:
                    lv = lf[:, :, 0:w]
                    self.tt("dve", lv, self.ps[bl][:, 0:4 * w].rearrange("p (r i) -> p r i", r=4),
                            esk[:, g * 4:(g + 1) * 4].unsqueeze(2).to_broadcast([128, 4, w]), ALU.add,
                            [("ps", bl), "esk"], ["lf"])
                    self.recip(rl[:, :, 0:w], lv, ["lf"], ["rl"])
                    self.tt("dve", of[:, :, 0:w], self.ps[bo][:, 0:4 * w].rearrange("p (r i) -> p r i", r=4),
                            rl[:, :, 0:w], ALU.mult, [("ps", bo), "rl"], ["of"])
                    ys = qi % 2
                    self.tt("dve", yb[ys][:, :, 0:w], of[:, :, 0:w], sg4[s_][:, :, c0:c0 + w], ALU.mult,
                            ["of", ("sg4", s_)], [("yb", ys)])
                    self.dma("sp", self.yblk(blks[q["bk"]][1])[:, 8 + g * 4:8 + (g + 1) * 4, c0:c0 + w],
                             yb[ys][:, :, 0:w], reads=[("yb", ys)])
            self.S.phase_end()

    def phase_out(self, l, w_out):
        T = self.T
        final = (l == 3)
        with contextlib.ExitStack() as ph:
            wo = self.sb(ph, [128, 16, D], BF16, "wo")
            for k in range(16):
                self.dma("pool", wo[:, k, :], w_out[k * 128:(k + 1) * 128, :], writes=[("wo", k)])
            wo_k = [("wo", k) for k in range(16)]
            yb = [self.sb(ph, [128, 16, 512], BF16, "yb") for _ in range(2)]
            hb = [self.sb(ph, [128, 8, 512], F32, "hb") for _ in range(2)]
            fuse = (not final) and (l + 1 < self.n_layers)
            if fuse:
                ln_ = l + 1
                npp = NPE if ln_ % 2 == 0 else NPO
                ppn = self.sb(ph, [128, npp], F32, "ppn")
                self.dma("sp", ppn[:], (self.pp_e if ln_ % 2 == 0 else self.pp_o)[ln_ // 2], writes=["ppn"])
                sqn = [self.sb(ph, [128, 8, 512], F32R, "sqn") for _ in range(2)]
                hn = [self.sb(ph, [128, 8, 512], F32, "hn") for _ in range(2)]
                zbn = [self.sb(ph, [128, 8, 512], BF16, "zbn") for _ in range(2)]
                sdn = self.sb(ph, [128, 512], F32, "sdn")
                rsn = self.sb(ph, [128, 512], F32, "rsn")

            def load_o(bi):
                t0, n = self.blocks[bi]
                s = bi % 2
                self.dma("sp", yb[s][:, :, 0:n], self.yblk(bi), writes=[("yb", s)])
                self.dma("sp", hb[s][:, :, 0:n], self.hblk(bi), writes=[("hb", s)])

            load_o(0)
            for bi, (t0, n) in enumerate(self.blocks):
                s = bi % 2
                if bi + 1 < len(self.blocks):
                    load_o(bi + 1)
                for mc in range(8):
                    b = self.nps()
                    self.mm(self.ps[b][:, 0:n], [(wo[:, k, mc * 128:(mc + 1) * 128], yb[s][:, k, 0:n]) for k in range(16)],
                            wo_k + [("yb", s)], [("ps", b)])
                    self.tt("dve", hb[s][:, mc, 0:n], hb[s][:, mc, 0:n], self.ps[b][:, 0:n], ALU.add,
                            [("hb", s), ("ps", b)], [("hb", s)])
                if final:
                    if t0 >= NMETA:
                        self.dma("sp", self.outT[:, t0 - NMETA:t0 - NMETA + n].rearrange("(c p) t -> p c t", p=128),
                                 hb[s][:, :, 0:n], reads=[("hb", s)])
                else:
                    self.dma("sp", self.hblk(bi), hb[s][:, :, 0:n], reads=[("hb", s)])
                if fuse:
                    self.act(sqn[s][:, :, 0:n], hb[s][:, :, 0:n], AF.Square, [("hb", s)], [("sqn", s)])
                    self.rstd((sdn, rsn), [(self.ones_r[:], sqn[s][:, c, 0:n]) for c in range(8)], [("sqn", s)],
                              1.0 / D, n, "nf")
                    self.tt("dve", hn[s][:, :, 0:n], hb[s][:, :, 0:n],
                            rsn[:, 0:n].unsqueeze(1).to_broadcast([128, 8, n]), ALU.mult,
                            [("hb", s), ("rs", "nf")], [("hn", s)])
                    self.tt("pool", zbn[s][:, :, 0:n], hn[s][:, :, 0:n],
                            ppn[:, 0:8].unsqueeze(2).to_broadcast([128, 8, n]), ALU.mult,
                            [("hn", s), "ppn"], [("zbn", s)])
                    self.dma("sp", self.zblk(bi), zbn[s][:, :, 0:n], reads=[("zbn", s)])
            self.S.phase_end()

    def phase_dump(self):
        for bi, (t0, n) in enumerate(self.blocks):
            if t0 >= NMETA:
                self.dma("sp", self.outT[:, t0 - NMETA:t0 - NMETA + n].rearrange("(c p) t -> p c t", p=128), self.hblk(bi))
        self.S.phase_end()

    def phase_lru(self, j):
        T = self.T
        NC = 16
        with contextlib.ExitStack() as ph:
            pp = self.sb(ph, [128, NPO], F32, "pp")
            self.dma("sp", pp[:], self.pp_o[j], writes=["pp"])
            cs = self.sb(ph, [128, 32], F32, "cs")
            one_t = self.sb(ph, [128, 1], F32, "one_t")
            self.memset("dve", one_t[:], 1.0, ["one_t"])
            self.act(cs[:], pp[:, 152:184], AF.Exp, ["pp"], ["cs"], scale=-1.0)
            self.act(cs[:], cs[:], AF.Ln, ["cs", "one_t"], ["cs"], bias=one_t[:, 0:1])
            self.ts("dve", cs[:], cs[:], -8.0, None, ALU.mult, None, ["cs"], ["cs"])
            cs2 = self.sb(ph, [128, 32], F32, "cs2")
            self.ts("dve", cs2[:], cs[:], 2.0, None, ALU.mult, None, ["cs"], ["cs2"])
            NZ = 2
            zb = [self.sb(ph, [128, 8, 512], BF16, "zb") for _ in range(NZ)]
            wu = [self.sb(ph, [128, 8, 128], BF16, "wu") for _ in range(2)]
            wg = [self.sb(ph, [128, 8, 128], BF16, "wg") for _ in range(2)]
            wgt = [self.sb(ph, [128, 4, 128], BF16, "wgt") for _ in range(2)]
            u = [self.sb(ph, [128, T + 3], F32, "u") for _ in range(2)]
            xc = [self.sb(ph, [128, T], F32, "xc") for _ in range(2)]
            xcb = [self.sb(ph, [128, T], BF16, "xcb") for _ in range(2)]
            sgc = [self.sb(ph, [128, T], BF16, "sgc") for _ in range(3)]
            sgs = [self.sb(ph, [128, 512], F32, "sgs") for _ in range(2)]
            gpre = [self.sb(ph, [128, 512], F32, "gpre") for _ in range(2)]
            ra = self.sb(ph, [128, T], F32, "ra")
            ib = self.sb(ph, [128, T], F32, "ib")
            a2 = self.sb(ph, [128, T], F32, "a2")
            hh0 = self.sb(ph, [128, T], F32, "hh0")
            for s in range(2):
                self.memset("dve", u[s][:, 0:2], 0.0, [("upad0", s)])
                self.memset("dve", u[s][:, T + 2:T + 3], 0.0, [("upad1", s)])
            zcount = [0]

            def load_wa(c):
                if c >= NC:
                    return
                s = c % 2
                self.dma("pool", wu[s][:, :, :], self.o_w_in[j, :, c * 128:(c + 1) * 128].rearrange("(k p) e -> p k e", p=128),
                         writes=[("wu", s)])
                self.dma("pool", wg[s][:, :, :], self.o_w_in[j, :, 2048 + c * 128:2048 + (c + 1) * 128].rearrange("(k p) e -> p k e", p=128),
                         writes=[("wg", s)])

            def load_wb(c):
                if c >= NC:
                    return
                s = c % 2
                for d_ in range(2):
                    self.dma("pool", wgt[s][:, d_, :], self.w_a[j, d_, c], writes=[("wgt", s, d_)])
                    self.dma("pool", wgt[s][:, 2 + d_, :], self.w_x[j, d_, c], writes=[("wgt", s, 2 + d_)])

            def a_block(c, bi, t0, n):
                s, s3 = c % 2, c % 3
                zs = zcount[0] % NZ
                zcount[0] += 1
                self.dma("sp", zb[zs][:, :, 0:n], self.zblk(bi), writes=[("zb", zs)])
                b = self.nps()
                self.mm(self.ps[b][:, 0:n], [(wu[s][:, k, :], zb[zs][:, k, 0:n]) for k in range(8)],
                        [("zb", zs), ("wu", s)], [("ps", b)])
                self.act(u[s][:, 2 + t0:2 + t0 + n], self.ps[b][:, 0:n], AF.Copy, [("ps", b)], [("u", s, t0)])
                b = self.nps()
                self.mm(self.ps[b][:, 0:n], [(wg[s][:, k, :], zb[zs][:, k, 0:n]) for k in range(8)],
                        [("zb", zs), ("wg", s)], [("ps", b)])
                self.act(sgs[zs][:, 0:n], self.ps[b][:, 0:n], AF.Sigmoid, [("ps", b)], [("sgs", zs)])
                self.act(gpre[zs][:, 0:n], self.ps[b][:, 0:n], AF.Copy, [("ps", b)], [("gpre", zs)])
                self.tt("pool", sgc[s3][:, t0:t0 + n], gpre[zs][:, 0:n], sgs[zs][:, 0:n], ALU.mult,
                        [("gpre", zs), ("sgs", zs)], [("sgc", s3, t0)])

            def conv(c):
                s = c % 2
                ukeys = [("u", s, t0) for (t0, n) in self.blocks] + [("upad0", s), ("upad1", s)]
                self.ts("dve", xc[s][:, :], u[s][:, 0:T], pp[:, 24 + c:25 + c], pp[:, 8 + c:9 + c], ALU.mult, ALU.add,
                        ukeys + ["pp"], [("xc", s)])
                for tap in range(1, 4):
                    self.stt(xc[s][:, :], u[s][:, tap:tap + T], pp[:, 24 + tap * 16 + c:25 + tap * 16 + c], xc[s][:, :],
                             ALU.mult, ALU.add, ukeys + ["pp", ("xc", s)], [("xc", s)])
                self.copy("dve", xcb[s][:, :], xc[s][:, :], [("xc", s)], [("xcb", s)])

            def stage_b(c, fillers, mid):
                s = c % 2
                rakeys = [("ra", t0) for (t0, n) in self.blocks]
                ibkeys = [("ib", t0) for (t0, n) in self.blocks]
                sgkeys = [("sgc", c % 3, t0) for (t0, n) in self.blocks]
                cnt = [0]

                def fill():
                    cnt[0] += 1
                    if cnt[0] % 4 == 0 and fillers:
                        fillers.pop(0)()

                for d_ in range(2):
                    for (t0, n) in self.blocks:
                        b = self.nps()
                        self.mm(self.ps[b][:, 0:n], [(wgt[s][:, d_, :], xcb[s][:, t0:t0 + n])], [("wgt", s, d_), ("xcb", s)], [("ps", b)])
                        self.act(ra[:, t0:t0 + n], self.ps[b][:, 0:n], AF.Sigmoid, [("ps", b), "pp"], [("ra", t0)],
                                 bias=pp[:, 88 + d_ * 16 + c:89 + d_ * 16 + c])
                        fill()
                    for (t0, n) in self.blocks:
                        b = self.nps()
                        self.mm(self.ps[b][:, 0:n], [(wgt[s][:, 2 + d_, :], xcb[s][:, t0:t0 + n])], [("wgt", s, 2 + d_), ("xcb", s)], [("ps", b)])
                        self.act(ib[:, t0:t0 + n], self.ps[b][:, 0:n], AF.Sigmoid, [("ps", b), "pp"], [("ib", t0)],
                                 bias=pp[:, 120 + d_ * 16 + c:121 + d_ * 16 + c])
                        fill()
                    if d_ == 1:
                        while fillers:
                            fillers.pop(0)()
                    self.act(a2[:, :], ra[:, :], AF.Exp, rakeys + ["cs2", "a2"], ["a2"], scale=cs2[:, d_ * 16 + c:d_ * 16 + c + 1])
                    self.act(ra[:, :], ra[:, :], AF.Exp, rakeys + ["cs"], rakeys, scale=cs[:, d_ * 16 + c:d_ * 16 + c + 1])
                    self.act(a2[:, :], a2[:, :], AF.Sqrt, ["a2", "one_t"], ["a2"], scale=-1.0, bias=one_t[:, 0:1])
                    self.tt("pool", ib[:, :], ib[:, :], xc[s][:, :], ALU.mult, ibkeys + [("xc", s)], ibkeys)
                    self.tt("dve", ib[:, :], ib[:, :], a2[:, :], ALU.mult, ibkeys + ["a2"], ibkeys)
                    if d_ == 0:
                        self.S.op("dve", lambda e: e.tensor_tensor_scan(hh0[:, :], ra[:, :], ib[:, :], 0.0, ALU.mult, ALU.add),
                                  rakeys + ibkeys, ["hh0"])
                        if mid is not None:
                            mid()
                    else:
                        self.S.op("dve", lambda e: e.tensor_tensor_scan(a2[:, ::-1], ra[:, ::-1], ib[:, ::-1], 0.0, ALU.mult, ALU.add),
                                  rakeys + ibkeys, ["a2"])
                self.tt("dve", hh0[:, :], hh0[:, :], a2[:, :], ALU.add, ["hh0", "a2"], ["hh0"])
                self.tt("dve", xcb[s][:, :], hh0[:, :], sgc[c % 3][:, :], ALU.mult, ["hh0"] + sgkeys, [("xcb", s)])
                for bi, (t0, n) in enumerate(self.blocks):
                    self.dma("pool", self.yblk(bi)[:, c, :], xcb[s][:, t0:t0 + n], reads=[("xcb", s)])

            def a_fillers(c):
                if c >= NC:
                    return []
                return [(lambda bi=bi, t0=t0, n=n: a_block(c, bi, t0, n)) for bi, (t0, n) in enumerate(self.blocks)]

            load_wa(0)
            load_wa(1)
            load_wb(0)
            load_wb(1)
            for c in range(2):
                for f in a_fillers(c):
                    f()
                load_wa(c + 2)
            conv(0)
            for k in range(NC):
                fl = a_fillers(k + 2)
                stage_b(k, fl, (lambda k=k: conv(k + 1)) if k + 1 < NC else None)
                load_wa(k + 4)
                load_wb(k + 2)
            self.S.phase_end()


def _pack_params(inp):
    f = lambda a: np.asarray(a, dtype=np.float32)
    pe = np.zeros((2, 128, NPE), np.float32)
    po = np.zeros((2, 128, NPO), np.float32)
    for j in range(2):
        pe[j, :, 0:8] = f(inp["norm_g"])[2 * j].reshape(8, 128).T
        pe[j, :, 8:11] = f(inp["mla_g_q_lat"])[j].reshape(3, 128).T
        pe[j, :, 11:13] = f(inp["mla_g_kv_lat"])[j].reshape(2, 128).T
        pe[j, :, 13] = f(inp["mla_g_qn"])[j, 0:128]
        pe[j, 0:64, 14] = f(inp["mla_g_qn"])[j, 128:192]
        pe[j, :, 15] = f(inp["mla_g_kn"])[j, 0:128]
        pe[j, 0:64, 16] = f(inp["mla_g_kn"])[j, 128:192]
        pe[j, :, 17] = f(inp["swa_g_qn"])[j]
        pe[j, :, 18] = f(inp["swa_g_kn"])[j]
        pe[j, :, 19:27] = np.broadcast_to(f(inp["swa_sink"])[j][None, :], (128, 8))
        po[j, :, 0:8] = f(inp["norm_g"])[2 * j + 1].reshape(8, 128).T
        po[j, :, 8:24] = f(inp["lru_conv_b"])[j].reshape(16, 128).T
        for tap in range(4):
            po[j, :, 24 + tap * 16:24 + (tap + 1) * 16] = f(inp["lru_conv_w"])[j, tap].reshape(16, 128).T
        for d_ in range(2):
            po[j, :, 88 + d_ * 16:88 + (d_ + 1) * 16] = f(inp["lru_b_a"])[j, d_].reshape(16, 128).T
            po[j, :, 120 + d_ * 16:120 + (d_ + 1) * 16] = f(inp["lru_b_x"])[j, d_].reshape(16, 128).T
            po[j, :, 152 + d_ * 16:152 + (d_ + 1) * 16] = f(inp["lru_lambda"])[j, d_].reshape(16, 128).T
    return pe, po


def _rope_consts():
    cst = np.zeros((128, 4), np.float32)
    inv128 = (10000.0 ** (-np.arange(0, 128, 2, dtype=np.float32) / np.float32(128))).astype(np.float32)
    inv64 = (10000.0 ** (-np.arange(0, 64, 2, dtype=np.float32) / np.float32(64))).astype(np.float32)
    cst[:, 0] = np.concatenate([inv128, inv128])
    cst[0:64, 1] = np.concatenate([inv64, inv64])
    return cst


def make_in_maps(inp, n_cores):
    f = lambda a: np.ascontiguousarray(np.asarray(a, dtype=np.float32))
    pe, po = _pack_params(inp)
    shared = {
        "metaT": f(np.asarray(inp["meta_tokens"]).T),
        "even_w_in": f(inp["even_w_in"]),
        "mla_w_uq": f(np.asarray(inp["mla_w_uq"]).reshape(2, 384, 8 * 192)),
        "mla_w_ukv": f(np.asarray(inp["mla_w_ukv"]).reshape(2, 256, 8 * 256)),
        "even_w_out": f(inp["even_w_out"]),
        "odd_w_in": f(inp["odd_w_in"]),
        "lru_w_a": f(inp["lru_w_a"]),
        "lru_w_x": f(inp["lru_w_x"]),
        "odd_w_out": f(inp["odd_w_out"]),
        "pp_even": pe, "pp_odd": po, "cst": _rope_consts(),
    }
    x = np.asarray(inp["x"], dtype=np.float32)
    maps = []
    for b in range(n_cores):
        m = dict(shared)
        m["xT"] = np.ascontiguousarray(x[b].T)
        maps.append(m)
    return maps


_CACHE = {}


def kernel(**inputs):
    x = np.asarray(inputs["x"])
    B, SEQ, _ = x.shape
    key = (SEQ, 4)
    if key not in _CACHE:
        _CACHE[key] = Prog(SEQ, 4).build()
    nc = _CACHE[key]
    in_maps = make_in_maps(inputs, B)
    res = run_bass_kernel_spmd(nc, in_maps, core_ids=list(range(B)))
    out = np.stack([np.asarray(r["outT"]).T for r in res.results], axis=0)
    return np.ascontiguousarray(out.astype(np.float32))
```

```python
import math
import contextlib
import numpy as np
import concourse.bass as bass
import concourse.mybir as mybir
from concourse.bass_utils import run_bass_kernel_spmd

F32 = mybir.dt.float32
F32R = mybir.dt.float32r
BF16 = mybir.dt.bfloat16
I32 = mybir.dt.int32
AF = mybir.ActivationFunctionType
ALU = mybir.AluOpType

D = 1024
NMETA = 16
EPS = 1e-6
NPE = 27
NPO = 184


class _Op:
    __slots__ = ("eng", "fn", "dma", "deps", "needs_sig", "sem", "val", "done", "order")

    def __init__(self, eng, fn, dma):
        self.eng = eng
        self.fn = fn
        self.dma = dma
        self.deps = ()
        self.needs_sig = False
        self.sem = None
        self.val = 0
        self.done = False
        self.order = 0


class Sched:
    COMPUTE = ("pe", "act", "dve", "pool")
    RING = 8

    def __init__(self, nc, stack):
        self.nc = nc
        self.E = {"pe": nc.tensor, "act": nc.scalar, "dve": nc.vector,
                  "pool": nc.gpsimd, "sp": nc.sync}
        self.csem = {e: stack.enter_context(nc.semaphore("c_" + e)) for e in self.COMPUTE}
        self.ccount = {e: 0 for e in self.COMPUTE}
        self.rings = {}
        for q in ("sp", "pool"):
            self.rings[q] = [stack.enter_context(nc.semaphore("d_%s%d" % (q, i)))
                             for i in range(self.RING)]
        self.dcount = {q: 0 for q in self.rings}
        self.ops = []
        self.last_w = {}
        self.readers = {}
        self.known = {e: {} for e in self.E}
        self.n_inst = 0
        self.n_ops = 0

    def op(self, eng, fn, reads=(), writes=(), dma=False):
        o = _Op(eng, fn, dma)
        deps = {}

        def add(d):
            if d is None or d.done:
                return
            if (not d.dma) and (not dma) and d.eng == "pe" and eng == "pe":
                return
            deps[id(d)] = d

        for k in reads:
            add(self.last_w.get(k))
        for k in writes:
            add(self.last_w.get(k))
            r = self.readers.get(k)
            if r:
                for d in r[0].values():
                    add(d)
                for d in r[1]:
                    add(d)
        best = {}
        out = []
        for d in deps.values():
            if d.dma:
                out.append(d)
            else:
                b = best.get(d.eng)
                if b is None or d.order > b.order:
                    best[d.eng] = d
        out.extend(best.values())
        o.deps = out
        for d in out:
            d.needs_sig = True
        self.n_ops += 1
        o.order = self.n_ops
        for k in writes:
            self.last_w[k] = o
            self.readers[k] = ({}, [])
        for k in reads:
            r = self.readers.get(k)
            if r is None:
                r = self.readers[k] = ({}, [])
            if dma:
                r[1].append(o)
            else:
                r[0][eng] = o
        self.ops.append(o)
        return o

    def _wait(self, eng, sem, val):
        kn = self.known[eng]
        key = id(sem)
        if kn.get(key, 0) >= val:
            return
        self.E[eng].wait_ge(sem, val)
        kn[key] = val
        self.n_inst += 1

    def phase_end(self):
        last = {}
        for o in self.ops:
            if not o.dma:
                last[o.eng] = o
        for o in last.values():
            o.needs_sig = True
        for o in self.ops:
            E = self.E[o.eng]
            for d in o.deps:
                if d.done:
                    continue
                self._wait(o.eng, d.sem, d.val)
            if o.dma:
                ring = self.rings[o.eng]
                i = self.dcount[o.eng]
                self.dcount[o.eng] = i + 1
                sem = ring[i % self.RING]
                tgt = 16 * (i // self.RING + 1)
                if tgt > 16:
                    self._wait(o.eng, sem, tgt - 16)
                ins = o.fn(E)
                ins.then_inc(sem, 16)
                o.sem, o.val = sem, tgt
            else:
                ins = o.fn(E)
                if o.needs_sig:
                    self.ccount[o.eng] += 1
                    ins.then_inc(self.csem[o.eng], 1)
                    o.sem, o.val = self.csem[o.eng], self.ccount[o.eng]
            self.n_inst += 1
        for eng in self.E:
            for e in self.COMPUTE:
                if self.ccount[e] > 0:
                    self._wait(eng, self.csem[e], self.ccount[e])
            for q, ring in self.rings.items():
                n = self.dcount[q]
                for j in range(self.RING):
                    uses = (n - j + self.RING - 1) // self.RING if n > j else 0
                    if uses > 0:
                        self._wait(eng, ring[j], 16 * uses)
        for o in self.ops:
            o.done = True
        self.last_w = {}
        self.readers = {}
        self.ops = []


class Prog:
    def __init__(self, SEQ, n_layers=4):
        assert SEQ % 512 == 0
        self.SEQ = SEQ
        self.T = SEQ + NMETA
        self.n_layers = n_layers
        self.blocks = [(0, NMETA)] + [(NMETA + 512 * j, 512) for j in range(SEQ // 512)]
        self.NT = SEQ // 128
        self.ktiles = [(0, NMETA)] + [(NMETA + 128 * j, 128) for j in range(self.NT)]
        self.uid = 0
        self.psi = 0

    def sb(self, ph, shape, dt, name="t"):
        self.uid += 1
        return ph.enter_context(self.nc.sbuf_tensor("%s_%d" % (name, self.uid), list(shape), dt))

    def dma(self, q, out, in_, reads=(), writes=()):
        self.S.op(q, lambda e: e.dma_start(out=out, in_=in_), reads, writes, dma=True)

    def mm(self, out, pairs, reads, writes, start=True, stop=True):
        pairs = list(pairs)

        def fn(e):
            n = len(pairs)
            ins = None
            for i, (l, r) in enumerate(pairs):
                ins = e.matmul(out, l, r, start=(start and i == 0), stop=(stop and i == n - 1))
            return ins
        self.S.op("pe", fn, reads, writes)

    def act(self, out, in_, func, reads, writes, scale=1.0, bias=None):
        if bias is None:
            self.S.op("act", lambda e: e.activation(out=out, in_=in_, func=func, scale=scale), reads, writes)
        else:
            self.S.op("act", lambda e: e.activation(out=out, in_=in_, func=func, scale=scale, bias=bias), reads, writes)

    def tt(self, eng, out, a, b, op, reads, writes):
        self.S.op(eng, lambda e: e.tensor_tensor(out, a, b, op), reads, writes)

    def ts(self, eng, out, a, s1, s2, op0, op1, reads, writes):
        if s2 is None:
            self.S.op(eng, lambda e: e.tensor_scalar(out, a, s1, None, op0), reads, writes)
        else:
            self.S.op(eng, lambda e: e.tensor_scalar(out, a, s1, s2, op0, op1), reads, writes)

    def stt(self, out, a, s, b, op0, op1, reads, writes):
        self.S.op("dve", lambda e: e.scalar_tensor_tensor(out, a, s, b, op0, op1), reads, writes)

    def recip(self, out, in_, reads, writes):
        self.S.op("dve", lambda e: e.reciprocal(out, in_), reads, writes)

    def copy(self, eng, out, in_, reads, writes):
        self.S.op(eng, lambda e: e.tensor_copy(out, in_), reads, writes)

    def memset(self, eng, ap, v, writes):
        self.S.op(eng, lambda e: e.memset(ap, v), (), writes)

    def hblk(self, bi):
        n = self.blocks[bi][1]
        return self.hB[bi, :, 0:8 * n].rearrange("p (c t) -> p c t", c=8)

    def zblk(self, bi):
        n = self.blocks[bi][1]
        return self.zB[bi, :, 0:8 * n].rearrange("p (c t) -> p c t", c=8)

    def yblk(self, bi):
        n = self.blocks[bi][1]
        return self.yB[bi, :, 0:16 * n].rearrange("p (c t) -> p c t", c=16)

    def nps(self):
        i = self.psi
        self.psi = (i + 1) % 8
        return i

    def rstd(self, ph_tiles, pairs, pair_reads, inv_n, n, tag):
        sd, rs = ph_tiles
        b = self.nps()
        self.mm(self.ps[b][:, 0:n], pairs, reads=list(pair_reads) + ["ones_r"], writes=[("ps", b)])
        self.act(sd[:, 0:n], self.ps[b][:, 0:n], AF.Ln, reads=[("ps", b), "eps"], writes=[("sd", tag)],
                 scale=inv_n, bias=self.eps[:, 0:1])
        self.act(rs[:, 0:n], sd[:, 0:n], AF.Exp, reads=[("sd", tag)], writes=[("rs", tag)], scale=-0.5)

    def build(self):
        SEQ, T = self.SEQ, self.T
        nc = bass.Bass("TRN2", target_bir_lowering=False)
        self.nc = nc

        def din(name, shape):
            return nc.dram_tensor(name, list(shape), F32, kind="ExternalInput").ap()

        def dscr(name, shape, dt):
            return nc.dram_tensor(name, list(shape), dt).ap()

        self.xT = din("xT", [D, SEQ])
        self.metaT = din("metaT", [D, NMETA])
        self.e_w_in = din("even_w_in", [2, D, 4288])
        self.w_uq = din("mla_w_uq", [2, 384, 8 * 192])
        self.w_ukv = din("mla_w_ukv", [2, 256, 8 * 256])
        self.e_w_out = din("even_w_out", [2, 2048, D])
        self.o_w_in = din("odd_w_in", [2, D, 4096])
        self.w_a = din("lru_w_a", [2, 2, 16, 128, 128])
        self.w_x = din("lru_w_x", [2, 2, 16, 128, 128])
        self.o_w_out = din("odd_w_out", [2, 2048, D])
        self.pp_e = din("pp_even", [2, 128, NPE])
        self.pp_o = din("pp_odd", [2, 128, NPO])
        self.cst = din("cst", [128, 4])
        self.outT = nc.dram_tensor("outT", [D, SEQ], F32, kind="ExternalOutput").ap()

        NB_ = len(self.blocks)
        self.hB = dscr("hB", [NB_, 128, 8 * 512], F32)
        self.zB = dscr("zB", [NB_, 128, 8 * 512], BF16)
        self.tab = dscr("tab", [4, 128, T], F32)
        self.qn = dscr("qn", [8, 128, T], BF16)
        self.qr = dscr("qr", [8, 64, T], BF16)
        self.kn = dscr("kn", [8, 128, T], BF16)
        self.kr = dscr("kr", [8, 64, T], BF16)
        self.vm = dscr("vm", [8, T, 128], BF16)
        self.qs = dscr("qs", [8, 128, T], BF16)
        self.ks = dscr("ks", [2, 128, T], BF16)
        self.vs = dscr("vs", [2, T, 128], BF16)
        self.sg = dscr("sg", [16, 128, T], BF16)
        self.yB = dscr("yB", [NB_, 128, 16 * 512], BF16)

        with contextlib.ExitStack() as gs:
            self.S = Sched(nc, gs)
            self.ps = [gs.enter_context(nc.psum_tensor("ps%d" % i, [128, 512], F32)) for i in range(8)]
            self.ones_f = self.sb(gs, [128, 128], F32, "ones_f")
            self.ones_r = self.sb(gs, [128, 128], F32R, "ones_r")
            self.ones_b = self.sb(gs, [128, 128], BF16, "ones_b")
            self.eps = self.sb(gs, [128, 1], F32, "eps")
            self.cst_t = self.sb(gs, [128, 4], F32, "cst")
            self.setup()
            for l in range(self.n_layers):
                j = l // 2
                if l == 0:
                    self.phase_norm(l)
                if l % 2 == 0:
                    self.phase_mla_proj(j)
                    self.phase_swa_proj(j)
                    self.phase_mla_attn(j)
                    self.phase_swa_attn(j)
                    if getattr(self, "debug_y", False):
                        cb_ = 8 if self.debug_y == 2 else 0
                        for c in range(8):
                            for bi, (t0, n) in enumerate(self.blocks):
                                if t0 >= NMETA:
                                    self.dma("pool", self.outT[c * 128:(c + 1) * 128, t0 - NMETA:t0 - NMETA + n],
                                             self.yblk(bi)[:, cb_ + c, :])
                        self.S.phase_end()
                        return nc
                    self.phase_out(l, self.e_w_out[j])
                else:
                    self.phase_lru(j)
                    self.phase_out(l, self.o_w_out[j])
            if self.n_layers < 4:
                self.phase_dump()
        return nc

    def setup(self):
        nc, T, SEQ = self.nc, self.T, self.SEQ
        with contextlib.ExitStack() as ph:
            self.memset("dve", self.ones_f[:], 1.0, ["ones_f"])
            self.memset("dve", self.ones_b[:], 1.0, ["ones_b"])
            self.memset("dve", self.eps[:], EPS, ["eps"])
            self.act(self.ones_r[:], self.ones_f[:], AF.Copy, ["ones_f"], ["ones_r"])
            self.dma("sp", self.cst_t[:], self.cst[:, :], writes=["cst"])
            for bi, (t0, n) in enumerate(self.blocks):
                if t0 < NMETA:
                    self.dma("sp", self.hblk(bi), self.metaT.rearrange("(c p) t -> p c t", p=128))
                else:
                    self.dma("sp", self.hblk(bi), self.xT[:, t0 - NMETA:t0 - NMETA + n].rearrange("(c p) t -> p c t", p=128))
            pos = self.sb(ph, [128, T], F32, "pos")
            ang = self.sb(ph, [128, T], F32, "ang")
            kf = self.sb(ph, [128, T], F32, "kf")
            ki = self.sb(ph, [128, T], I32, "ki")
            rr = self.sb(ph, [128, T], F32, "rr")
            sn = self.sb(ph, [128, T], F32, "sn")
            sgn = self.sb(ph, [128, 2], F32, "sgn")
            self.S.op("pool", lambda e: e.iota(pos[:], [[1, T]], base=0, channel_multiplier=0,
                                               allow_small_or_imprecise_dtypes=True), (), ["pos"])
            self.memset("dve", sgn[0:64, 0:1], 1.0, ["sgn0"])
            self.memset("dve", sgn[64:128, 0:1], -1.0, ["sgn1"])
            self.memset("dve", sgn[:, 1:2], 0.0, ["sgn2"])
            self.S.op("dve", lambda e: e.memset(sgn[0:32, 1:2], 1.0), ["sgn2"], ["sgn3"])
            self.S.op("dve", lambda e: e.memset(sgn[32:64, 1:2], -1.0), ["sgn2"], ["sgn4"])
            sgn_keys = ["sgn0", "sgn1", "sgn3", "sgn4"]
            for col in range(2):
                self.ts("dve", ang[:], pos[:], self.cst_t[:, col:col + 1], None, ALU.mult, None,
                        ["pos", "cst", "ang"], ["ang"])
                for which in range(2):
                    off = math.pi / 2 if which == 0 else 0.0
                    self.ts("dve", kf[:], ang[:], off, 1.0 / (2 * math.pi), ALU.add, ALU.mult, ["ang", "kf"], ["kf"])
                    self.copy("dve", ki[:], kf[:], ["kf", "ki"], ["ki"])
                    self.copy("dve", kf[:], ki[:], ["ki", "kf"], ["kf"])
                    self.stt(rr[:], kf[:], -2 * math.pi, ang[:], ALU.mult, ALU.add, ["kf", "ang", "rr"], ["rr"])
                    self.ts("dve", rr[:], rr[:], off, 3.1415925, ALU.add, ALU.min, ["rr"], ["rr"])
                    self.ts("dve", rr[:], rr[:], -3.1415925, None, ALU.max, None, ["rr"], ["rr"])
                    self.act(sn[:], rr[:], AF.Sin, ["rr", "sn"], ["sn"])
                    if which == 1:
                        self.ts("dve", sn[:], sn[:], sgn[:, col:col + 1], None, ALU.mult, None,
                                ["sn"] + sgn_keys, ["sn"])
                    self.dma("sp", self.tab[2 * col + which], sn[:], reads=["sn"])
            self.S.phase_end()

    def phase_norm(self, l):
        j = l // 2
        with contextlib.ExitStack() as ph:
            npp = NPE if l % 2 == 0 else NPO
            pp = self.sb(ph, [128, npp], F32, "pp")
            self.dma("sp", pp[:], (self.pp_e if l % 2 == 0 else self.pp_o)[j], writes=["pp"])
            hb = [self.sb(ph, [128, 8, 512], F32, "hb") for _ in range(2)]
            sq = [self.sb(ph, [128, 8, 512], F32R, "sq") for _ in range(2)]
            zb = [self.sb(ph, [128, 8, 512], BF16, "zb") for _ in range(2)]
            sd = self.sb(ph, [128, 512], F32, "sd")
            rs = self.sb(ph, [128, 512], F32, "rs")
            def load_n(bi):
                t0, n = self.blocks[bi]
                s = bi % 2
                self.dma("sp", hb[s][:, :, 0:n], self.hblk(bi), writes=[("hb", s)])

            load_n(0)
            for bi, (t0, n) in enumerate(self.blocks):
                s = bi % 2
                if bi + 1 < len(self.blocks):
                    load_n(bi + 1)
                self.act(sq[s][:, :, 0:n], hb[s][:, :, 0:n], AF.Square, [("hb", s)], [("sq", s)])
                self.rstd((sd, rs), [(self.ones_r[:], sq[s][:, c, 0:n]) for c in range(8)], [("sq", s)],
                          1.0 / D, n, "n")
                self.tt("dve", hb[s][:, :, 0:n], hb[s][:, :, 0:n],
                        rs[:, 0:n].unsqueeze(1).to_broadcast([128, 8, n]), ALU.mult,
                        [("hb", s), ("rs", "n")], [("hb", s)])
                self.tt("pool", zb[s][:, :, 0:n], hb[s][:, :, 0:n],
                        pp[:, 0:8].unsqueeze(2).to_broadcast([128, 8, n]), ALU.mult,
                        [("hb", s), "pp"], [("zb", s)])
                self.dma("sp", self.zblk(bi), zb[s][:, :, 0:n], reads=[("zb", s)])
            self.S.phase_end()

    def rope(self, out_bf, u, tcos, tsinx, rs, half, n, tmp1, tmp2, ukey, tkey, tskey, rskey, okey, tk1, tk2):
        full = 2 * half
        self.tt("dve", tmp1[0:full, 0:n], u[0:full, 0:n], tcos[0:full, 0:n], ALU.mult, [ukey, tkey], [tk1])
        self.tt("pool", tmp2[0:half, 0:n], u[half:full, 0:n], tsinx[half:full, 0:n], ALU.mult,
                [ukey, tskey], [(tk2, 0)])
        self.tt("pool", tmp2[half:full, 0:n], u[0:half, 0:n], tsinx[0:half, 0:n], ALU.mult,
                [ukey, tskey], [(tk2, 1)])
        self.tt("pool", tmp1[0:full, 0:n], tmp1[0:full, 0:n], tmp2[0:full, 0:n], ALU.add,
                [tk1, (tk2, 0), (tk2, 1)], [tk1])
        self.tt("dve", out_bf[0:full, 0:n], tmp1[0:full, 0:n], rs[0:full, 0:n], ALU.mult, [tk1, rskey], [okey])

    def phase_mla_proj(self, j):
        T = self.T
        with contextlib.ExitStack() as ph:
            pp = self.sb(ph, [128, NPE], F32, "pp")
            self.dma("sp", pp[:], self.pp_e[j], writes=["pp"])
            wA = self.sb(ph, [128, 8, 704], BF16, "wA")
            wuq = self.sb(ph, [128, 3, 1536], BF16, "wuq")
            wukv = self.sb(ph, [128, 2, 2048], BF16, "wukv")
            for k in range(8):
                self.dma("pool", wA[:, k, :], self.e_w_in[j, k * 128:(k + 1) * 128, 0:704], writes=[("wA", k)])
            for k in range(3):
                self.dma("pool", wuq[:, k, :], self.w_uq[j, k * 128:(k + 1) * 128, :], writes=[("wuq", k)])
            for k in range(2):
                self.dma("pool", wukv[:, k, :], self.w_ukv[j, k * 128:(k + 1) * 128, :], writes=[("wukv", k)])
            wA_k = [("wA", k) for k in range(8)]
            wuq_k = [("wuq", k) for k in range(3)]
            wukv_k = [("wukv", k) for k in range(2)]
            zb = [self.sb(ph, [128, 8, 512], BF16, "zb") for _ in range(2)]
            tcm = [self.sb(ph, [64, 512], F32, "tcm") for _ in range(2)]
            tsm = [self.sb(ph, [64, 512], F32, "tsm") for _ in range(2)]
            cq = self.sb(ph, [128, 3, 512], F32, "cq")
            sqc = self.sb(ph, [128, 3, 512], F32R, "sqc")
            cqn = self.sb(ph, [128, 3, 512], BF16, "cqn")
            ckv = self.sb(ph, [128, 2, 512], F32, "ckv")
            sqk = self.sb(ph, [128, 2, 512], F32R, "sqk")
            ckvn = self.sb(ph, [128, 2, 512], BF16, "ckvn")
            sd = self.sb(ph, [128, 512], F32, "sd")
            rs = self.sb(ph, [128, 512], F32, "rs")
            kpu = self.sb(ph, [64, 512], F32, "kpu")
            kpsq = self.sb(ph, [64, 512], F32R, "kpsq")
            kpr = self.sb(ph, [64, 512], F32, "kpr")
            t1 = self.sb(ph, [128, 512], F32, "t1")
            t2 = self.sb(ph, [128, 512], F32, "t2")
            NS = 3
            sqh = [self.sb(ph, [128, 512], F32R, "sqh") for _ in range(NS)]
            sqh2 = [self.sb(ph, [64, 512], F32R, "sqh2") for _ in range(NS)]
            uh = [self.sb(ph, [128, 512], F32, "uh") for _ in range(NS)]
            uh2 = [self.sb(ph, [64, 512], F32, "uh2") for _ in range(NS)]
            sdh = [self.sb(ph, [128, 512], F32, "sdh") for _ in range(NS)]
            rsh = [self.sb(ph, [128, 512], F32, "rsh") for _ in range(NS)]
            ob = [self.sb(ph, [128, 512], BF16, "ob") for _ in range(4)]
            ob2 = [self.sb(ph, [64, 512], BF16, "ob2") for _ in range(4)]
            vb = [self.sb(ph, [128, 512], BF16, "vb") for _ in range(2)]
            obi = 0
            ob2i = 0
            vbi = 0
            hs = 0
            def load_m(bi):
                t0, n = self.blocks[bi]
                s = bi % 2
                self.dma("sp", zb[s][:, :, 0:n], self.zblk(bi), writes=[("zb", s)])
                self.dma("sp", tcm[s][:, 0:n], self.tab[2, 0:64, t0:t0 + n], writes=[("tcm", s)])
                self.dma("sp", tsm[s][:, 0:n], self.tab[3, 0:64, t0:t0 + n], writes=[("tsm", s)])

            load_m(0)
            for bi, (t0, n) in enumerate(self.blocks):
                s = bi % 2
                if bi + 1 < len(self.blocks):
                    load_m(bi + 1)
                for mc in range(3):
                    b = self.nps()
                    self.mm(self.ps[b][:, 0:n], [(wA[:, k, mc * 128:(mc + 1) * 128], zb[s][:, k, 0:n]) for k in range(8)],
                            wA_k + [("zb", s)], [("ps", b)])
                    self.act(cq[:, mc, 0:n], self.ps[b][:, 0:n], AF.Copy, [("ps", b)], [("cq", mc)])
                    self.act(sqc[:, mc, 0:n], self.ps[b][:, 0:n], AF.Square, [("ps", b)], [("sqc", mc)])
                self.rstd((sd, rs), [(self.ones_r[:], sqc[:, mc, 0:n]) for mc in range(3)],
                          [("sqc", mc) for mc in range(3)], 1.0 / 384, n, "lat")
                self.tt("dve", cq[:, :, 0:n], cq[:, :, 0:n], rs[:, 0:n].unsqueeze(1).to_broadcast([128, 3, n]),
                        ALU.mult, [("cq", mc) for mc in range(3)] + [("rs", "lat")], [("cq", mc) for mc in range(3)])
                self.tt("pool", cqn[:, :, 0:n], cq[:, :, 0:n], pp[:, 8:11].unsqueeze(2).to_broadcast([128, 3, n]),
                        ALU.mult, [("cq", mc) for mc in range(3)] + ["pp"], ["cqn"])
                for mc in range(2):
                    b = self.nps()
                    self.mm(self.ps[b][:, 0:n], [(wA[:, k, 384 + mc * 128:384 + (mc + 1) * 128], zb[s][:, k, 0:n]) for k in range(8)],
                            wA_k + [("zb", s)], [("ps", b)])
                    self.act(ckv[:, mc, 0:n], self.ps[b][:, 0:n], AF.Copy, [("ps", b)], [("ckv", mc)])
                    self.act(sqk[:, mc, 0:n], self.ps[b][:, 0:n], AF.Square, [("ps", b)], [("sqk", mc)])
                self.rstd((sd, rs), [(self.ones_r[:], sqk[:, mc, 0:n]) for mc in range(2)],
                          [("sqk", mc) for mc in range(2)], 1.0 / 256, n, "lat")
                self.tt("dve", ckv[:, :, 0:n], ckv[:, :, 0:n], rs[:, 0:n].unsqueeze(1).to_broadcast([128, 2, n]),
                        ALU.mult, [("ckv", mc) for mc in range(2)] + [("rs", "lat")], [("ckv", mc) for mc in range(2)])
                self.tt("pool", ckvn[:, :, 0:n], ckv[:, :, 0:n], pp[:, 11:13].unsqueeze(2).to_broadcast([128, 2, n]),
                        ALU.mult, [("ckv", mc) for mc in range(2)] + ["pp"], ["ckvn"])
                b = self.nps()
                self.mm(self.ps[b][0:64, 0:n], [(wA[:, k, 640:704], zb[s][:, k, 0:n]) for k in range(8)],
                        wA_k + [("zb", s)], [("ps", b)])
                self.act(kpu[:, 0:n], self.ps[b][0:64, 0:n], AF.Copy, [("ps", b), "pp"], ["kpu"], scale=pp[0:64, 16:17])
                self.act(kpsq[:, 0:n], self.ps[b][0:64, 0:n], AF.Square, [("ps", b)], ["kpsq"])
                self.tt("dve", t1[0:64, 0:n], kpu[0:64, 0:n], tcm[s][0:64, 0:n], ALU.mult, ["kpu", ("tcm", s)], ["t1"])
                self.tt("pool", t2[0:32, 0:n], kpu[32:64, 0:n], tsm[s][32:64, 0:n], ALU.mult, ["kpu", ("tsm", s)], [("t2", 0)])
                self.tt("pool", t2[32:64, 0:n], kpu[0:32, 0:n], tsm[s][0:32, 0:n], ALU.mult, ["kpu", ("tsm", s)], [("t2", 1)])
                self.tt("dve", kpr[0:64, 0:n], t1[0:64, 0:n], t2[0:64, 0:n], ALU.add, ["t1", ("t2", 0), ("t2", 1)], ["kpr"])
                pend = [None]

                def flush_pend():
                    if pend[0] is not None:
                        f_ = pend[0]
                        pend[0] = None
                        f_()

                for h in range(8):
                    hs = (hs + 1) % NS
                    bn_ = self.nps()
                    self.mm(self.ps[bn_][:, 0:n], [(wuq[:, k, h * 192:h * 192 + 128], cqn[:, k, 0:n]) for k in range(3)],
                            wuq_k + ["cqn"], [("ps", bn_)])
                    br_ = self.nps()
                    self.mm(self.ps[br_][0:64, 0:n], [(wuq[:, k, h * 192 + 128:h * 192 + 192], cqn[:, k, 0:n]) for k in range(3)],
                            wuq_k + ["cqn"], [("ps", br_)])
                    self.act(sqh[hs][:, 0:n], self.ps[bn_][:, 0:n], AF.Square, [("ps", bn_)], [("sqh", hs)])
                    self.act(sqh2[hs][:, 0:n], self.ps[br_][0:64, 0:n], AF.Square, [("ps", br_)], [("sqh2", hs)])
                    self.act(uh[hs][:, 0:n], self.ps[bn_][:, 0:n], AF.Copy, [("ps", bn_), "pp"], [("uh", hs)], scale=pp[:, 13:14])
                    self.act(uh2[hs][:, 0:n], self.ps[br_][0:64, 0:n], AF.Copy, [("ps", br_), "pp"], [("uh2", hs)], scale=pp[0:64, 14:15])
                    flush_pend()

                    def fin_q(hs=hs, h=h, obi=obi, ob2i=ob2i):
                        self.rstd((sdh[hs], rsh[hs]), [(self.ones_r[:], sqh[hs][:, 0:n]), (self.ones_r[0:64, :], sqh2[hs][0:64, 0:n])],
                                  [("sqh", hs), ("sqh2", hs)], 1.0 / 192, n, ("h", hs))
                        o = ob[obi]
                        self.tt("dve", o[:, 0:n], uh[hs][:, 0:n], rsh[hs][:, 0:n], ALU.mult,
                                [("uh", hs), ("rs", ("h", hs))], [("ob", obi)])
                        self.dma("sp", self.qn[h, :, t0:t0 + n], o[:, 0:n], reads=[("ob", obi)])
                        o2 = ob2[ob2i]
                        self.rope(o2, uh2[hs], tcm[s], tsm[s], rsh[hs], 32, n, t1, t2, ("uh2", hs), ("tcm", s), ("tsm", s),
                                  ("rs", ("h", hs)), ("ob2", ob2i), "t1", "t2")
                        self.dma("sp", self.qr[h, :, t0:t0 + n], o2[0:64, 0:n], reads=[("ob2", ob2i)])
                    pend[0] = fin_q
                    obi = (obi + 1) % 4
                    ob2i = (ob2i + 1) % 4
                    hs = (hs + 1) % NS
                    bk_ = self.nps()
                    self.mm(self.ps[bk_][:, 0:n], [(wukv[:, k, h * 256:h * 256 + 128], ckvn[:, k, 0:n]) for k in range(2)],
                            wukv_k + ["ckvn"], [("ps", bk_)])
                    self.act(sqh[hs][:, 0:n], self.ps[bk_][:, 0:n], AF.Square, [("ps", bk_)], [("sqh", hs)])
                    self.act(uh[hs][:, 0:n], self.ps[bk_][:, 0:n], AF.Copy, [("ps", bk_), "pp"], [("uh", hs)], scale=pp[:, 15:16])
                    flush_pend()

                    def fin_k(hs=hs, h=h, obi=obi, ob2i=ob2i):
                        self.rstd((sdh[hs], rsh[hs]), [(self.ones_r[:], sqh[hs][:, 0:n]), (self.ones_r[0:64, :], kpsq[0:64, 0:n])],
                                  [("sqh", hs), "kpsq"], 1.0 / 192, n, ("h", hs))
                        o = ob[obi]
                        self.tt("dve", o[:, 0:n], uh[hs][:, 0:n], rsh[hs][:, 0:n], ALU.mult,
                                [("uh", hs), ("rs", ("h", hs))], [("ob", obi)])
                        self.dma("sp", self.kn[h, :, t0:t0 + n], o[:, 0:n], reads=[("ob", obi)])
                        o2 = ob2[ob2i]
                        self.tt("pool", o2[0:64, 0:n], kpr[0:64, 0:n], rsh[hs][0:64, 0:n], ALU.mult,
                                ["kpr", ("rs", ("h", hs))], [("ob2", ob2i)])
                        self.dma("sp", self.kr[h, :, t0:t0 + n], o2[0:64, 0:n], reads=[("ob2", ob2i)])
                    pend[0] = fin_k
                    obi = (obi + 1) % 4
                    ob2i = (ob2i + 1) % 4
                flush_pend()
                ntt = max(1, n // 128)
                for tt_ in range(ntt):
                    m = min(128, n)
                    c0 = tt_ * 128
                    for hg in range(2):
                        b = self.nps()
                        rhs = [wukv[:, k, :].rearrange("p (h e) -> p h e", e=256)[:, hg * 4:(hg + 1) * 4, 128:256]
                               for k in range(2)]
                        self.mm(self.ps[b][0:m, :].rearrange("p (h e) -> p h e", e=128),
                                [(ckvn[:, k, c0:c0 + m], rhs[k]) for k in range(2)],
                                wukv_k + ["ckvn"], [("ps", b)])
                        v = vb[vbi]
                        self.act(v[0:m, :], self.ps[b][0:m, :], AF.Copy, [("ps", b)], [("vb", vbi)])
                        self.dma("sp", self.vm[hg * 4:(hg + 1) * 4, t0 + c0:t0 + c0 + m, :].rearrange("h t e -> t h e"),
                                 v[0:m, :].rearrange("p (h e) -> p h e", e=128), reads=[("vb", vbi)])
                        vbi = (vbi + 1) % 2
            self.S.phase_end()

    def phase_swa_proj(self, j):
        T = self.T
        with contextlib.ExitStack() as ph:
            pp = self.sb(ph, [128, NPE], F32, "pp")
            self.dma("sp", pp[:], self.pp_e[j], writes=["pp"])
            wB = self.sb(ph, [128, 8, 3584], BF16, "wB")
            for k in range(8):
                for half in range(2):
                    self.dma("pool", wB[:, k, half * 1792:(half + 1) * 1792],
                             self.e_w_in[j, k * 128:(k + 1) * 128, 704 + half * 1792:704 + (half + 1) * 1792],
                             writes=[("wB", k, half)])
            wB_k = [("wB", k, hf) for k in range(8) for hf in range(2)]
            zb = [self.sb(ph, [128, 8, 512], BF16, "zb") for _ in range(2)]
            tcs = [self.sb(ph, [128, 512], F32, "tcs") for _ in range(2)]
            tss = [self.sb(ph, [128, 512], F32, "tss") for _ in range(2)]
            NS = 3
            sqh = [self.sb(ph, [128, 512], F32R, "sqh") for _ in range(NS)]
            uh = [self.sb(ph, [128, 512], F32, "uh") for _ in range(NS)]
            sdh = [self.sb(ph, [128, 512], F32, "sdh") for _ in range(NS)]
            rsh = [self.sb(ph, [128, 512], F32, "rsh") for _ in range(NS)]
            t1 = [self.sb(ph, [128, 512], F32, "t1") for _ in range(NS)]
            t2 = [self.sb(ph, [128, 512], F32, "t2") for _ in range(NS)]
            ob = [self.sb(ph, [128, 512], BF16, "ob") for _ in range(4)]
            vb = [self.sb(ph, [128, 256], BF16, "vb") for _ in range(2)]
            obi = 0
            vbi = 0
            hs = 0
            def load_s(bi):
                t0, n = self.blocks[bi]
                s = bi % 2
                self.dma("sp", zb[s][:, :, 0:n], self.zblk(bi), writes=[("zb", s)])
                self.dma("sp", tcs[s][:, 0:n], self.tab[0, :, t0:t0 + n], writes=[("tcs", s)])
                self.dma("sp", tss[s][:, 0:n], self.tab[1, :, t0:t0 + n], writes=[("tss", s)])

            load_s(0)
            for bi, (t0, n) in enumerate(self.blocks):
                s = bi % 2
                if bi + 1 < len(self.blocks):
                    load_s(bi + 1)
                pend = [None]

                def flush_pend():
                    if pend[0] is not None:
                        f_ = pend[0]
                        pend[0] = None
                        f_()

                for hh in range(10):
                    hs = (hs + 1) % NS
                    c0 = hh * 128 if hh < 8 else 1024 + (hh - 8) * 128
                    gcol = 17 if hh < 8 else 18
                    b = self.nps()
                    self.mm(self.ps[b][:, 0:n], [(wB[:, k, c0:c0 + 128], zb[s][:, k, 0:n]) for k in range(8)],
                            wB_k + [("zb", s)], [("ps", b)])
                    self.act(sqh[hs][:, 0:n], self.ps[b][:, 0:n], AF.Square, [("ps", b)], [("sqh", hs)])
                    self.act(uh[hs][:, 0:n], self.ps[b][:, 0:n], AF.Copy, [("ps", b), "pp"], [("uh", hs)],
                             scale=pp[:, gcol:gcol + 1])
                    flush_pend()

                    def fin(hs=hs, hh=hh, obi=obi):
                        self.rstd((sdh[hs], rsh[hs]), [(self.ones_r[:], sqh[hs][:, 0:n])], [("sqh", hs)], 1.0 / 128, n, ("h", hs))
                        o = ob[obi]
                        self.rope(o, uh[hs], tcs[s], tss[s], rsh[hs], 64, n, t1[hs], t2[hs], ("uh", hs), ("tcs", s), ("tss", s),
                                  ("rs", ("h", hs)), ("ob", obi), ("t1", hs), ("t2", hs))
                        dst = self.qs[hh, :, t0:t0 + n] if hh < 8 else self.ks[hh - 8, :, t0:t0 + n]
                        self.dma("sp", dst, o[:, 0:n], reads=[("ob", obi)])
                    pend[0] = fin
                    obi = (obi + 1) % 4
                flush_pend()
                ntt = max(1, n // 128)
                for tt_ in range(ntt):
                    m = min(128, n)
                    cc = tt_ * 128
                    b = self.nps()
                    self.mm(self.ps[b][0:m, 0:256], [(zb[s][:, k, cc:cc + m], wB[:, k, 1280:1536]) for k in range(8)],
                            wB_k + [("zb", s)], [("ps", b)])
                    v = vb[vbi]
                    self.act(v[0:m, :], self.ps[b][0:m, 0:256], AF.Copy, [("ps", b)], [("vb", vbi)])
                    self.dma("sp", self.vs[:, t0 + cc:t0 + cc + m, :].rearrange("g t e -> t g e"),
                             v[0:m, :].rearrange("p (g e) -> p g e", e=128), reads=[("vb", vbi)])
                    vbi = (vbi + 1) % 2
                for c in range(16):
                    b = self.nps()
                    c0 = 1536 + c * 128
                    self.mm(self.ps[b][:, 0:n], [(wB[:, k, c0:c0 + 128], zb[s][:, k, 0:n]) for k in range(8)],
                            wB_k + [("zb", s)], [("ps", b)])
                    o = ob[obi]
                    self.act(o[:, 0:n], self.ps[b][:, 0:n], AF.Silu, [("ps", b)], [("ob", obi)])
                    self.dma("sp", self.sg[c, :, t0:t0 + n], o[:, 0:n], reads=[("ob", obi)])
                    obi = (obi + 1) % 4
            self.S.phase_end()

    def phase_mla_attn(self, j):
        T, NT = self.T, self.NT
        scale = 192.0 ** -0.5
        with contextlib.ExitStack() as ph:
            knb = [self.sb(ph, [128, T], BF16, "knb") for _ in range(2)]
            krb = [self.sb(ph, [128, T], BF16, "krb") for _ in range(2)]
            vtb = [self.sb(ph, [128, NT + 1, 128], BF16, "vtb") for _ in range(2)]
            qnb = [self.sb(ph, [128, 512], BF16, "qnb") for _ in range(2)]
            qrb = [self.sb(ph, [128, 512], BF16, "qrb") for _ in range(2)]
            sgb = [self.sb(ph, [128, 512], BF16, "sgb") for _ in range(2)]
            NP = 4
            NPT = 8
            pt = [self.sb(ph, [128, 512], BF16, "pt") for _ in range(NPT)]
            pacc = [self.sb(ph, [128, 512], BF16, "pacc") for _ in range(2)]
            slot0 = [0]
            pend_ones = []
            rl = self.sb(ph, [128, 512], F32, "rl")
            of = self.sb(ph, [128, 512], F32, "of")
            yb = [self.sb(ph, [128, 512], BF16, "yb") for _ in range(2)]
            qbs = [(h, bi) for h in range(8) for bi in range(len(self.blocks))]
            units = [(qi, kt) for qi in range(len(qbs)) for kt in range(len(self.ktiles))]
            LOOK = 3

            def load_head(h):
                s = h % 2
                self.dma("sp", knb[s][:, :], self.kn[h], writes=[("knb", s)])
                self.dma("sp", krb[s][0:64, :], self.kr[h], writes=[("krb", s, 0)])
                self.dma("sp", krb[s][64:128, :], self.kr[h], writes=[("krb", s, 1)])
                self.dma("sp", vtb[s][0:NMETA, 0, :], self.vm[h, 0:NMETA, :], writes=[("vtb", s, 0)])
                self.dma("sp", vtb[s][:, 1:NT + 1, :], self.vm[h, NMETA:T, :].rearrange("(k p) e -> p k e", p=128),
                         writes=[("vtb", s, 1)])

            def load_q(qi):
                h, bi = qbs[qi]
                t0, n = self.blocks[bi]
                s = qi % 2
                self.dma("sp", qnb[s][:, 0:n], self.qn[h, :, t0:t0 + n], writes=[("qnb", s)])
                self.dma("sp", qrb[s][0:64, 0:n], self.qr[h, :, t0:t0 + n], writes=[("qrb", s, 0)])
                self.dma("sp", qrb[s][64:128, 0:n], self.qr[h, :, t0:t0 + n], writes=[("qrb", s, 1)])
                self.dma("sp", sgb[s][:, 0:n], self.sg[h, :, t0:t0 + n], writes=[("sgb", s)])

            def emit_pair(u0):
                us = [u for u in (u0, u0 + 1) if u < len(units)]
                info = []
                for u in us:
                    qi, kt = units[u]
                    h, bi = qbs[qi]
                    t0, n = self.blocks[bi]
                    k0, kn_ = self.ktiles[kt]
                    info.append((u, h % 2, qi % 2, n, k0, kn_, u % NP, (u % 2) * 64))

                def fn(e):
                    ins = None
                    for (u, hsl, qs_, n, k0, kn_, b, r0) in info:
                        ins = e.matmul(self.ps[b][0:kn_, 0:n], knb[hsl][:, k0:k0 + kn_], qnb[qs_][:, 0:n], start=True, stop=False)
                    for (u, hsl, qs_, n, k0, kn_, b, r0) in info:
                        ins = e.matmul(self.ps[b][0:kn_, 0:n], krb[hsl][r0:r0 + 64, k0:k0 + kn_], qrb[qs_][r0:r0 + 64, 0:n],
                                       start=False, stop=True)
                    return ins
                reads = []
                writes = []
                for (u, hsl, qs_, n, k0, kn_, b, r0) in info:
                    reads += [("knb", hsl), ("krb", hsl, 0), ("krb", hsl, 1), ("qnb", qs_), ("qrb", qs_, 0), ("qrb", qs_, 1)]
                    writes.append(("ps", b))
                self.S.op("pe", fn, reads, writes)

            load_head(0)
            load_q(0)
            emit_pair(0)
            for u, (qi, kt) in enumerate(units):
                h, bi = qbs[qi]
                t0, n = self.blocks[bi]
                k0, kn_ = self.ktiles[kt]
                hsl, qs_ = h % 2, qi % 2
                if kt == 0:
                    if qi + 1 < len(qbs):
                        if qbs[qi + 1][0] != h:
                            load_head(qbs[qi + 1][0])
                        load_q(qi + 1)
                if u % 2 == 0 and u + 2 < len(units):
                    emit_pair(u + 2)
                b = u % NP
                pb = u % NPT
                while pend_ones and pend_ones[0][0] <= u:
                    pend_ones.pop(0)[1]()
                bo = 4 + qi % 2
                bl = 6 + qi % 2
                self.act(pt[pb][0:kn_, 0:n], self.ps[b][0:kn_, 0:n], AF.Exp, [("ps", b)], [("pt", pb)], scale=scale)
                last = (kt == len(self.ktiles) - 1)
                vkey = ("vtb", hsl, 0 if kt == 0 else 1)
                self.mm(self.ps[bo][:, 0:n], [(vtb[hsl][0:kn_, kt, :], pt[pb][0:kn_, 0:n])],
                        [vkey, ("pt", pb)], [("ps", bo)], start=(kt == 0), stop=last)
                if kt == 0:
                    self.mm(self.ps[bl][:, 0:n], [(self.ones_b[0:kn_, :], pt[pb][0:kn_, 0:n])],
                            ["ones_b", ("pt", pb)], [("ps", bl)], start=True, stop=last)
                else:
                    gi = (kt - 1) % 4
                    qd = ((kt - 1) // 4 + qi) % 2
                    if gi == 0:
                        slot0[0] = pb
                    elif gi == 1:
                        self.tt("dve", pacc[qd][:, 0:n], pt[slot0[0]][:, 0:n], pt[pb][:, 0:n], ALU.add,
                                [("pt", slot0[0]), ("pt", pb)], [("pacc", qd)])
                    else:
                        self.tt("dve", pacc[qd][:, 0:n], pacc[qd][:, 0:n], pt[pb][:, 0:n], ALU.add,
                                [("pacc", qd), ("pt", pb)], [("pacc", qd)])
                    if gi == 3:
                        def ones_mm(bl=bl, qd=qd, n=n, last=last):
                            self.mm(self.ps[bl][:, 0:n], [(self.ones_b[:, :], pacc[qd][:, 0:n])],
                                    ["ones_b", ("pacc", qd)], [("ps", bl)], start=False, stop=last)
                        if last:
                            ones_mm()
                        else:
                            pend_ones.append((u + 2, ones_mm))
                if last:
                    self.recip(rl[:, 0:n], self.ps[bl][:, 0:n], [("ps", bl)], ["rl"])
                    self.tt("dve", of[:, 0:n], self.ps[bo][:, 0:n], rl[:, 0:n], ALU.mult, [("ps", bo), "rl"], ["of"])
                    ys = qi % 2
                    self.tt("pool", yb[ys][:, 0:n], of[:, 0:n], sgb[qs_][:, 0:n], ALU.mult, ["of", ("sgb", qs_)], [("yb", ys)])
                    self.dma("sp", self.yblk(bi)[:, h, :], yb[ys][:, 0:n], reads=[("yb", ys)])
            self.S.phase_end()

    def phase_swa_attn(self, j):
        T, NT = self.T, self.NT
        scale = 128.0 ** -0.5
        with contextlib.ExitStack() as ph:
            pp = self.sb(ph, [128, NPE], F32, "pp")
            self.dma("sp", pp[:], self.pp_e[j], writes=["pp"])
            esk = self.sb(ph, [128, 8], F32, "esk")
            self.act(esk[:], pp[:, 19:27], AF.Exp, ["pp"], ["esk"])
            ones_m = self.sb(ph, [128, 128], BF16, "ones_m")
            mprev = self.sb(ph, [128, 128], BF16, "mprev")
            mnext = self.sb(ph, [128, 128], BF16, "mnext")
            mmeta = self.sb(ph, [128, 16], BF16, "mmeta")
            self.memset("pool", ones_m[:], 1.0, ["ones_m"])
            self.S.op("pool", lambda e: e.affine_select(out=mprev[:], in_=ones_m[:], pattern=[[-1, 128]], base=0,
                                                        channel_multiplier=1, compare_op=ALU.is_ge, fill=0.0),
                      ["ones_m"], ["mprev"])
            self.S.op("pool", lambda e: e.affine_select(out=mnext[:], in_=ones_m[:], pattern=[[1, 128]], base=0,
                                                        channel_multiplier=-1, compare_op=ALU.is_ge, fill=0.0),
                      ["ones_m"], ["mnext"])
            self.S.op("pool", lambda e: e.affine_select(out=mmeta[:], in_=ones_m[:, 0:16], pattern=[[1, 16]], base=112,
                                                        channel_multiplier=-1, compare_op=ALU.is_ge, fill=0.0),
                      ["ones_m"], ["mmeta"])
            ksb = [self.sb(ph, [128, T], BF16, "ksb") for _ in range(2)]
            vtb = [self.sb(ph, [128, NT + 1, 128], BF16, "vtb") for _ in range(2)]
            q4 = [self.sb(ph, [128, 4, 512], BF16, "q4") for _ in range(3)]
            sg4 = [self.sb(ph, [128, 4, 512], BF16, "sg4") for _ in range(3)]
            NP = 4
            pt = [self.sb(ph, [128, 4, 128], BF16, "pt") for _ in range(NP)]
            lf = self.sb(ph, [128, 4, 128], F32, "lf")
            rl = self.sb(ph, [128, 4, 128], F32, "rl")
            of = self.sb(ph, [128, 4, 128], F32, "of")
            yb = [self.sb(ph, [128, 4, 128], BF16, "yb") for _ in range(2)]
            pi = 0
            qt_count = 0
            for g in range(2):
                self.dma("sp", ksb[g][:, :], self.ks[g], writes=[("ksb", g)])
                self.dma("sp", vtb[g][0:NMETA, 0, :], self.vs[g, 0:NMETA, :], writes=[("vtb", g, 0)])
                self.dma("sp", vtb[g][:, 1:NT + 1, :], self.vs[g, NMETA:T, :].rearrange("(k p) e -> p k e", p=128),
                         writes=[("vtb", g, 1)])
            blks = [(g, bi) for g in range(2) for bi in range(len(self.blocks))]
            qts = []
            units = []
            for bk, (g, bi) in enumerate(blks):
                t0, n = self.blocks[bi]
                nq = 1 if n == NMETA else n // 128
                for qt in range(nq):
                    w = NMETA if n == NMETA else 128
                    if n == NMETA:
                        tiles = [(0, None, None), (1, mmeta, "mmeta")]
                    else:
                        kti = (t0 - NMETA) // 128 + qt + 1
                        tiles = [(0, None, None)]
                        if kti - 1 >= 1:
                            tiles.append((kti - 1, mprev, "mprev"))
                        tiles.append((kti, None, None))
                        if kti + 1 <= NT:
                            tiles.append((kti + 1, mnext, "mnext"))
                    qts.append(dict(bk=bk, g=g, t0=t0, n=n, s=bk % 3, c0=qt * 128, w=w, tiles=tiles, qt=qt))
                    for ti in range(len(tiles)):
                        units.append((len(qts) - 1, ti))
            LOOK = 3

            def load_blk(bk):
                g, bi = blks[bk]
                t0, n = self.blocks[bi]
                s_ = bk % 3
                self.dma("sp", q4[s_][:, :, 0:n], self.qs[g * 4:(g + 1) * 4, :, t0:t0 + n].rearrange("r p t -> p r t"),
                         writes=[("q4", s_)])
                self.dma("sp", sg4[s_][:, :, 0:n], self.sg[8 + g * 4:8 + (g + 1) * 4, :, t0:t0 + n].rearrange("r p t -> p r t"),
                         writes=[("sg4", s_)])

            loaded = [0]

            def ensure_loaded(bk):
                while loaded[0] <= bk and loaded[0] < len(blks):
                    load_blk(loaded[0])
                    loaded[0] += 1

            def emit_s(u):
                qi, ti = units[u]
                q = qts[qi]
                ensure_loaded(q["bk"])
                kt = q["tiles"][ti][0]
                k0, kn_ = self.ktiles[kt]
                b = u % NP
                w = q["w"]
                self.mm(self.ps[b][0:kn_, 0:4 * w].rearrange("p (r i) -> p r i", r=4),
                        [(ksb[q["g"]][:, k0:k0 + kn_], q4[q["s"]][:, :, q["c0"]:q["c0"] + w])],
                        [("ksb", q["g"]), ("q4", q["s"])], [("ps", b)])

            for u in range(min(LOOK, len(units))):
                emit_s(u)
            for u, (qi, ti) in enumerate(units):
                q = qts[qi]
                g, w, c0, s_, t0 = q["g"], q["w"], q["c0"], q["s"], q["t0"]
                kt, mask, mk = q["tiles"][ti]
                k0, kn_ = self.ktiles[kt]
                b = u % NP
                bo = 4 + qi % 2
                bl = 6 + qi % 2
                if q["qt"] == 0 and ti == 0:
                    ensure_loaded(q["bk"] + 1)
                pv = pt[b][0:kn_, :, 0:w]
                self.act(pv, self.ps[b][0:kn_, 0:4 * w].rearrange("p (r i) -> p r i", r=4), AF.Exp,
                         [("ps", b)], [("pt", b)], scale=scale)
                if mask is not None:
                    self.tt("pool", pv, pv, mask[0:kn_, 0:w].unsqueeze(1).to_broadcast([kn_, 4, w]), ALU.mult,
                            [("pt", b), mk], [("pt", b)])
                vkey = ("vtb", g, 0 if kt == 0 else 1)
                first, last = (ti == 0), (ti == len(q["tiles"]) - 1)
                self.mm(self.ps[bo][:, 0:4 * w].rearrange("p (r i) -> p r i", r=4),
                        [(vtb[g][0:kn_, kt, :], pv)], [vkey, ("pt", b)], [("ps", bo)], start=first, stop=last)
                self.mm(self.ps[bl][:, 0:4 * w].rearrange("p (r i) -> p r i", r=4),
                        [(self.ones_b[0:kn_, :], pv)], ["ones_b", ("pt", b)], [("ps", bl)], start=first, stop=last)
                if u + LOOK < len(units):
                    emit_s(u + LOOK)
                if last:
                    lv = lf[:, :, 0:w]
                    self.tt("dve", lv, self.ps[bl][:, 0:4 * w].rearrange("p (r i) -> p r i", r=4),
                            esk[:, g * 4:(g + 1) * 4].unsqueeze(2).to_broadcast([128, 4, w]), ALU.add,
                            [("ps", bl), "esk"], ["lf"])
                    self.recip(rl[:, :, 0:w], lv, ["lf"], ["rl"])
                    self.tt("dve", of[:, :, 0:w], self.ps[bo][:, 0:4 * w].rearrange("p (r i) -> p r i", r=4),
                            rl[:, :, 0:w], ALU.mult, [("ps", bo), "rl"], ["of"])
                    ys = qi % 2
                    self.tt("dve", yb[ys][:, :, 0:w], of[:, :, 0:w], sg4[s_][:, :, c0:c0 + w], ALU.mult,
                            ["of", ("sg4", s_)], [("yb", ys)])
                    self.dma("sp", self.yblk(blks[q["bk"]][1])[:, 8 + g * 4:8 + (g + 1) * 4, c0:c0 + w],
                             yb[ys][:, :, 0:w], reads=[("yb", ys)])
            self.S.phase_end()

    def phase_out(self, l, w_out):
        T = self.T
        final = (l == 3)
        with contextlib.ExitStack() as ph:
            wo = self.sb(ph, [128, 16, D], BF16, "wo")
            for k in range(16):
                self.dma("pool", wo[:, k, :], w_out[k * 128:(k + 1) * 128, :], writes=[("wo", k)])
            wo_k = [("wo", k) for k in range(16)]
            yb = [self.sb(ph, [128, 16, 512], BF16, "yb") for _ in range(2)]
            hb = [self.sb(ph, [128, 8, 512], F32, "hb") for _ in range(2)]
            fuse = (not final) and (l + 1 < self.n_layers)
            if fuse:
                ln_ = l + 1
                npp = NPE if ln_ % 2 == 0 else NPO
                ppn = self.sb(ph, [128, npp], F32, "ppn")
                self.dma("sp", ppn[:], (self.pp_e if ln_ % 2 == 0 else self.pp_o)[ln_ // 2], writes=["ppn"])
                sqn = [self.sb(ph, [128, 8, 512], F32R, "sqn") for _ in range(2)]
                hn = [self.sb(ph, [128, 8, 512], F32, "hn") for _ in range(2)]
                zbn = [self.sb(ph, [128, 8, 512], BF16, "zbn") for _ in range(2)]
                sdn = self.sb(ph, [128, 512], F32, "sdn")
                rsn = self.sb(ph, [128, 512], F32, "rsn")

            def load_o(bi):
                t0, n = self.blocks[bi]
                s = bi % 2
                self.dma("sp", yb[s][:, :, 0:n], self.yblk(bi), writes=[("yb", s)])
                self.dma("sp", hb[s][:, :, 0:n], self.hblk(bi), writes=[("hb", s)])

            load_o(0)
            for bi, (t0, n) in enumerate(self.blocks):
                s = bi % 2
                if bi + 1 < len(self.blocks):
                    load_o(bi + 1)
                for mc in range(8):
                    b = self.nps()
                    self.mm(self.ps[b][:, 0:n], [(wo[:, k, mc * 128:(mc + 1) * 128], yb[s][:, k, 0:n]) for k in range(16)],
                            wo_k + [("yb", s)], [("ps", b)])
                    self.tt("dve", hb[s][:, mc, 0:n], hb[s][:, mc, 0:n], self.ps[b][:, 0:n], ALU.add,
                            [("hb", s), ("ps", b)], [("hb", s)])
                if final:
                    if t0 >= NMETA:
                        self.dma("sp", self.outT[:, t0 - NMETA:t0 - NMETA + n].rearrange("(c p) t -> p c t", p=128),
                                 hb[s][:, :, 0:n], reads=[("hb", s)])
                else:
                    self.dma("sp", self.hblk(bi), hb[s][:, :, 0:n], reads=[("hb", s)])
                if fuse:
                    self.act(sqn[s][:, :, 0:n], hb[s][:, :, 0:n], AF.Square, [("hb", s)], [("sqn", s)])
                    self.rstd((sdn, rsn), [(self.ones_r[:], sqn[s][:, c, 0:n]) for c in range(8)], [("sqn", s)],
                              1.0 / D, n, "nf")
                    self.tt("dve", hn[s][:, :, 0:n], hb[s][:, :, 0:n],
                            rsn[:, 0:n].unsqueeze(1).to_broadcast([128, 8, n]), ALU.mult,
                            [("hb", s), ("rs", "nf")], [("hn", s)])
                    self.tt("pool", zbn[s][:, :, 0:n], hn[s][:, :, 0:n],
                            ppn[:, 0:8].unsqueeze(2).to_broadcast([128, 8, n]), ALU.mult,
                            [("hn", s), "ppn"], [("zbn", s)])
                    self.dma("sp", self.zblk(bi), zbn[s][:, :, 0:n], reads=[("zbn", s)])
            self.S.phase_end()

    def phase_dump(self):
        for bi, (t0, n) in enumerate(self.blocks):
            if t0 >= NMETA:
                self.dma("sp", self.outT[:, t0 - NMETA:t0 - NMETA + n].rearrange("(c p) t -> p c t", p=128), self.hblk(bi))
        self.S.phase_end()

    def phase_lru(self, j):
        T = self.T
        NC = 16
        with contextlib.ExitStack() as ph:
            pp = self.sb(ph, [128, NPO], F32, "pp")
            self.dma("sp", pp[:], self.pp_o[j], writes=["pp"])
            cs = self.sb(ph, [128, 32], F32, "cs")
            one_t = self.sb(ph, [128, 1], F32, "one_t")
            self.memset("dve", one_t[:], 1.0, ["one_t"])
            self.act(cs[:], pp[:, 152:184], AF.Exp, ["pp"], ["cs"], scale=-1.0)
            self.act(cs[:], cs[:], AF.Ln, ["cs", "one_t"], ["cs"], bias=one_t[:, 0:1])
            self.ts("dve", cs[:], cs[:], -8.0, None, ALU.mult, None, ["cs"], ["cs"])
            cs2 = self.sb(ph, [128, 32], F32, "cs2")
            self.ts("dve", cs2[:], cs[:], 2.0, None, ALU.mult, None, ["cs"], ["cs2"])
            NZ = 2
            zb = [self.sb(ph, [128, 8, 512], BF16, "zb") for _ in range(NZ)]
            wu = [self.sb(ph, [128, 8, 128], BF16, "wu") for _ in range(2)]
            wg = [self.sb(ph, [128, 8, 128], BF16, "wg") for _ in range(2)]
            wgt = [self.sb(ph, [128, 4, 128], BF16, "wgt") for _ in range(2)]
            u = [self.sb(ph, [128, T + 3], F32, "u") for _ in range(2)]
            xc = [self.sb(ph, [128, T], F32, "xc") for _ in range(2)]
            xcb = [self.sb(ph, [128, T], BF16, "xcb") for _ in range(2)]
            sgc = [self.sb(ph, [128, T], BF16, "sgc") for _ in range(3)]
            sgs = [self.sb(ph, [128, 512], F32, "sgs") for _ in range(2)]
            gpre = [self.sb(ph, [128, 512], F32, "gpre") for _ in range(2)]
            ra = self.sb(ph, [128, T], F32, "ra")
            ib = self.sb(ph, [128, T], F32, "ib")
            a2 = self.sb(ph, [128, T], F32, "a2")
            hh0 = self.sb(ph, [128, T], F32, "hh0")
            for s in range(2):
                self.memset("dve", u[s][:, 0:2], 0.0, [("upad0", s)])
                self.memset("dve", u[s][:, T + 2:T + 3], 0.0, [("upad1", s)])
            zcount = [0]

            def load_wa(c):
                if c >= NC:
                    return
                s = c % 2
                self.dma("pool", wu[s][:, :, :], self.o_w_in[j, :, c * 128:(c + 1) * 128].rearrange("(k p) e -> p k e", p=128),
                         writes=[("wu", s)])
                self.dma("pool", wg[s][:, :, :], self.o_w_in[j, :, 2048 + c * 128:2048 + (c + 1) * 128].rearrange("(k p) e -> p k e", p=128),
                         writes=[("wg", s)])

            def load_wb(c):
                if c >= NC:
                    return
                s = c % 2
                for d_ in range(2):
                    self.dma("pool", wgt[s][:, d_, :], self.w_a[j, d_, c], writes=[("wgt", s, d_)])
                    self.dma("pool", wgt[s][:, 2 + d_, :], self.w_x[j, d_, c], writes=[("wgt", s, 2 + d_)])

            def a_block(c, bi, t0, n):
                s, s3 = c % 2, c % 3
                zs = zcount[0] % NZ
                zcount[0] += 1
                self.dma("sp", zb[zs][:, :, 0:n], self.zblk(bi), writes=[("zb", zs)])
                b = self.nps()
                self.mm(self.ps[b][:, 0:n], [(wu[s][:, k, :], zb[zs][:, k, 0:n]) for k in range(8)],
                        [("zb", zs), ("wu", s)], [("ps", b)])
                self.act(u[s][:, 2 + t0:2 + t0 + n], self.ps[b][:, 0:n], AF.Copy, [("ps", b)], [("u", s, t0)])
                b = self.nps()
                self.mm(self.ps[b][:, 0:n], [(wg[s][:, k, :], zb[zs][:, k, 0:n]) for k in range(8)],
                        [("zb", zs), ("wg", s)], [("ps", b)])
                self.act(sgs[zs][:, 0:n], self.ps[b][:, 0:n], AF.Sigmoid, [("ps", b)], [("sgs", zs)])
                self.act(gpre[zs][:, 0:n], self.ps[b][:, 0:n], AF.Copy, [("ps", b)], [("gpre", zs)])
                self.tt("pool", sgc[s3][:, t0:t0 + n], gpre[zs][:, 0:n], sgs[zs][:, 0:n], ALU.mult,
                        [("gpre", zs), ("sgs", zs)], [("sgc", s3, t0)])

            def conv(c):
                s = c % 2
                ukeys = [("u", s, t0) for (t0, n) in self.blocks] + [("upad0", s), ("upad1", s)]
                self.ts("dve", xc[s][:, :], u[s][:, 0:T], pp[:, 24 + c:25 + c], pp[:, 8 + c:9 + c], ALU.mult, ALU.add,
                        ukeys + ["pp"], [("xc", s)])
                for tap in range(1, 4):
                    self.stt(xc[s][:, :], u[s][:, tap:tap + T], pp[:, 24 + tap * 16 + c:25 + tap * 16 + c], xc[s][:, :],
                             ALU.mult, ALU.add, ukeys + ["pp", ("xc", s)], [("xc", s)])
                self.copy("dve", xcb[s][:, :], xc[s][:, :], [("xc", s)], [("xcb", s)])

            def stage_b(c, fillers, mid):
                s = c % 2
                rakeys = [("ra", t0) for (t0, n) in self.blocks]
                ibkeys = [("ib", t0) for (t0, n) in self.blocks]
                sgkeys = [("sgc", c % 3, t0) for (t0, n) in self.blocks]
                cnt = [0]

                def fill():
                    cnt[0] += 1
                    if cnt[0] % 4 == 0 and fillers:
                        fillers.pop(0)()

                a2keys = [("a2", t0) for (t0, n) in self.blocks]
                for d_ in range(2):
                    if d_ == 0:
                        R, Rn, Rk, Q, Qk = ra, "ra", rakeys, a2, a2keys
                    else:
                        R, Rn, Rk, Q, Qk = a2, "a2", a2keys, ra, rakeys
                    for (t0, n) in self.blocks:
                        b = self.nps()
                        self.mm(self.ps[b][:, 0:n], [(wgt[s][:, d_, :], xcb[s][:, t0:t0 + n])], [("wgt", s, d_), ("xcb", s)], [("ps", b)])
                        self.act(R[:, t0:t0 + n], self.ps[b][:, 0:n], AF.Sigmoid, [("ps", b), "pp"], [(Rn, t0)],
                                 bias=pp[:, 88 + d_ * 16 + c:89 + d_ * 16 + c])
                        fill()
                    for (t0, n) in self.blocks:
                        b = self.nps()
                        self.mm(self.ps[b][:, 0:n], [(wgt[s][:, 2 + d_, :], xcb[s][:, t0:t0 + n])], [("wgt", s, 2 + d_), ("xcb", s)], [("ps", b)])
                        self.act(ib[:, t0:t0 + n], self.ps[b][:, 0:n], AF.Sigmoid, [("ps", b), "pp"], [("ib", t0)],
                                 bias=pp[:, 120 + d_ * 16 + c:121 + d_ * 16 + c])
                        fill()
                    if d_ == 1:
                        while fillers:
                            fillers.pop(0)()
                    self.act(Q[:, :], R[:, :], AF.Exp, Rk + ["cs2"] + Qk, Qk, scale=cs2[:, d_ * 16 + c:d_ * 16 + c + 1])
                    self.act(R[:, :], R[:, :], AF.Exp, Rk + ["cs"], Rk, scale=cs[:, d_ * 16 + c:d_ * 16 + c + 1])
                    self.act(Q[:, :], Q[:, :], AF.Sqrt, Qk + ["one_t"], Qk, scale=-1.0, bias=one_t[:, 0:1])
                    self.tt("pool", ib[:, :], ib[:, :], xc[s][:, :], ALU.mult, ibkeys + [("xc", s)], ibkeys)
                    self.tt("dve", ib[:, :], ib[:, :], Q[:, :], ALU.mult, ibkeys + Qk, ibkeys)
                    if d_ == 0:
                        self.S.op("dve", lambda e: e.tensor_tensor_scan(hh0[:, :], ra[:, :], ib[:, :], 0.0, ALU.mult, ALU.add),
                                  rakeys + ibkeys, ["hh0"])
                        if mid is not None:
                            mid()
                    else:
                        self.S.op("dve", lambda e: e.tensor_tensor_scan(ra[:, ::-1], a2[:, ::-1], ib[:, ::-1], 0.0, ALU.mult, ALU.add),
                                  a2keys + ibkeys + rakeys, rakeys)
                self.tt("dve", hh0[:, :], hh0[:, :], ra[:, :], ALU.add, ["hh0"] + rakeys, ["hh0"])
                self.tt("dve", xcb[s][:, :], hh0[:, :], sgc[c % 3][:, :], ALU.mult, ["hh0"] + sgkeys, [("xcb", s)])
                for bi, (t0, n) in enumerate(self.blocks):
                    self.dma("pool", self.yblk(bi)[:, c, :], xcb[s][:, t0:t0 + n], reads=[("xcb", s)])

            def a_fillers(c):
                if c >= NC:
                    return []
                return [(lambda bi=bi, t0=t0, n=n: a_block(c, bi, t0, n)) for bi, (t0, n) in enumerate(self.blocks)]

            load_wa(0)
            load_wa(1)
            load_wb(0)
            load_wb(1)
            for c in range(2):
                for f in a_fillers(c):
                    f()
                load_wa(c + 2)
            conv(0)
            for k in range(NC):
                fl = a_fillers(k + 2)
                stage_b(k, fl, (lambda k=k: conv(k + 1)) if k + 1 < NC else None)
                load_wa(k + 4)
                load_wb(k + 2)
            self.S.phase_end()


def _pack_params(inp):
    f = lambda a: np.asarray(a, dtype=np.float32)
    pe = np.zeros((2, 128, NPE), np.float32)
    po = np.zeros((2, 128, NPO), np.float32)
    for j in range(2):
        pe[j, :, 0:8] = f(inp["norm_g"])[2 * j].reshape(8, 128).T
        pe[j, :, 8:11] = f(inp["mla_g_q_lat"])[j].reshape(3, 128).T
        pe[j, :, 11:13] = f(inp["mla_g_kv_lat"])[j].reshape(2, 128).T
        pe[j, :, 13] = f(inp["mla_g_qn"])[j, 0:128]
        pe[j, 0:64, 14] = f(inp["mla_g_qn"])[j, 128:192]
        pe[j, :, 15] = f(inp["mla_g_kn"])[j, 0:128]
        pe[j, 0:64, 16] = f(inp["mla_g_kn"])[j, 128:192]
        pe[j, :, 17] = f(inp["swa_g_qn"])[j]
        pe[j, :, 18] = f(inp["swa_g_kn"])[j]
        pe[j, :, 19:27] = np.broadcast_to(f(inp["swa_sink"])[j][None, :], (128, 8))
        po[j, :, 0:8] = f(inp["norm_g"])[2 * j + 1].reshape(8, 128).T
        po[j, :, 8:24] = f(inp["lru_conv_b"])[j].reshape(16, 128).T
        for tap in range(4):
            po[j, :, 24 + tap * 16:24 + (tap + 1) * 16] = f(inp["lru_conv_w"])[j, tap].reshape(16, 128).T
        for d_ in range(2):
            po[j, :, 88 + d_ * 16:88 + (d_ + 1) * 16] = f(inp["lru_b_a"])[j, d_].reshape(16, 128).T
            po[j, :, 120 + d_ * 16:120 + (d_ + 1) * 16] = f(inp["lru_b_x"])[j, d_].reshape(16, 128).T
            po[j, :, 152 + d_ * 16:152 + (d_ + 1) * 16] = f(inp["lru_lambda"])[j, d_].reshape(16, 128).T
    return pe, po


def _rope_consts():
    cst = np.zeros((128, 4), np.float32)
    inv128 = (10000.0 ** (-np.arange(0, 128, 2, dtype=np.float32) / np.float32(128))).astype(np.float32)
    inv64 = (10000.0 ** (-np.arange(0, 64, 2, dtype=np.float32) / np.float32(64))).astype(np.float32)
    cst[:, 0] = np.concatenate([inv128, inv128])
    cst[0:64, 1] = np.concatenate([inv64, inv64])
    return cst


def make_in_maps(inp, n_cores):
    f = lambda a: np.ascontiguousarray(np.asarray(a, dtype=np.float32))
    pe, po = _pack_params(inp)
    shared = {
        "metaT": f(np.asarray(inp["meta_tokens"]).T),
        "even_w_in": f(inp["even_w_in"]),
        "mla_w_uq": f(np.asarray(inp["mla_w_uq"]).reshape(2, 384, 8 * 192)),
        "mla_w_ukv": f(np.asarray(inp["mla_w_ukv"]).reshape(2, 256, 8 * 256)),
        "even_w_out": f(inp["even_w_out"]),
        "odd_w_in": f(inp["odd_w_in"]),
        "lru_w_a": f(inp["lru_w_a"]),
        "lru_w_x": f(inp["lru_w_x"]),
        "odd_w_out": f(inp["odd_w_out"]),
        "pp_even": pe, "pp_odd": po, "cst": _rope_consts(),
    }
    x = np.asarray(inp["x"], dtype=np.float32)
    maps = []
    for b in range(n_cores):
        m = dict(shared)
        m["xT"] = np.ascontiguousarray(x[b].T)
        maps.append(m)
    return maps


_CACHE = {}


def kernel(**inputs):
    x = np.asarray(inputs["x"])
    B, SEQ, _ = x.shape
    key = (SEQ, 4)
    if key not in _CACHE:
        _CACHE[key] = Prog(SEQ, 4).build()
    nc = _CACHE[key]
    in_maps = make_in_maps(inputs, B)
    res = run_bass_kernel_spmd(nc, in_maps, core_ids=list(range(B)))
    out = np.stack([np.asarray(r["outT"]).T for r in res.results], axis=0)
    return np.ascontiguousarray(out.astype(np.float32))
```
